# Optimizing a Trainium2 kernel written in Bass

```python
import math
import jax
import jax.numpy as jnp
from jax import lax
import numpy as np

D_MODEL = 1024
BATCH = 2
SEQ = 8192
DEPTH = 4

GRID_W = 64
CTX_LEN = 256
N_MIXERS = 4
HEAD_DIM = 64
GQA_HEADS = 16
GQA_KV_HEADS = 4
CONV_WIDTH = 31
DIFF_HEADS = 8
DIFF_V_DIM = 2 * HEAD_DIM
SWA_HEADS = 16
SWA_KV_HEADS = 4
WINDOW = 128
Q_BLOCK = 128
D_FF = 4 * D_MODEL
ROPE_THETA = 10000.0
EPS = 1e-6
NEG_INF = -1e30

kernel_name = 'hybrid_interleaved_dit_prefix_trunk'


def _n_layers_of(m):
    return (DEPTH - m + N_MIXERS - 1) // N_MIXERS


def rms_norm(x, g):
    xf = x.astype(jnp.float32)
    y = xf * lax.rsqrt(jnp.mean(xf * xf, axis=-1, keepdims=True) + EPS)
    return (y * g.astype(jnp.float32)).astype(x.dtype)


def layer_norm(x, g, b):
    xf = x.astype(jnp.float32)
    mu = jnp.mean(xf, axis=-1, keepdims=True)
    var = jnp.mean(jnp.square(xf - mu), axis=-1, keepdims=True)
    y = (xf - mu) * lax.rsqrt(var + EPS)
    return (y * g.astype(jnp.float32) + b.astype(jnp.float32)).astype(x.dtype)


def modulate(x, g, shift, scale):
    return rms_norm(x, g) * (1 + scale) + shift


def rope_2d(rows, dim):
    row = jnp.repeat(jnp.arange(rows, dtype=jnp.float32), GRID_W)
    col = jnp.tile(jnp.arange(GRID_W, dtype=jnp.float32), rows)
    half = dim // 2
    inv = 1.0 / jnp.power(ROPE_THETA, jnp.arange(0, half, 2, dtype=jnp.float32) / half)
    ang = jnp.concatenate([row[:, None] * inv, col[:, None] * inv], axis=-1)
    return jnp.cos(ang), jnp.sin(ang)


def apply_rope(x, cos, sin):
    shp = x.shape
    extra = x.ndim - 3
    tab = (1, cos.shape[0]) + (1,) * extra + (cos.shape[1],)
    c, s = cos.reshape(tab), sin.reshape(tab)
    xf = x.astype(jnp.float32).reshape(shp[:-1] + (shp[-1] // 2, 2))
    x1, x2 = xf[..., 0], xf[..., 1]
    out = jnp.stack([x1 * c - x2 * s, x1 * s + x2 * c], axis=-1).reshape(shp)
    return out.astype(x.dtype)


def _to_blocks(x):
    b, t = x.shape[:2]
    return jnp.moveaxis(x.reshape((b, t // Q_BLOCK, Q_BLOCK) + x.shape[2:]), 1, 0)


def _from_blocks(x):
    x = jnp.moveaxis(x, 0, 1)
    return x.reshape((x.shape[0], x.shape[1] * x.shape[2]) + x.shape[3:])


def gqa_project(h, w_qkv, q_g, k_g, n_heads, n_kv):
    b, t = h.shape[:2]
    q, k, v = jnp.split(h @ w_qkv, [n_heads * HEAD_DIM, (n_heads + n_kv) * HEAD_DIM], axis=-1)
    q = rms_norm(q.reshape(b, t, n_kv, n_heads // n_kv, HEAD_DIM), q_g)
    k = rms_norm(k.reshape(b, t, n_kv, HEAD_DIM), k_g)
    return q, k, v.reshape(b, t, n_kv, HEAD_DIM)


def gqa_scores(q, k):
    return jnp.einsum('bqhgd,bkhd->bhgqk', q, k, preferred_element_type=jnp.float32) / math.sqrt(q.shape[-1])


def gqa_attend(q, k, v):
    p = jax.nn.softmax(gqa_scores(q, k), axis=-1)
    return jnp.einsum('bhgqk,bkhd->bqhgd', p.astype(v.dtype), v)


def mixer_dense_gqa(h, hc, w_qkv, q_g, k_g, w_o, cos, sin, need_ctx):
    b, t = h.shape[:2]
    q, k, v = gqa_project(h, w_qkv, q_g, k_g, GQA_HEADS, GQA_KV_HEADS)
    q, k = apply_rope(q, cos, sin), apply_rope(k, cos, sin)
    qc, kc, vc = gqa_project(hc, w_qkv, q_g, k_g, GQA_HEADS, GQA_KV_HEADS)
    k_all = jnp.concatenate([kc, k], axis=1)
    v_all = jnp.concatenate([vc, v], axis=1)
    o = _from_blocks(lax.map(lambda qb: gqa_attend(qb, k_all, v_all), _to_blocks(q)))
    y = o.reshape(b, t, -1) @ w_o
    yc = gqa_attend(qc, kc, vc).reshape(b, hc.shape[1], -1) @ w_o if need_ctx else None
    return y, yc


def conformer_conv(h, w_pw1, b_pw1, w_dw, b_dw, ln_g, ln_b, w_pw2, b_pw2):
    a, g = jnp.split(h @ w_pw1 + b_pw1, 2, axis=-1)
    u = a * jax.nn.sigmoid(g)
    u = lax.conv_general_dilated(u, w_dw[:, None, :].astype(u.dtype), window_strides=(1,),
                                 padding=[(CONV_WIDTH // 2, CONV_WIDTH // 2)],
                                 dimension_numbers=('NWC', 'WIO', 'NWC'),
                                 feature_group_count=u.shape[-1]) + b_dw
    u = jax.nn.silu(layer_norm(u, ln_g, ln_b))
    return u @ w_pw2 + b_pw2


def mixer_conformer(h, hc, w_pw1, b_pw1, w_dw, b_dw, ln_g, ln_b, w_pw2, b_pw2, need_ctx):
    y = conformer_conv(h, w_pw1, b_pw1, w_dw, b_dw, ln_g, ln_b, w_pw2, b_pw2)
    yc = conformer_conv(hc, w_pw1, b_pw1, w_dw, b_dw, ln_g, ln_b, w_pw2, b_pw2) if need_ctx else None
    return y, yc


def diff_project(h, w_qkv, q_g, k_g):
    b, t = h.shape[:2]
    q, k, v = jnp.split(h @ w_qkv, 3, axis=-1)
    q = rms_norm(q.reshape(b, t, DIFF_HEADS, 2, HEAD_DIM), q_g)
    k = rms_norm(k.reshape(b, t, DIFF_HEADS, 2, HEAD_DIM), k_g)
    return q, k, v.reshape(b, t, DIFF_HEADS, DIFF_V_DIM)


def diff_attend(q, k, v, lam):
    s = jnp.einsum('bqhcd,bkhcd->bchqk', q, k, preferred_element_type=jnp.float32) / math.sqrt(HEAD_DIM)
    p = jax.nn.softmax(s, axis=-1)
    a = p[:, 0] - lam * p[:, 1]
    return jnp.einsum('bhqk,bkhd->bqhd', a.astype(v.dtype), v)


def mixer_diff(h, hc, w_qkv, q_g, k_g, lq1, lk1, lq2, lk2, subln_g, w_o, lam_init, cos, sin, need_ctx):
    b, t = h.shape[:2]
    f32 = jnp.float32
    lam = (jnp.exp(jnp.sum(lq1.astype(f32) * lk1.astype(f32)))
           - jnp.exp(jnp.sum(lq2.astype(f32) * lk2.astype(f32))) + lam_init)
    q, k, v = diff_project(h, w_qkv, q_g, k_g)
    q, k = apply_rope(q, cos, sin), apply_rope(k, cos, sin)
    qc, kc, vc = diff_project(hc, w_qkv, q_g, k_g)
    k_all = jnp.concatenate([kc, k], axis=1)
    v_all = jnp.concatenate([vc, v], axis=1)

    def out(o):
        return (rms_norm(o, subln_g) * (1.0 - lam_init)).reshape(o.shape[0], o.shape[1], -1) @ w_o

    o = _from_blocks(lax.map(lambda qb: diff_attend(qb, k_all, v_all, lam), _to_blocks(q)))
    y = out(o)
    yc = out(diff_attend(qc, kc, vc, lam)) if need_ctx else None
    return y, yc


def mixer_window_gqa(h, hc, w_qkv, q_g, k_g, sink, w_o, cos, sin, need_ctx):
    b, t = h.shape[:2]
    q, k, v = gqa_project(h, w_qkv, q_g, k_g, SWA_HEADS, SWA_KV_HEADS)
    q, k = apply_rope(q, cos, sin), apply_rope(k, cos, sin)
    qc, kc, vc = gqa_project(hc, w_qkv, q_g, k_g, SWA_HEADS, SWA_KV_HEADS)
    sink_logit = sink.astype(jnp.float32).reshape(1, SWA_KV_HEADS, SWA_HEADS // SWA_KV_HEADS, 1, 1)
    span = Q_BLOCK + 2 * WINDOW
    pad = ((0, 0), (WINDOW, WINDOW), (0, 0), (0, 0))
    k_pad, v_pad = jnp.pad(k, pad), jnp.pad(v, pad)
    rel = jnp.arange(span)[None, :] - WINDOW - jnp.arange(Q_BLOCK)[:, None]
    in_band = jnp.abs(rel) <= WINDOW
    n_ctx = kc.shape[1]

    def with_sink(s):
        col = jnp.broadcast_to(sink_logit, s.shape[:-1] + (1,))
        return jax.nn.softmax(jnp.concatenate([s, col], axis=-1), axis=-1)[..., :-1]

    def block(args):
        i, qb = args
        start = i * Q_BLOCK
        kb = lax.dynamic_slice_in_dim(k_pad, start, span, axis=1)
        vb = lax.dynamic_slice_in_dim(v_pad, start, span, axis=1)
        kpos = start - WINDOW + jnp.arange(span)
        valid = in_band & ((kpos >= 0) & (kpos < t))[None, :]
        s = jnp.concatenate([jnp.where(valid, gqa_scores(qb, kb), NEG_INF), gqa_scores(qb, kc)], axis=-1)
        p = with_sink(s).astype(v.dtype)
        return (jnp.einsum('bhgqk,bkhd->bqhgd', p[..., :span], vb)
                + jnp.einsum('bhgqk,bkhd->bqhgd', p[..., span:span + n_ctx], vc))

    o = _from_blocks(lax.map(block, (jnp.arange(t // Q_BLOCK), _to_blocks(q))))
    y = o.reshape(b, t, -1) @ w_o
    yc = None
    if need_ctx:
        pc = with_sink(gqa_scores(qc, kc)).astype(vc.dtype)
        yc = jnp.einsum('bhgqk,bkhd->bqhgd', pc, vc).reshape(b, n_ctx, -1) @ w_o
    return y, yc


def squared_relu_mlp(h, w_up, w_down):
    return jnp.square(jax.nn.relu(h @ w_up)) @ w_down


def setup_inputs(seed: int = 0) -> dict:
    key = jax.random.key(seed)
    ks = iter(jax.random.split(key, 48))
    D = D_MODEL

    def nrm(shape, scale):
        return jax.random.normal(next(ks), shape, jnp.float32) * scale

    def gain(shape):
        return 1.0 + nrm(shape, 0.02)

    nA, nB, nC, nD = (_n_layers_of(m) for m in range(N_MIXERS))
    gqa_w = (GQA_HEADS + 2 * GQA_KV_HEADS) * HEAD_DIM
    swa_w = (SWA_HEADS + 2 * SWA_KV_HEADS) * HEAD_DIM
    return {
        'x': nrm((BATCH, SEQ, D), 1.0),
        'c': nrm((BATCH, D), 1.0),
        'ctx': nrm((BATCH, CTX_LEN, D), 1.0),
        'c_ctx': nrm((D,), 1.0),
        'norm1_g': gain((DEPTH, D)),
        'norm2_g': gain((DEPTH, D)),
        'mod_w': nrm((DEPTH, D, 6 * D), 0.5 * D ** -0.5),
        'mod_b': nrm((DEPTH, 6 * D), 0.02),
        'mlp_up': nrm((DEPTH, D, D_FF), D ** -0.5),
        'mlp_down': nrm((DEPTH, D_FF, D), D_FF ** -0.5),
        'gqa_w_qkv': nrm((nA, D, gqa_w), D ** -0.5),
        'gqa_q_g': gain((nA, HEAD_DIM)),
        'gqa_k_g': gain((nA, HEAD_DIM)),
        'gqa_w_o': nrm((nA, GQA_HEADS * HEAD_DIM, D), (GQA_HEADS * HEAD_DIM) ** -0.5),
        'conv_w_pw1': nrm((nB, D, 2 * D), D ** -0.5),
        'conv_b_pw1': nrm((nB, 2 * D), 0.02),
        'conv_w_dw': nrm((nB, CONV_WIDTH, D), CONV_WIDTH ** -0.5),
        'conv_b_dw': nrm((nB, D), 0.02),
        'conv_ln_g': gain((nB, D)),
        'conv_ln_b': nrm((nB, D), 0.02),
        'conv_w_pw2': nrm((nB, D, D), D ** -0.5),
        'conv_b_pw2': nrm((nB, D), 0.02),
        'diff_w_qkv': nrm((nC, D, 3 * DIFF_HEADS * DIFF_V_DIM), D ** -0.5),
        'diff_q_g': gain((nC, HEAD_DIM)),
        'diff_k_g': gain((nC, HEAD_DIM)),
        'diff_lam_q1': nrm((nC, HEAD_DIM), 0.1),
        'diff_lam_k1': nrm((nC, HEAD_DIM), 0.1),
        'diff_lam_q2': nrm((nC, HEAD_DIM), 0.1),
        'diff_lam_k2': nrm((nC, HEAD_DIM), 0.1),
        'diff_subln_g': gain((nC, DIFF_V_DIM)),
        'diff_w_o': nrm((nC, DIFF_HEADS * DIFF_V_DIM, D), (DIFF_HEADS * DIFF_V_DIM) ** -0.5),
        'swa_w_qkv': nrm((nD, D, swa_w), D ** -0.5),
        'swa_q_g': gain((nD, HEAD_DIM)),
        'swa_k_g': gain((nD, HEAD_DIM)),
        'swa_sink': nrm((nD, SWA_HEADS), 0.5),
        'swa_w_o': nrm((nD, SWA_HEADS * HEAD_DIM, D), (SWA_HEADS * HEAD_DIM) ** -0.5),
    }


def reference(x, c, ctx, c_ctx, norm1_g, norm2_g, mod_w, mod_b, mlp_up, mlp_down,
              gqa_w_qkv, gqa_q_g, gqa_k_g, gqa_w_o,
              conv_w_pw1, conv_b_pw1, conv_w_dw, conv_b_dw, conv_ln_g, conv_ln_b, conv_w_pw2, conv_b_pw2,
              diff_w_qkv, diff_q_g, diff_k_g, diff_lam_q1, diff_lam_k1, diff_lam_q2, diff_lam_k2,
              diff_subln_g, diff_w_o,
              swa_w_qkv, swa_q_g, swa_k_g, swa_sink, swa_w_o):
    rows = x.shape[1] // GRID_W
    cos, sin = rope_2d(rows, HEAD_DIM)
    s_lat = jax.nn.silu(c)
    s_ctx = jax.nn.silu(c_ctx)
    xc = ctx
    for i in range(DEPTH):
        m, j = i % N_MIXERS, i // N_MIXERS
        need_ctx = i < DEPTH - 1
        mod_l = (s_lat @ mod_w[i] + mod_b[i])[:, None, :]
        mod_c = (s_ctx @ mod_w[i] + mod_b[i])[None, None, :]
        sh1, sc1, g1, sh2, sc2, g2 = jnp.split(mod_l, 6, axis=-1)
        csh1, csc1, cg1, csh2, csc2, cg2 = jnp.split(mod_c, 6, axis=-1)
        h = modulate(x, norm1_g[i], sh1, sc1)
        hc = modulate(xc, norm1_g[i], csh1, csc1)
        if m == 0:
            y, yc = mixer_dense_gqa(h, hc, gqa_w_qkv[j], gqa_q_g[j], gqa_k_g[j], gqa_w_o[j], cos, sin, need_ctx)
        elif m == 1:
            y, yc = mixer_conformer(h, hc, conv_w_pw1[j], conv_b_pw1[j], conv_w_dw[j], conv_b_dw[j],
                                    conv_ln_g[j], conv_ln_b[j], conv_w_pw2[j], conv_b_pw2[j], need_ctx)
        elif m == 2:
            lam_init = 0.8 - 0.6 * math.exp(-0.3 * i)
            y, yc = mixer_diff(h, hc, diff_w_qkv[j], diff_q_g[j], diff_k_g[j], diff_lam_q1[j], diff_lam_k1[j],
                               diff_lam_q2[j], diff_lam_k2[j], diff_subln_g[j], diff_w_o[j], lam_init,
                               cos, sin, need_ctx)
        else:
            y, yc = mixer_window_gqa(h, hc, swa_w_qkv[j], swa_q_g[j], swa_k_g[j], swa_sink[j], swa_w_o[j],
                                     cos, sin, need_ctx)
        x = x + g1 * y
        x = x + g2 * squared_relu_mlp(modulate(x, norm2_g[i], sh2, sc2), mlp_up[i], mlp_down[i])
        if need_ctx:
            xc = xc + cg1 * yc
            xc = xc + cg2 * squared_relu_mlp(modulate(xc, norm2_g[i], csh2, csc2), mlp_up[i], mlp_down[i])
    return x
```

```python
import contextlib
import math
import numpy as np
import ml_dtypes
import concourse.bass as bass
import concourse.mybir as mybir
from concourse.bass_utils import run_bass_kernel_spmd

F32 = mybir.dt.float32
BF16 = mybir.dt.bfloat16
ALU = mybir.AluOpType
AF = mybir.ActivationFunctionType
AX = mybir.AxisListType
NPBF = ml_dtypes.bfloat16

NCORES = 8
D = 1024
KD = 8
TL = 2048
TC = 256
TT = TL + TC
SEQ = 8192
DFF = 4096
EPS = 1e-6
DEPTH = 4

EPOCH = 16000
NDMA = 20
ENGS = ("pe", "act", "dve", "pool", "sp")
SAME_ENGINE_SYNC = {"pe": False, "act": True, "dve": True, "pool": True, "sp": True}


class Buf:
    __slots__ = ("name", "w", "r")

    def __init__(self, name):
        self.name = name
        self.w = None
        self.r = {}


class Prog:
    def __init__(self, nc, stack):
        self.nc = nc
        self.stack = stack
        self.streams = {e: [] for e in ENGS}
        self.cnt = {e: 0 for e in ENGS}
        self.dcnt = {e: 0 for e in ENGS}
        self.known = {e: {} for e in ENGS}
        self.psem = {e: [] for e in ENGS}
        self.dsem = {e: [stack.enter_context(nc.semaphore(f"d_{e}_{i}")) for i in range(NDMA)]
                     for e in ("sp", "act", "pool")}
        self.nbuf = 0

    def init_arena(self, sb_bytes=204 * 1024):
        self.sb_cap = sb_bytes
        self.ar_f32 = self.stack.enter_context(self.nc.sbuf_tensor("arena", [128, sb_bytes // 4], F32))[:]
        self.ar_bf = self.ar_f32.bitcast(BF16)
        self.ps_f32 = self.stack.enter_context(self.nc.psum_tensor("psarena", [128, 4096], F32))[:]
        self.ps_bf = self.ps_f32.bitcast(BF16)
        self.sb_off = 0
        self.ps_off = 0
        self.extra = []

    @staticmethod
    def _carve(base, off_elems, shape):
        dims = [[base.ap[0][0], shape[0]]]
        rev, st_ = [], 1
        for n_ in reversed(shape[1:]):
            rev.append([st_, n_])
            st_ *= n_
        return bass.AP(base.tensor, base.offset + off_elems, dims + list(reversed(rev)))

    def barrier(self):
        toks = list(self.extra)
        for e in ENGS:
            n = self.cnt[e]
            if n > 0:
                ep = (n - 1) // EPOCH
                toks.append((("p", e, ep), self.psem[e][ep], (n - 1) % EPOCH + 1))
        for q, sems in self.dsem.items():
            for slot in range(NDMA):
                c = (self.dcnt[q] - slot + NDMA - 1) // NDMA
                if c > 0:
                    toks.append((("d", q, slot), sems[slot], 16 * c))
        for e in ENGS:
            waits = self._waits(e, (), (), toks)
            self.streams[e].append((waits, None, None, 0))

    def phase(self):
        self.barrier()
        self.sb_off = 0
        self.ps_off = 0

    def coll(self, kind, groups, pairs):
        self.barrier()
        if not hasattr(self, "ccsem"):
            self.ccsem = self.stack.enter_context(self.nc.semaphore("ccsem"))
            self.ccn = 0
        for (src, dst) in pairs:
            self.ccn += 1
            tok = (("c",), self.ccsem, self.ccn)

            def fn(e, src=src, dst=dst):
                return e.collective_compute(kind, ALU.bypass, replica_groups=groups, ins=[src], outs=[dst])
            self.streams["pool"].append(([], fn, tok, 1))
        self.extra = [(("c",), self.ccsem, self.ccn)]
        self.barrier()

    def buf(self, name=None):
        self.nbuf += 1
        return Buf(name or f"b{self.nbuf}")

    def bufs(self, n, name="b"):
        return [self.buf(f"{name}{i}") for i in range(n)]

    def sbuf(self, name, shape, dtype):
        size = 4 if dtype == F32 else 2
        nbytes = size * int(np.prod(shape[1:]))
        off = (self.sb_off + 31) // 32 * 32
        self.sb_off = off + nbytes
        assert self.sb_off <= self.sb_cap, f"SBUF arena overflow at {name}: {self.sb_off}"
        return self._carve(self.ar_f32 if dtype == F32 else self.ar_bf, off // size, list(shape))

    def psum(self, name, shape, dtype):
        size = 4 if dtype == F32 else 2
        nbytes = (size * int(np.prod(shape[1:])) + 2047) // 2048 * 2048
        off = self.ps_off
        self.ps_off = off + nbytes
        assert self.ps_off <= 16384, f"PSUM arena overflow at {name}"
        return self._carve(self.ps_f32 if dtype == F32 else self.ps_bf, off // size, list(shape))

    def _waits(self, eng, reads, writes, extra=()):
        waits = []
        kn = self.known[eng]

        def need(tok):
            if tok is None:
                return
            key, _, val = tok
            if key[0] == "p" and key[1] == eng and not SAME_ENGINE_SYNC[eng]:
                return
            if kn.get(key, 0) >= val:
                return
            kn[key] = val
            waits.append(tok)

        for b in reads:
            need(b.w)
        for b in writes:
            need(b.w)
            for t in b.r.values():
                need(t)
        for t in extra:
            need(t)
        return waits

    def _commit(self, tok, reads, writes):
        key = tok[0]
        for b in reads:
            b.r[key] = tok
        for b in writes:
            b.w = tok
            b.r = {}

    def op(self, eng, fn, reads=(), writes=()):
        waits = self._waits(eng, reads, writes)
        n = self.cnt[eng]
        self.cnt[eng] += 1
        ep = n // EPOCH
        while len(self.psem[eng]) <= ep:
            self.psem[eng].append(self.stack.enter_context(self.nc.semaphore(f"p_{eng}_{len(self.psem[eng])}")))
        tok = (("p", eng, ep), self.psem[eng][ep], n % EPOCH + 1)
        self.streams[eng].append((waits, fn, tok, 1))
        self._commit(tok, reads, writes)
        return tok

    def dma(self, q, out, in_, reads=(), writes=()):
        j = self.dcnt[q]
        self.dcnt[q] += 1
        slot, rnd = j % NDMA, j // NDMA
        sem = self.dsem[q][slot]
        key = ("d", q, slot)
        extra = [(key, sem, 16 * rnd)] if rnd > 0 else []
        waits = self._waits(q, reads, writes, extra)
        tok = (key, sem, 16 * (rnd + 1))
        self.streams[q].append((waits, lambda e: e.dma_start(out=out, in_=in_), tok, 16))
        self._commit(tok, reads, writes)
        return tok

    def wait_all(self, eng, bufs):
        waits = self._waits(eng, (), bufs)
        self.streams[eng].append((waits, None, None, 0))

    def emit(self):
        nc = self.nc
        with nc.Block() as block:
            def run(stream):
                def body(e):
                    for waits, fn, tok, inc in stream:
                        for (_, sem, val) in waits:
                            e.wait_ge(sem, val)
                        if fn is not None:
                            fn(e).then_inc(tok[1], inc)
                return body

            block.sync(run(self.streams["sp"]))
            block.scalar(run(self.streams["act"]))
            block.vector(run(self.streams["dve"]))
            block.gpsimd(run(self.streams["pool"]))
            block.tensor(run(self.streams["pe"]))

    def mm(self, out, lhsT, rhs, start, stop, r, w):
        return self.op("pe", lambda e: e.matmul(out, lhsT=lhsT, rhs=rhs, start=start, stop=stop), r, w)

    def tr(self, out, in_, ident, r, w):
        return self.op("pe", lambda e: e.transpose(out, in_, ident), r, w)

    def act(self, out, in_, func, r, w, bias=None, scale=None):
        kw = {}
        if bias is not None:
            kw["bias"] = bias
        if scale is not None:
            kw["scale"] = scale
        return self.op("act", lambda e: e.activation(out=out, in_=in_, func=func, **kw), r, w)

    def tt(self, eng, out, in0, in1, op, r, w):
        return self.op(eng, lambda e: e.tensor_tensor(out=out, in0=in0, in1=in1, op=op), r, w)

    def ts(self, eng, out, in0, s1, op0, r, w, s2=None, op1=None):
        if op1 is None:
            return self.op(eng, lambda e: e.tensor_scalar(out=out, in0=in0, scalar1=s1, scalar2=None, op0=op0), r, w)
        return self.op(eng, lambda e: e.tensor_scalar(out=out, in0=in0, scalar1=s1, scalar2=s2, op0=op0, op1=op1), r, w)

    def stt(self, eng, out, in0, scalar, in1, op0, op1, r, w):
        return self.op(eng, lambda e: e.scalar_tensor_tensor(out=out, in0=in0, scalar=scalar, in1=in1,
                                                             op0=op0, op1=op1), r, w)

    def copy(self, eng, out, in_, r, w):
        if eng == "act":
            return self.act(out, in_, AF.Copy, r, w)
        return self.op(eng, lambda e: e.tensor_copy(out=out, in_=in_), r, w)

    def recip(self, out, in_, r, w):
        return self.op("dve", lambda e: e.reciprocal(out=out, in_=in_), r, w)

    def reduce(self, out, in_, op, r, w):
        return self.op("dve", lambda e: e.tensor_reduce(out=out, in_=in_, axis=AX.X, op=op), r, w)

    def memset(self, eng, ap, val, w):
        return self.op(eng, lambda e: e.memset(ap, val), (), w)


def view(ap, dims):
    return bass.AP(ap.tensor, ap.offset, dims)


def new_nc():
    return bass.Bass("TRN2", target_bir_lowering=False)


def din(nc, name, shape, dt=F32):
    if isinstance(nc, dict):
        ap = nc[name]
        assert tuple(ap.shape) == tuple(shape), (name, tuple(ap.shape), tuple(shape))
        return ap
    return nc.dram_tensor(name, list(shape), dt, kind="ExternalInput").ap()


def dout(nc, name, shape, dt=F32):
    if isinstance(nc, dict):
        return din(nc, name, shape, dt)
    return nc.dram_tensor(name, list(shape), dt, kind="ExternalOutput").ap()


def dint(nc, name, shape, dt=F32):
    return nc.dram_tensor(name, list(shape), dt, kind="Internal").ap()


class NormCtx:
    def __init__(self, P, tag=""):
        self.P = P
        self.ones = P.sbuf("nm_ones" + tag, [128, 128], BF16)
        self.sq = P.sbuf("nm_sq" + tag, [128, KD, 512], BF16)
        self.rs = P.sbuf("nm_rs" + tag, [128, 512], F32)
        self.tmp = P.sbuf("nm_tmp" + tag, [128, KD, 512], F32)
        self.ps = P.psum("nm_ps" + tag, [128, 512], F32)
        self.b_ones = P.buf("nm_ones")
        self.b_sq = P.buf("nm_sq")
        self.b_rs = P.buf("nm_rs")
        self.b_tmp = P.buf("nm_tmp")
        self.b_ps = P.buf("nm_ps")
        P.memset("dve", self.ones[:], 1.0 / 1024.0, [self.b_ones])


def norm_mod(P, C, xg, n, a, b, bmod, hg, bx, bh, func=AF.Identity):
    P.act(C.sq[:, :, 0:n], xg, AF.Square, [bx], [C.b_sq])
    for k in range(KD):
        P.mm(C.ps[:, 0:n], C.ones[:], C.sq[:, k, 0:n], k == 0, k == KD - 1, [C.b_ones, C.b_sq], [C.b_ps])
    P.act(C.rs[:, 0:n], C.ps[:, 0:n], AF.Sqrt, [C.b_ps], [C.b_rs], bias=EPS, scale=1.0)
    P.recip(C.rs[:, 0:n], C.rs[:, 0:n], [C.b_rs], [C.b_rs])
    for k in range(KD):
        P.stt("dve", C.tmp[:, k, 0:n], xg[:, k, :], a[:, k:k + 1], C.rs[:, 0:n], ALU.mult, ALU.mult,
              [bx, C.b_rs, bmod], [C.b_tmp])
    for k in range(KD):
        P.act(hg[:, k, :], C.tmp[:, k, 0:n], func, [C.b_tmp, bmod], [bh], bias=b[:, k:k + 1], scale=1.0)


def load_mod(P, modT_d, ng_d, which):
    mod = P.sbuf("mod_sb", [128, 48, 2], F32)
    ng = P.sbuf("ng_sb", [128, KD], F32)
    aa = P.sbuf("mod_a", [128, 2, KD], F32)
    bb = P.sbuf("mod_b", [128, 2, KD], F32)
    gg = P.sbuf("mod_g", [128, 2, KD], F32)
    bm = P.buf("mod")
    P.dma("sp", mod[:], modT_d, (), [bm])
    P.dma("sp", ng[:], ng_d, (), [bm])
    base = which * 24
    for j in range(2):
        P.stt("dve", aa[:, j, :], mod[:, base + 8:base + 16, j], 1.0, ng[:], ALU.add, ALU.mult, [bm], [bm])
        P.copy("dve", bb[:, j, :], mod[:, base:base + 8, j], [bm], [bm])
        P.copy("dve", gg[:, j, :], mod[:, base + 16:base + 24, j], [bm], [bm])
    return {"a": [aa[:, 0, :], aa[:, 1, :]], "b": [bb[:, 0, :], bb[:, 1, :]],
            "g": [gg[:, 0, :], gg[:, 1, :]], "buf": bm}


PIPE = 4
GROUPS = [(0, 512, 0), (512, 512, 0), (1024, 512, 0), (1536, 512, 0), (2048, 256, 1)]


def build_mod(P, T):
    nc = T
    NV = 2
    NCH = 12
    cT = din(nc, "cT", [128, KD, NV])
    mw = din(nc, "mod_w", [DEPTH, 128, KD, NCH * 128])
    mb = din(nc, "mod_b", [DEPTH, 128, NCH])
    out = dout(nc, "modloc", [128, DEPTH * NCH * NV])
    P.phase()
    c_sb = P.sbuf("c_sb", [128, KD, NV], F32)
    s_sb = P.sbuf("s_sb", [128, KD, NV], F32)
    s_bf = P.sbuf("s_bf", [128, KD, NV], BF16)
    mb_sb = P.sbuf("mb_sb", [128, DEPTH, NCH], F32)
    res = P.sbuf("res", [128, DEPTH, NCH, NV], F32)
    wch = [P.sbuf(f"wch{i}", [128, KD, NCH * 128], BF16) for i in range(2)]
    bw = P.bufs(2, "wch")
    ps = [P.psum(f"ps{i}", [128, 512], F32) for i in range(2)]
    bps = P.bufs(2, "ps")
    bc, bs, bmb, bres = P.buf("c"), P.buf("s"), P.buf("mb"), P.buf("res")
    P.dma("sp", c_sb[:], cT, (), [bc])
    for l in range(DEPTH):
        P.dma("sp", mb_sb[:, l, :], mb[l], (), [bmb])
    P.act(s_sb[:], c_sb[:], AF.Sigmoid, [bc], [bs])
    P.tt("dve", s_sb[:], s_sb[:], c_sb[:], ALU.mult, [bc, bs], [bs])
    P.copy("dve", s_bf[:], s_sb[:], [bs], [bs])
    for l in range(DEPTH):
        w = wch[l % 2]
        for k in range(0, KD, 2):
            P.dma("pool", w[:, k:k + 2, :], mw[l, :, k:k + 2, :], (), [bw[l % 2]])
        pt = ps[l % 2]
        for oc in range(NCH):
            for k in range(KD):
                P.mm(pt[:, oc * NV:(oc + 1) * NV], w[:, k, oc * 128:(oc + 1) * 128], s_bf[:, k, :],
                     k == 0, k == KD - 1, [bw[l % 2], bs], [bps[l % 2]])
        P.tt("dve", res[:, l, :, :], pt[:, 0:NCH * NV].rearrange("p (a b) -> p a b", b=NV),
             view(mb_sb[:, l, :], [list(mb_sb[:].ap[0]), [1, NCH], [0, NV]]),
             ALU.add, [bps[l % 2], bmb], [bres])
    P.dma("sp", out, res[:].rearrange("p a b c -> p (a b c)"), [bres], [P.buf("out")])
    return None


def build_pre_gqa(P, T, kind="gqa"):
    nc = T
    diff = kind == "diff"
    NH = 16
    NKV = 16 if diff else 4
    NKC = 8 if diff else 4
    NQK = NH + NKV
    WQK = NQK * 64
    NV_, DV, DVP = (8, 128, 128) if diff else (4, 64, 65)
    WTOT = WQK + NV_ * DV
    NB = WTOT // 512
    xT = din(nc, "xT", [128, KD, TT])
    modT = din(nc, "modT", [128, 48, 2])
    ng = din(nc, "ng", [128, KD])
    wqkv = din(nc, "wqkv", [128, KD, WTOT])
    gvec = din(nc, "gvec", [128, NQK * 64])
    cs = din(nc, "cs", [128, 16, 2, 32])
    ident_d = din(nc, "ident", [128, 128], BF16)
    qT_o = dout(nc, "qT", [128, KD, TT], BF16)
    kT_o = dout(nc, "kT", [128, NKC, TT], BF16)
    vP_tiles = T["vP_tiles"]
    with contextlib.nullcontext():
        P.phase()
        C = NormCtx(P)
        M = load_mod(P, modT, ng, 0)
        w_sb = P.sbuf("w_sb", [128, KD, WTOT], BF16)
        g_sb = P.sbuf("g_sb", [128, NQK * 64], F32)
        cs_sb = P.sbuf("cs_sb", [128, 16, 2, 32], F32)
        ident = P.sbuf("ident_sb", [128, 128], BF16)
        qst = [P.sbuf(f"qst{i}", [128, KD, 128], BF16) for i in range(2)]
        kst = [P.sbuf(f"kst{i}", [128, NKC, 128], BF16) for i in range(2)]
        vst = [P.sbuf(f"vst{i}", [128, NV_, DVP], BF16) for i in range(2)]
        xg = [P.sbuf("xg0", [128, KD, 512], F32)] * 2 if diff else [P.sbuf(f"xg{i}", [128, KD, 512], F32) for i in range(2)]
        hg = P.sbuf("hg", [128, KD, 512], BF16)
        sq = P.sbuf("sq", [128, NQK * 64], F32)
        ss = P.sbuf("ss", [128, NQK], F32)
        qk = P.sbuf("qk", [128, NQK * 64], F32)
        r1 = P.sbuf("r1", [128, NQK * 32], F32)
        r2 = P.sbuf("r2", [128, NQK * 32], F32)
        r3 = P.sbuf("r3", [128, NQK * 32], F32)
        r4 = P.sbuf("r4", [128, NQK * 32], F32)
        qkb = P.sbuf("qkb", [128, NQK * 64], BF16)
        kdup = P.sbuf("kdup", [128, 4, 2, 64], BF16)
        ps_qkv = P.psum("ps_qkv", [128, WTOT], F32)
        ps_qT = P.psum("ps_qT", [128, 1024], BF16)
        ps_kT = ps_qT if diff else P.psum("ps_kT", [128, 1024], BF16)
        bw, bg, bcs, bid = P.buf("w"), P.buf("g"), P.buf("cs"), P.buf("id")
        bqst, bkst, bvst = P.bufs(2, "qst"), P.bufs(2, "kst"), P.bufs(2, "vst")
        outs = []
        ti = 0
        bxg = [P.buf("xg")] * 2 if diff else P.bufs(2, "xg")
        bhg, bsq, bss, bqk, bqkb, bkd = P.buf("hg"), P.buf("sq"), P.buf("ss"), P.buf("qk"), P.buf("qkb"), P.buf("kd")
        br = P.bufs(4, "r")
        bpq, bpqT = P.buf("psqkv"), P.buf("psqT")
        bpkT = bpqT if diff else P.buf("pskT")
        for k in range(KD):
            P.dma("pool", w_sb[:, k, :], wqkv[:, k, :], (), [bw])
        P.dma("sp", g_sb[:], gvec, (), [bg])
        P.dma("sp", cs_sb[:], cs, (), [bcs])
        P.dma("sp", ident[:], ident_d, (), [bid])
        if not diff:
            for i2 in range(2):
                P.memset("pool", vst[i2][:, :, 64:65], 1.0, [bvst[i2]])
        pst = list(qk[:].ap[0])
        for gi, (c0, n, isctx) in enumerate(GROUPS):
            x_t = xg[gi % 2]
            bx = bxg[gi % 2]
            P.dma("sp", x_t[:, :, 0:n], xT[:, :, c0:c0 + n], (), [bx])
            norm_mod(P, C, x_t[:, :, 0:n], n, M["a"][isctx], M["b"][isctx], M["buf"], hg[:, :, 0:n], bx, bhg)
            for tl in range(n // 128):
                t = c0 // 128 + tl
                tc = slice(c0 + tl * 128, c0 + tl * 128 + 128)
                q_s, k_s, v_s = qst[ti % 2], kst[ti % 2], vst[ti % 2]
                bqT, bkT, bvP = bqst[ti % 2], bkst[ti % 2], bvst[ti % 2]
                ti += 1
                for nb in range(NB):
                    for k in range(KD):
                        P.mm(ps_qkv[:, nb * 512:(nb + 1) * 512], hg[:, k, tl * 128:(tl + 1) * 128],
                             w_sb[:, k, nb * 512:(nb + 1) * 512], k == 0, k == KD - 1, [bhg, bw], [bpq])
                for a0 in range(0, WQK, 512):
                    a1 = min(a0 + 512, WQK)
                    P.act(sq[:, a0:a1], ps_qkv[:, a0:a1], AF.Square, [bpq], [bsq])
                P.reduce(ss[:], sq[:].rearrange("p (h d) -> p h d", d=64), ALU.add, [bsq], [bss])
                P.act(ss[:], ss[:], AF.Sqrt, [bss], [bss], bias=EPS, scale=1.0 / 64.0)
                P.recip(ss[:], ss[:], [bss], [bss])
                for h0 in range(0, NQK, 8):
                    h1 = min(h0 + 8, NQK)
                    P.tt("dve", qk[:, h0 * 64:h1 * 64].rearrange("p (h d) -> p h d", d=64),
                         ps_qkv[:, h0 * 64:h1 * 64].rearrange("p (h d) -> p h d", d=64),
                         view(ss[:, h0:h1], [list(ss[:].ap[0]), [1, h1 - h0], [0, 64]]),
                         ALU.mult, [bpq, bss], [bqk])
                for v0 in range(0, NV_ * DV, 512):
                    v1 = min(v0 + 512, NV_ * DV)
                    P.act(v_s[:, v0 // DV:v1 // DV, 0:DV],
                          ps_qkv[:, WQK + v0:WQK + v1].rearrange("p (h d) -> p h d", d=DV), AF.Copy, [bpq], [bvP])
                if isctx:
                    P.tt("dve", qkb[:], qk[:], g_sb[:], ALU.mult, [bqk, bg], [bqkb])
                else:
                    P.tt("dve", qk[:], qk[:], g_sb[:], ALU.mult, [bqk, bg], [bqk])
                    ev = view(qk[:], [pst, [64, NQK], [2, 32]])
                    od = view(qk[:, 1:2], [pst, [64, NQK], [2, 32]])
                    cosv = view(cs_sb[:, t, 0, :], [list(cs_sb[:].ap[0]), [0, NQK], [1, 32]])
                    sinv = view(cs_sb[:, t, 1, :], [list(cs_sb[:].ap[0]), [0, NQK], [1, 32]])
                    rv = [x[:].rearrange("p (h d) -> p h d", d=32) for x in (r1, r2, r3, r4)]
                    P.tt("dve", rv[0], ev, cosv, ALU.mult, [bqk, bcs], [br[0]])
                    P.tt("dve", rv[1], od, sinv, ALU.mult, [bqk, bcs], [br[1]])
                    P.tt("dve", rv[2], ev, sinv, ALU.mult, [bqk, bcs], [br[2]])
                    P.tt("dve", rv[3], od, cosv, ALU.mult, [bqk, bcs], [br[3]])
                    pb = list(qkb[:].ap[0])
                    evo = view(qkb[:], [pb, [64, NQK], [2, 32]])
                    odo = view(qkb[:, 1:2], [pb, [64, NQK], [2, 32]])
                    P.tt("dve", evo, rv[0], rv[1], ALU.subtract, [br[0], br[1]], [bqkb])
                    P.tt("dve", odo, rv[2], rv[3], ALU.add, [br[2], br[3]], [bqkb])
                if not diff:
                    kv_ = qkb[:, 1024:1280].rearrange("p (h d) -> p h d", d=64)
                    P.copy("act", kdup[:, :, 0, :], kv_, [bqkb], [bkd])
                    P.copy("act", kdup[:, :, 1, :], kv_, [bqkb], [bkd])
                for j in range(8):
                    P.tr(ps_qT[:, j * 128:(j + 1) * 128], qkb[:, j * 128:(j + 1) * 128], ident[:], [bqkb, bid], [bpqT])
                P.copy("dve", q_s[:], ps_qT[:].rearrange("p (j t) -> p j t", t=128), [bpqT], [bqT])
                for g in range(NKC):
                    src = qkb[:, 1024 + g * 128:1024 + (g + 1) * 128] if diff else \
                        kdup[:, g, :, :].rearrange("p a d -> p (a d)")
                    P.tr(ps_kT[:, g * 128:(g + 1) * 128], src, ident[:], [bqkb if diff else bkd, bid], [bpkT])
                P.copy("act", k_s[:], ps_kT[:, 0:NKC * 128].rearrange("p (j t) -> p j t", t=128),
                       [bpkT], [bkT])
                bo = P.buf("out")
                P.dma("sp", qT_o[:, :, tc], q_s[:], [bqT], [bo])
                P.dma("sp", kT_o[:, :, tc], k_s[:], [bkT], [bo])
                P.dma("sp", vP_tiles[t], v_s[:], [bvP], [bo])
                outs.append(bo)
        P.wait_all("sp", outs)
    return None


def build_att(P, T, mode, need_ctx):
    nc = T
    NH, NKV = 16, 4
    NKT = 66 if mode == "dense" else 20
    xT = din(nc, "xT", [128, KD, TT])
    modT = din(nc, "modT", [128, 48, 2])
    ng = din(nc, "ng", [128, KD])
    qT = din(nc, "qT", [128, KD, TT], BF16)
    kloc = din(nc, "kloc", [NKV, 128, TT], BF16).rearrange("g p t -> p g t")
    vloc5 = din(nc, "vloc", [2, 128, 9, NKV, 65], BF16)
    kall = din(nc, "kall", [NKV, 4 * 128, TT], BF16)
    vall5 = din(nc, "vall", [2, 4 * 128, 9, NKV, 65], BF16)
    wo = din(nc, "wo", [64, NH, D])
    if mode == "win":
        masks_d = din(nc, "masks", [128, 6, 512], BF16)
        sink_d = din(nc, "sink", [1, NH])
        selv = din(nc, "selv", [128, 2, 4])
    xo = dout(nc, "xo", [128, KD, TT])
    with contextlib.nullcontext():
        P.phase()
        M = load_mod(P, modT, ng, 0)
        kT_sb = P.sbuf("kT_sb", [128, NKV, NKT * 128], BF16)
        vP_sb = P.sbuf("vP_sb", [128, NKT, NKV, 65], BF16)
        wo_sb = P.sbuf("wo_sb", [64, NH, D], BF16)
        qg = [P.sbuf(f"qg{i}", [128, KD, 512], BF16) for i in range(2)]
        xg = [P.sbuf("xg0", [128, KD, 512], F32)] * 2
        ao = P.sbuf("ao", [64, NH, 512], BF16)
        NP_ = 2 * PIPE + 4
        pt = [P.sbuf(f"pt{i}", [128, 512], BF16) for i in range(NP_)]
        o_sb = P.sbuf("o_sb", [65, 512], F32)
        onesf = P.sbuf("onesf", [65, 64], F32)
        NS = 4
        ps_s = [P.psum(f"ps_s{i}", [128, 512], F32) for i in range(NS)]
        ps_o = [P.psum(f"ps_o{i}", [128, 512], F32) for i in range(2)]
        ps_rb = P.psum("ps_rb", [128, 512], F32)
        ps_y = [P.psum("ps_y0", [128, 512], F32)] * 2
        bk, bv, bwo, bones = P.buf("k"), P.buf("v"), P.buf("wo"), P.buf("ones")
        bqg = P.bufs(2, "qg")
        bxg = [P.buf("xg")] * 2
        bao, bosb = P.buf("ao"), P.buf("osb")
        bpt = P.bufs(NP_, "pt")
        bps, bpo, bprb, bpy = P.bufs(NS, "pss"), P.bufs(2, "pso"), P.buf("psrb"), [P.buf("psy")] * 2
        kall4 = kall.rearrange("g (r p) t -> r p g t", p=128)

        def vcopy(dst0, r, t0, t1):
            for hf in range(2):
                lo, hi = max(t0, 9 * hf), min(t1, 9 * hf + 9)
                if lo < hi:
                    src = vloc5[hf] if r is None else vall5[hf, r * 128:(r + 1) * 128]
                    P.dma("sp", vP_sb[:, dst0 + lo - t0:dst0 + hi - t0], src[:, lo - 9 * hf:hi - 9 * hf], (), [bv])

        if mode == "dense":
            P.dma("sp", kT_sb[:, :, 0:TC], kloc[:, :, TL:TT], (), [bk])
            vcopy(0, None, 16, 18)
            for r in range(4):
                for g in range(NKV):
                    P.dma("sp", kT_sb[:, g, TC + r * TL:TC + (r + 1) * TL], kall4[r, :, g, 0:TL], (), [bk])
                vcopy(2 + 16 * r, r, 0, 16)
        else:
            sv = P.sbuf("sv", [128, 2, 4], F32)
            kc_ = P.sbuf("kc_", [128, 2, 4, NKV, 128], BF16)
            vc_ = P.sbuf("vc_", [128, 2, 4, NKV, 65], BF16)
            bsv, bkc = P.buf("sv"), P.buf("kc")
            P.dma("sp", sv[:], selv, (), [bsv])
            P.dma("sp", kT_sb[:, :, 0:TC], kloc[:, :, TL:TT], (), [bk])
            P.dma("sp", kT_sb[:, :, 3 * 128:3 * 128 + TL], kloc[:, :, 0:TL], (), [bk])
            vcopy(0, None, 16, 18)
            vcopy(3, None, 0, 16)
            for r in range(4):
                P.dma("sp", kc_[:, 0, r], kall4[r, :, :, TL - 128:TL], (), [bkc])
                P.dma("sp", kc_[:, 1, r], kall4[r, :, :, 0:128], (), [bkc])
                P.dma("sp", vc_[:, 0, r], vall5[1, r * 128:(r + 1) * 128, 6], (), [bkc])
                P.dma("sp", vc_[:, 1, r], vall5[0, r * 128:(r + 1) * 128, 0], (), [bkc])
            for side, (kslot, vslot) in enumerate(((2 * 128, 2), (19 * 128, 19))):
                kd_ = kT_sb[:, :, kslot:kslot + 128]
                vd_ = vP_sb[:, vslot]
                P.ts("dve", kd_, kc_[:, side, 0], sv[:, side, 0:1], ALU.mult, [bkc, bsv], [bk])
                P.ts("dve", vd_, vc_[:, side, 0], sv[:, side, 0:1], ALU.mult, [bkc, bsv], [bv])
                for r in range(1, 4):
                    P.stt("dve", kd_, kc_[:, side, r], sv[:, side, r:r + 1], kd_, ALU.mult, ALU.add, [bkc, bsv, bk], [bk])
                    P.stt("dve", vd_, vc_[:, side, r], sv[:, side, r:r + 1], vd_, ALU.mult, ALU.add, [bkc, bsv, bv], [bv])
        for h in range(0, NH, 4):
            P.dma("pool", wo_sb[:, h:h + 4, :], wo[:, h:h + 4, :], (), [bwo])
        P.memset("dve", onesf[:], 1.0, [bones])
        if mode == "win":
            mk = P.sbuf("mk", [128, 6, 512], BF16)
            snk = P.sbuf("snk", [65, NH], F32)
            bmk, bsn = P.buf("mk"), P.buf("snk")
            P.dma("sp", mk[:], masks_d, (), [bmk])
            P.dma("sp", snk[64:65, :], sink_d, (), [bsn])
            P.act(snk[64:65, :], snk[64:65, :], AF.Exp, [bsn], [bsn])
        groups = GROUPS if need_ctx else GROUPS[:4]
        si = 0
        oi = 0
        yi = 0
        outs = []
        P.dma("sp", qg[0][:, :, 0:groups[0][1]], qT[:, :, 0:groups[0][1]], (), [bqg[0]])
        for gi, (c0, n, isctx) in enumerate(groups):
            q_t, x_t = qg[gi % 2], xg[gi % 2]
            bq, bx = bqg[gi % 2], bxg[gi % 2]
            if gi + 1 < len(groups):
                c1, n1, _ = groups[gi + 1]
                P.dma("sp", qg[(gi + 1) % 2][:, :, 0:n1], qT[:, :, c1:c1 + n1], (), [bqg[(gi + 1) % 2]])
            P.dma("sp", x_t[:, :, 0:n], xT[:, :, c0:c0 + n], (), [bx])
            if isctx:
                ktl = [(0, None), (1, None)]
            elif mode == "dense":
                ktl = [(kt, None) for kt in range(NKT)]
            else:
                ktl = [(0, None), (1, None)] + [(2 + 4 * gi + r, r) for r in range(6)]
            units = [(2 * j + hp, ki, kt, mr) for j in range(NH // 2) for ki, (kt, mr) in enumerate(ktl)
                     for hp in range(2)]
            pend = []

            def stage1(u):
                h, ki, kt, mr = u
                j, hp, g = h // 2, h % 2, h // 4
                i_ = stage1.si
                stage1.si += 1
                s_ps, bs_ = ps_s[i_ % NS], bps[i_ % NS]
                p_t, bp_ = pt[i_ % NP_], bpt[i_ % NP_]
                P.mm(s_ps[:, 0:n], kT_sb[hp * 64:(hp + 1) * 64, g, kt * 128:(kt + 1) * 128],
                     q_t[hp * 64:(hp + 1) * 64, j, 0:n], True, True, [bk, bq], [bs_])
                P.act(p_t[:, 0:n], s_ps[:, 0:n], AF.Exp, [bs_], [bp_], scale=0.125)
                if mr is not None:
                    P.tt("dve", p_t[:, 0:n], p_t[:, 0:n], mk[:, mr, 0:n], ALU.mult, [bp_, bmk], [bp_])
                return p_t, bp_

            stage1.si = si

            def stage2(u, p_t, bp_):
                h, ki, kt, mr = u
                g = h // 4
                po, bo_ = ps_o[h % 2], bpo[h % 2]
                P.mm(po[0:65, 0:n], vP_sb[:, kt, g, :], p_t[:, 0:n], ki == 0, ki == len(ktl) - 1, [bv, bp_], [bo_])
                if ki != len(ktl) - 1:
                    return
                P.copy("act", o_sb[:, 0:n], po[0:65, 0:n], [bo_], [bosb])
                if mode == "win" and not isctx:
                    P.ts("dve", o_sb[64:65, 0:n], o_sb[64:65, 0:n], snk[64:65, h:h + 1], ALU.add, [bosb, bsn], [bosb])
                P.recip(o_sb[64:65, 0:n], o_sb[64:65, 0:n], [bosb], [bosb])
                P.mm(ps_rb[0:64, 0:n], onesf[64:65, :], o_sb[64:65, 0:n], True, True, [bones, bosb], [bprb])
                P.tt("dve", ao[:, h, 0:n], o_sb[0:64, 0:n], ps_rb[0:64, 0:n], ALU.mult, [bosb, bprb], [bao])

            PP = 2 * PIPE
            for idx in range(0, len(units) + PP, 2):
                for i2 in (idx, idx + 1):
                    if i2 < len(units):
                        pend.append(stage1(units[i2]))
                for i2 in (idx - PP, idx - PP + 1):
                    if 0 <= i2 < len(units):
                        stage2(units[i2], *pend[i2])
            si = stage1.si
            for oc in range(KD):
                y_ps, by_ = ps_y[yi % 2], bpy[yi % 2]
                yi += 1
                for h in range(NH):
                    P.mm(y_ps[:, 0:n], wo_sb[:, h, oc * 128:(oc + 1) * 128], ao[:, h, 0:n], h == 0, h == NH - 1,
                         [bwo, bao], [by_])
                P.stt("dve", x_t[:, oc, 0:n], y_ps[:, 0:n], M["g"][isctx][:, oc:oc + 1], x_t[:, oc, 0:n],
                      ALU.mult, ALU.add, [by_, M["buf"], bx], [bx])
            bo = P.buf("out")
            P.dma("sp", xo[:, :, c0:c0 + n], x_t[:, :, 0:n], [bx], [bo])
            outs.append(bo)
        if not need_ctx:
            x_t, bx = xg[0], bxg[0]
            bo = P.buf("outc")
            P.dma("sp", x_t[:, :, 0:TC], xT[:, :, TL:TT], (), [bx])
            P.dma("sp", xo[:, :, TL:TT], x_t[:, :, 0:TC], [bx], [bo])
            outs.append(bo)
        P.wait_all("sp", outs)
    return None


def build_mlp(P, T, need_ctx):
    nc = T
    xT = din(nc, "xT", [128, KD, TT])
    modT = din(nc, "modT", [128, 48, 2])
    ng = din(nc, "ng", [128, KD])
    wu = din(nc, "wu", [128, KD, DFF])
    wd = din(nc, "wd", [128, 32, D])
    xo = dout(nc, "xo", [128, KD, TT])
    with contextlib.nullcontext():
        P.phase()
        C = NormCtx(P)
        M = load_mod(P, modT, ng, 1)
        wu_sb = P.sbuf("wu_sb", [128, KD, DFF], BF16)
        wd_sb = P.sbuf("wd_sb", [128, 32, D], BF16)
        N = 256
        xg = [P.sbuf(f"xg{i}", [128, KD, N], F32) for i in range(2)]
        hg = P.sbuf("hg", [128, KD, N], BF16)
        aT = P.sbuf("aT", [128, 32, N], BF16)
        rl = [P.sbuf(f"rl{i}", [128, N], F32) for i in range(3)]
        ps_u = [P.psum(f"ps_u{i}", [128, 512], F32) for i in range(3)]
        ps_d = [P.psum(f"ps_d{i}", [128, 512], F32) for i in range(2)]
        bwu, bwd = P.bufs(KD, "wu"), P.bufs(8, "wd")
        bxg, bhg, baT = P.bufs(2, "xg"), P.buf("hg"), P.bufs(32, "aT")
        brl, bpu, bpd = P.bufs(3, "rl"), P.bufs(3, "psu"), P.bufs(2, "psd")
        for c in range(8):
            P.dma("pool", wu_sb[:, :, c * 512:(c + 1) * 512], wu[:, :, c * 512:(c + 1) * 512], (), [bwu[c]])
        for c in range(8):
            P.dma("pool", wd_sb[:, c * 4:(c + 1) * 4, :], wd[:, c * 4:(c + 1) * 4, :], (), [bwd[c]])
        ncol = TT if need_ctx else TL
        ui = 0
        di = 0
        outs = []
        cols = list(range(0, ncol, N))

        def load_norm(gi):
            c0_ = cols[gi]
            ic = 1 if c0_ >= TL else 0
            P.dma("sp", xg[gi % 2][:], xT[:, :, c0_:c0_ + N], (), [bxg[gi % 2]])
            norm_mod(P, C, xg[gi % 2][:], N, M["a"][ic], M["b"][ic], M["buf"], hg[:], bxg[gi % 2], bhg)

        load_norm(0)
        for gi, c0 in enumerate(cols):
            isctx = 1 if c0 >= TL else 0
            x_t, bx = xg[gi % 2], bxg[gi % 2]
            for fc in range(32):
                u_ps, bu_ = ps_u[ui % 3], bpu[ui % 3]
                r_t, br_ = rl[ui % 3], brl[ui % 3]
                ui += 1
                for k in range(KD):
                    P.mm(u_ps[:, 0:N], wu_sb[:, k, fc * 128:(fc + 1) * 128], hg[:, k, :], k == 0, k == KD - 1,
                         [bwu[fc // 4], bhg], [bu_])
                P.act(r_t[:], u_ps[:, 0:N], AF.Relu, [bu_], [br_])
                P.tt("dve" if fc % 2 == 0 else "pool", aT[:, fc, :], r_t[:], r_t[:], ALU.mult, [br_], [baT[fc]])
            if gi + 1 < len(cols):
                load_norm(gi + 1)
            for oc in range(KD):
                d_ps, bd_ = ps_d[di % 2], bpd[di % 2]
                di += 1
                for fc in range(32):
                    P.mm(d_ps[:, 0:N], wd_sb[:, fc, oc * 128:(oc + 1) * 128], aT[:, fc, :], fc == 0, fc == 31,
                         [bwd[fc // 4], baT[fc]], [bd_])
                P.stt("dve", x_t[:, oc, :], d_ps[:, 0:N], M["g"][isctx][:, oc:oc + 1], x_t[:, oc, :],
                      ALU.mult, ALU.add, [bd_, M["buf"], bx], [bx])
            bo = P.buf("out")
            P.dma("sp", xo[:, :, c0:c0 + N], x_t[:], [bx], [bo])
            outs.append(bo)
        if not need_ctx:
            x_t, bx = xg[0], bxg[0]
            bo = P.buf("outc")
            P.dma("sp", x_t[:], xT[:, :, TL:TT], (), [bx])
            P.dma("sp", xo[:, :, TL:TT], x_t[:], [bx], [bo])
            outs.append(bo)
        P.wait_all("sp", outs)
    return None


_PROGS = {}


def prog(name, builder, *args):
    key = (name,) + args
    if key not in _PROGS:
        _PROGS[key] = builder(*args)
    return _PROGS[key]


def run(nc, in_maps):
    res = run_bass_kernel_spmd(nc, in_maps, core_ids=list(range(NCORES)))
    return res.results


def fm(a):
    T, F = a.shape
    return np.ascontiguousarray(a.T.reshape(F // 128, 128, T).transpose(1, 0, 2))


def fm_inv(a):
    p, k, t = a.shape
    return np.ascontiguousarray(a.transpose(1, 0, 2).reshape(k * p, t).T)


def wl(w):
    fin, o = w.shape
    return np.ascontiguousarray(w.reshape(fin // 128, 128, o).transpose(1, 0, 2))


def colv(v):
    return np.ascontiguousarray(v.reshape(-1, 128).T)


def rope_tables():
    rows = SEQ // 64
    row = np.repeat(np.arange(rows, dtype=np.float32), 64)
    col = np.tile(np.arange(64, dtype=np.float32), rows)
    half = 32
    inv = (1.0 / np.power(np.float32(10000.0), np.arange(0, half, 2, dtype=np.float32) / half)).astype(np.float32)
    ang = np.concatenate([row[:, None] * inv, col[:, None] * inv], axis=-1).astype(np.float32)
    return np.cos(ang).astype(np.float32), np.sin(ang).astype(np.float32)


def cs_for_core(cos, sin, q):
    c = cos[q * TL:(q + 1) * TL].reshape(16, 128, 32)
    s = sin[q * TL:(q + 1) * TL].reshape(16, 128, 32)
    return np.ascontiguousarray(np.stack([c, s], axis=2).transpose(1, 0, 2, 3))


def run_mod(c, c_ctx, mod_w, mod_b):
    nc = prog("mod", build_mod)
    cT = np.ascontiguousarray(np.stack([colv(c[0]), colv(c[1]), colv(c_ctx)], axis=-1))
    maps = []
    for core in range(NCORES):
        cols = slice(core * 768, (core + 1) * 768)
        mw = np.ascontiguousarray(mod_w[:, :, cols].reshape(DEPTH, KD, 128, 768).transpose(0, 2, 1, 3))
        mb = np.ascontiguousarray(mod_b[:, cols].reshape(DEPTH, 6, 128).transpose(0, 2, 1))
        maps.append({"cT": cT, "mod_w": mw, "mod_b": mb})
    res = run(nc, maps)
    full = np.concatenate([r["modT"] for r in res], axis=2)
    return [np.ascontiguousarray(full[:, :, :, [b, 2]]) for b in range(2)]


def gather_kv_dense(kTs, vPs):
    out = []
    for b in range(2):
        cores = [b * 4 + q for q in range(4)]
        kT = np.concatenate([kTs[cores[0]][:, :, TL:TT]] + [kTs[c][:, :, 0:TL] for c in cores], axis=2)
        vP = np.concatenate([vPs[cores[0]][:, 16:18]] + [vPs[c][:, 0:16] for c in cores], axis=1)
        out.append((np.ascontiguousarray(kT), np.ascontiguousarray(vP)))
    return out


def layer_gqa_dense(i, xs, mods, inp, need_ctx, cos, sin):
    ident = np.eye(128, dtype=np.float32).astype(NPBF)
    wqkv = wl(inp["gqa_w_qkv"][0])
    gvec = np.concatenate([np.tile(inp["gqa_q_g"][0], 16), np.tile(inp["gqa_k_g"][0], 4)]).astype(np.float32)
    gvec = np.ascontiguousarray(np.broadcast_to(gvec[None, :], (128, gvec.size)))
    ng1 = colv(inp["norm1_g"][i])
    nc = prog("pre_gqa", build_pre_gqa, "gqa")
    maps = [{"xT": xs[c], "modT": mods[c // 4][i], "ng": ng1, "wqkv": wqkv, "gvec": gvec,
             "cs": cs_for_core(cos, sin, c % 4), "ident": ident} for c in range(NCORES)]
    res = run(nc, maps)
    kv = gather_kv_dense([r["kT"] for r in res], [r["vP"] for r in res])
    wo = np.ascontiguousarray(inp["gqa_w_o"][0].reshape(16, 64, D).transpose(1, 0, 2))
    nc = prog("att", build_att, "dense", need_ctx)
    maps = [{"xT": xs[c], "modT": mods[c // 4][i], "ng": ng1, "qT": res[c]["qT"], "kT": kv[c // 4][0],
             "vP": kv[c // 4][1], "wo": wo} for c in range(NCORES)]
    res2 = run(nc, maps)
    return [r["xo"] for r in res2]


def layer_mlp(i, xs, mods, inp, need_ctx):
    nc = prog("mlp", build_mlp, need_ctx)
    wu = wl(inp["mlp_up"][i])
    wd = wl(inp["mlp_down"][i])
    ng2 = colv(inp["norm2_g"][i])
    maps = [{"xT": xs[c], "modT": mods[c // 4][i], "ng": ng2, "wu": wu, "wd": wd} for c in range(NCORES)]
    res = run(nc, maps)
    return [r["xo"] for r in res]


def shard_x(x, ctx):
    xs = []
    for c in range(NCORES):
        b, q = c // 4, c % 4
        xs.append(fm(np.concatenate([x[b, q * TL:(q + 1) * TL], ctx[b]], axis=0)))
    return xs


def unshard_x(xs):
    out = np.empty((2, SEQ, D), np.float32)
    for c in range(NCORES):
        b, q = c // 4, c % 4
        out[b, q * TL:(q + 1) * TL] = fm_inv(xs[c])[0:TL]
    return out


def build_att_diff(P, T, lam_init):
    nc = T
    NHD, NKT = 8, 66
    xT = din(nc, "xT", [128, KD, TT])
    modT = din(nc, "modT", [128, 48, 2])
    ng = din(nc, "ng", [128, KD])
    qT = din(nc, "qT", [128, KD, TT], BF16)
    kloc = din(nc, "kloc", [NHD, 128, TT], BF16).rearrange("g p t -> p g t")
    vloc = din(nc, "vloc", [NHD, 128, 18, 128], BF16).rearrange("h p t d -> p h t d")
    kall = din(nc, "kall", [NHD, 4 * 128, TT], BF16)
    vall = din(nc, "vall", [NHD, 4 * 128, 18, 128], BF16)
    kall4 = kall.rearrange("g (r p) t -> r p g t", p=128)
    vall4 = vall.rearrange("h (r p) t d -> r p h t d", p=128)
    wo = din(nc, "wo", [128, NHD, D])
    lamv = din(nc, "lamv", [1, 4, 64])
    slg = din(nc, "slg", [128, 1])
    xo = dout(nc, "xo", [128, KD, TT])
    with contextlib.nullcontext():
        P.phase()
        M = load_mod(P, modT, ng, 0)
        qT_sb = P.sbuf("qT_sb", [128, KD, TT], BF16)
        aoT = P.sbuf("aoT", [128, NHD, TT], BF16)
        kh = [P.sbuf(f"kh{i}", [128, NKT * 128], BF16) for i in range(2)]
        vh = [P.sbuf(f"vh{i}", [128, NKT, 128], BF16) for i in range(2)]
        wo_sb = P.sbuf("wo_sb", [128, NHD, D], BF16)
        xg = P.sbuf("xg", [128, KD, 512], F32)
        NP_ = 2 * PIPE + 4
        pt = [P.sbuf(f"pt{i}", [128, 512], BF16) for i in range(NP_)]
        o_sb = [P.sbuf(f"o_sb{i}", [128, 512], F32) for i in range(2)]
        l_sb = P.sbuf("l_sb", [2, 512], F32)
        accl = [P.sbuf(f"accl{i}", [128, 512], F32) for i in range(2)]
        accb = [P.sbuf(f"accb{i}", [128, 512], BF16) for i in range(2)]
        baccl, baccb = P.bufs(2, "accl"), P.bufs(2, "accb")
        od = P.sbuf("od", [128, 512], F32)
        sqb = P.sbuf("sqb", [128, 512], BF16)
        rs = P.sbuf("rs", [128, 512], F32)
        onesb = P.sbuf("onesb", [128, 128], BF16)
        sel = P.sbuf("sel", [128, 2, 2], BF16)
        self_f = P.sbuf("self_f", [2, 2, 128], F32)
        lam_sb = P.sbuf("lam_sb", [1, 4, 64], F32)
        lam_t = P.sbuf("lam_t", [1, 8], F32)
        sg = P.sbuf("sg", [128, 1], F32)
        ps_s = [P.psum(f"ps_s{i}", [128, 512], F32) for i in range(4)]
        ps_o = [P.psum(f"ps_o{i}", [128, 512], F32) for i in range(2)]
        ps_l = P.psum("ps_l", [128, 512], F32)
        ps_rb = [P.psum("ps_rb0", [128, 512], F32)] * 2
        ps_y = ps_rb[0]
        bq, bao, bwo, bx = P.buf("q"), P.buf("ao"), P.buf("wo"), P.buf("x")
        bkh, bvh = P.bufs(2, "kh"), P.bufs(2, "vh")
        bpt, bosb = P.bufs(NP_, "pt"), P.bufs(2, "osb")
        blsb, bod, bsqb, brs, bcst, blam = P.buf("lsb"), P.buf("od"), P.buf("sqb"), P.buf("rs"), P.buf("cst"), P.buf("lam")
        bps, bpo, bpl = P.bufs(4, "pss"), P.bufs(2, "pso"), P.buf("psl")
        bprb = [P.buf("psrb")] * 2
        bpy = bprb[0]
        for k in range(KD):
            P.dma("sp", qT_sb[:, k, :], qT[:, k, :], (), [bq])
        for h in range(0, NHD, 2):
            P.dma("pool", wo_sb[:, h:h + 2, :], wo[:, h:h + 2, :], (), [bwo])
        P.dma("sp", lam_sb[:], lamv, (), [blam])
        P.dma("sp", sg[:], slg, (), [bcst])
        P.memset("dve", onesb[:], 1.0 / 128.0, [bcst])
        P.memset("dve", sel[:], 0.0, [bcst])
        P.memset("dve", sel[:, 0, 0:1], 1.0, [bcst])
        P.memset("dve", sel[:, 1, 1:2], 1.0, [bcst])
        P.memset("dve", self_f[:], 0.0, [bcst])
        P.memset("dve", self_f[0:1, 0, :], 1.0, [bcst])
        P.memset("dve", self_f[0:2, 1, :], 1.0, [bcst])
        P.memset("dve", self_f[0:1, 1, :], 0.0, [bcst])
        P.ts("dve", sg[:], sg[:], 1.0 - lam_init, ALU.mult, [bcst], [bcst])
        P.tt("dve", lam_sb[:, 0, :], lam_sb[:, 0, :], lam_sb[:, 1, :], ALU.mult, [blam], [blam])
        P.tt("dve", lam_sb[:, 2, :], lam_sb[:, 2, :], lam_sb[:, 3, :], ALU.mult, [blam], [blam])
        P.reduce(lam_t[:, 0:1], lam_sb[:, 0, :], ALU.add, [blam], [blam])
        P.reduce(lam_t[:, 1:2], lam_sb[:, 2, :], ALU.add, [blam], [blam])
        P.act(lam_t[:, 2:4], lam_t[:, 0:2], AF.Exp, [blam], [blam])
        P.tt("dve", lam_t[:, 4:5], lam_t[:, 3:4], lam_t[:, 2:3], ALU.subtract, [blam], [blam])
        P.ts("dve", lam_t[:, 5:6], lam_t[:, 4:5], -lam_init, ALU.add, [blam], [blam])
        nlam = P.sbuf("nlam", [128, 1], F32)
        bnl = P.buf("nlam")
        P.mm(ps_y[:, 0:1], self_f[0:1, 0, :], lam_t[0:1, 5:6], True, True, [bcst, blam], [bpy])
        P.copy("dve", nlam[:], ps_y[:, 0:1], [bpy], [bnl])
        si = 0
        for h in range(NHD):
            k_t, v_t = kh[h % 2], vh[h % 2]
            bk, bv = bkh[h % 2], bvh[h % 2]
            P.dma("sp", k_t[:, 0:TC], kloc[:, h, TL:TT], (), [bk])
            P.dma("sp", v_t[:, 0:2, :], vloc[:, h, 16:18, :], (), [bv])
            for r in range(4):
                P.dma("sp", k_t[:, TC + r * TL:TC + (r + 1) * TL], kall4[r, :, h, 0:TL], (), [bk])
                P.dma("sp", v_t[:, 2 + 16 * r:2 + 16 * (r + 1), :], vall4[r, :, h, 0:16, :], (), [bv])
            for gi, (c0, n, isctx) in enumerate(GROUPS):
                ktl = [0, 1] if isctx else list(range(NKT))
                units = [(c, ki, kt) for ki, kt in enumerate(ktl) for c in range(2)]
                pend = []

                def stage1(u):
                    c, ki, kt = u
                    i_ = stage1.si
                    stage1.si += 1
                    s_ps, bs_ = ps_s[i_ % 4], bps[i_ % 4]
                    p_t, bp_ = pt[i_ % NP_], bpt[i_ % NP_]
                    P.mm(s_ps[:, 0:n], k_t[c * 64:(c + 1) * 64, kt * 128:(kt + 1) * 128],
                         qT_sb[c * 64:(c + 1) * 64, h, c0:c0 + n], True, True, [bk, bq], [bs_])
                    P.act(p_t[:, 0:n], s_ps[:, 0:n], AF.Exp, [bs_], [bp_], scale=0.125)
                    return p_t, bp_

                stage1.si = si

                def stage2(u, p_t, bp_):
                    c, ki, kt = u
                    po, bo_ = ps_o[c], bpo[c]
                    P.mm(po[:, 0:n], v_t[:, kt, :], p_t[:, 0:n], ki == 0, ki == len(ktl) - 1, [bv, bp_], [bo_])
                    if ki % 3 == 0:
                        P.mm(ps_l[0:2, 0:n], sel[:, c, :], p_t[:, 0:n], c == 0 and ki == 0, False, [bcst, bp_], [bpl])
                    elif ki == 1:
                        P.copy("dve", accl[c][:, 0:n], p_t[:, 0:n], [bp_], [baccl[c]])
                    else:
                        P.tt("dve", accl[c][:, 0:n], accl[c][:, 0:n], p_t[:, 0:n], ALU.add, [bp_, baccl[c]], [baccl[c]])
                    if ki == len(ktl) - 1:
                        P.copy("dve", accb[c][:, 0:n], accl[c][:, 0:n], [baccl[c]], [baccb[c]])
                        P.mm(ps_l[0:2, 0:n], sel[:, c, :], accb[c][:, 0:n], False, c == 1, [bcst, baccb[c]], [bpl])
                        P.copy("act", o_sb[c][:, 0:n], po[:, 0:n], [bo_], [bosb[c]])

                PP = 2 * PIPE
                for idx in range(0, len(units) + PP, 2):
                    for i2 in (idx, idx + 1):
                        if i2 < len(units):
                            pend.append(stage1(units[i2]))
                    for i2 in (idx - PP, idx - PP + 1):
                        if 0 <= i2 < len(units):
                            stage2(units[i2], *pend[i2])
                si = stage1.si
                P.copy("act", l_sb[:, 0:n], ps_l[0:2, 0:n], [bpl], [blsb])
                P.recip(l_sb[:, 0:n], l_sb[:, 0:n], [blsb], [blsb])
                for c in range(2):
                    P.mm(ps_rb[c][:, 0:n], self_f[:, c, :], l_sb[:, 0:n], True, True, [bcst, blsb], [bprb[c]])
                    P.tt("dve", o_sb[c][:, 0:n], o_sb[c][:, 0:n], ps_rb[c][:, 0:n], ALU.mult, [bosb[c], bprb[c]],
                         [bosb[c]])
                P.stt("dve", od[:, 0:n], o_sb[1][:, 0:n], nlam[:, 0:1], o_sb[0][:, 0:n], ALU.mult, ALU.add,
                      [bosb[0], bosb[1], bnl], [bod])
                P.act(sqb[:, 0:n], od[:, 0:n], AF.Square, [bod], [bsqb])
                P.mm(ps_rb[0][:, 0:n], onesb[:], sqb[:, 0:n], True, True, [bcst, bsqb], [bprb[0]])
                P.act(rs[:, 0:n], ps_rb[0][:, 0:n], AF.Sqrt, [bprb[0]], [brs], bias=EPS, scale=1.0)
                P.recip(rs[:, 0:n], rs[:, 0:n], [brs], [brs])
                P.stt("dve", aoT[:, h, c0:c0 + n], od[:, 0:n], sg[:, 0:1], rs[:, 0:n], ALU.mult, ALU.mult,
                      [bod, bcst, brs], [bao])
        outs = []
        for gi, (c0, n, isctx) in enumerate(GROUPS):
            P.dma("sp", xg[:, :, 0:n], xT[:, :, c0:c0 + n], (), [bx])
            for oc in range(KD):
                for h in range(NHD):
                    P.mm(ps_y[:, 0:n], wo_sb[:, h, oc * 128:(oc + 1) * 128], aoT[:, h, c0:c0 + n], h == 0,
                         h == NHD - 1, [bwo, bao], [bpy])
                P.stt("dve", xg[:, oc, 0:n], ps_y[:, 0:n], M["g"][isctx][:, oc:oc + 1], xg[:, oc, 0:n],
                      ALU.mult, ALU.add, [bpy, M["buf"], bx], [bx])
            bo = P.buf("out")
            P.dma("sp", xo[:, :, c0:c0 + n], xg[:, :, 0:n], [bx], [bo])
            outs.append(bo)
        P.wait_all("sp", outs)
    return None


HALO = 15
TE = TL + 2 * HALO
TCE = TC + 2 * HALO


def build_conv(P, T):
    nc = T
    xT = din(nc, "xT", [128, KD, TE + TC])
    modT = din(nc, "modT", [128, 48, 2])
    ng = din(nc, "ng", [128, KD])
    w1 = din(nc, "w1", [128, KD, 2 * D])
    b1 = din(nc, "b1", [128, 16])
    wdw = din(nc, "wdw", [128, KD, 31])
    cvec = din(nc, "cvec", [128, 4, KD])
    w2 = din(nc, "w2", [128, KD, D])
    hv = din(nc, "hv", [128, 2])
    xo = dout(nc, "xo", [128, KD, TT])
    with contextlib.nullcontext():
        P.phase()
        C = NormCtx(P)
        M = load_mod(P, modT, ng, 0)
        w1_sb = P.sbuf("w1_sb", [128, KD, 2 * D], BF16)
        w2_sb = P.sbuf("w2_sb", [128, KD, D], BF16)
        b1_sb = P.sbuf("b1_sb", [128, 16], F32)
        wdw_sb = P.sbuf("wdw_sb", [128, KD, 31], F32)
        cv = P.sbuf("cv", [128, 4, KD], F32)
        gb = P.sbuf("gb", [128, 2, KD], F32)
        hv_sb = P.sbuf("hv_sb", [128, 2], F32)
        uT = P.sbuf("uT", [128, KD, TE], F32)
        uc = P.sbuf("uc", [128, KD, TCE], F32)
        xg = P.sbuf("xg", [128, KD, 512], F32)
        hg = P.sbuf("hg", [128, KD, 512], BF16)
        sg_ = [P.sbuf(f"sg{i}", [128, 512], F32) for i in range(2)]
        N2 = 256
        acc = P.sbuf("acc", [128, KD, N2], F32)
        ctmps = {oc: [P.sbuf(f"ctmp{oc}_{i}", [128, N2], F32) for i in range(2)] for oc in range(5, KD)}
        bcts = {oc: P.bufs(2, f"ctmp{oc}") for oc in range(5, KD)}
        jn = P.sbuf("jn", [128, 1], F32)
        baccs = P.bufs(KD, "accs")
        ps_a = [P.psum(f"ps_a{i}", [128, 512], F32) for i in range(2)]
        ps_g = [P.psum(f"ps_g{i}", [128, 512], F32) for i in range(2)]
        ps_m = P.psum("ps_m", [128, 512], F32)
        ps_y = [P.psum(f"ps_y{i}", [128, 512], F32) for i in range(2)]
        bw1, bw2, bcst, bhv = P.buf("w1"), P.buf("w2"), P.buf("cst"), P.buf("hv")
        bu, buc, bx, bhg, bacc = P.buf("u"), P.buf("uc"), P.buf("x"), P.buf("hg"), P.buf("acc")
        bsg, bpa, bpg, bpm, bpy = P.bufs(2, "sg"), P.bufs(2, "psa"), P.bufs(2, "psg"), P.buf("psm"), P.bufs(2, "psy")
        for k in range(KD):
            P.dma("pool", w1_sb[:, k, :], w1[:, k, :], (), [bw1])
        for k in range(0, KD, 2):
            P.dma("pool", w2_sb[:, k:k + 2, :], w2[:, k:k + 2, :], (), [bw2])
        P.dma("sp", b1_sb[:], b1, (), [bcst])
        P.dma("sp", wdw_sb[:], wdw, (), [bcst])
        P.dma("sp", cv[:], cvec, (), [bcst])
        P.dma("sp", hv_sb[:], hv, (), [bhv])
        for j in range(2):
            P.tt("dve", gb[:, j, :], M["g"][j], cv[:, 3, :], ALU.mult, [M["buf"], bcst], [bcst])
        P.memset("pool", uc[:], 0.0, [buc])
        groups1 = [(0, 512, 0), (512, 512, 0), (1024, 512, 0), (1536, 512, 0), (2048, 2 * HALO, 0), (TE, TC, 1)]
        ai = 0
        for (c0, n, isctx) in groups1:
            P.dma("sp", xg[:, :, 0:n], xT[:, :, c0:c0 + n], (), [bx])
            norm_mod(P, C, xg[:, :, 0:n], n, M["a"][isctx], M["b"][isctx], M["buf"], hg[:, :, 0:n], bx, bhg)
            for oc in range(KD):
                pa, pg = ps_a[ai % 2], ps_g[ai % 2]
                bpa_, bpg_ = bpa[ai % 2], bpg[ai % 2]
                s_t, bs_ = sg_[ai % 2], bsg[ai % 2]
                ai += 1
                for k in range(KD):
                    P.mm(pa[:, 0:n], w1_sb[:, k, oc * 128:(oc + 1) * 128], hg[:, k, 0:n], k == 0, k == KD - 1,
                         [bw1, bhg], [bpa_])
                for k in range(KD):
                    P.mm(pg[:, 0:n], w1_sb[:, k, D + oc * 128:D + (oc + 1) * 128], hg[:, k, 0:n], k == 0, k == KD - 1,
                         [bw1, bhg], [bpg_])
                P.act(s_t[:, 0:n], pg[:, 0:n], AF.Sigmoid, [bpg_, bcst], [bs_], bias=b1_sb[:, 8 + oc:9 + oc], scale=1.0)
                if isctx:
                    dst, bd = uc[:, oc, HALO:HALO + TC], buc
                else:
                    dst, bd = uT[:, oc, c0:c0 + n], bu
                P.stt("dve", dst, pa[:, 0:n], b1_sb[:, oc:oc + 1], s_t[:, 0:n], ALU.add, ALU.mult,
                      [bpa_, bcst, bs_], [bd])
        for oc in range(KD):
            P.ts("dve", uT[:, oc, 0:HALO], uT[:, oc, 0:HALO], hv_sb[:, 0:1], ALU.mult, [bu, bhv], [bu])
            P.ts("dve", uT[:, oc, HALO + TL:TE], uT[:, oc, HALO + TL:TE], hv_sb[:, 1:2], ALU.mult, [bu, bhv], [bu])
        outs = []
        yi = 0
        groups2 = [(c0, N2, 0) for c0 in range(0, TL, N2)] + [(0, TC, 1)]
        for gi, (c0, n, isctx) in enumerate(groups2):
            src, bsrc = (uc, buc) if isctx else (uT, bu)
            NDV = 5
            for oc in range(NDV):
                P.ts("dve", acc[:, oc, 0:n], src[:, oc, c0:c0 + n], wdw_sb[:, oc, 0:1], ALU.mult, [bsrc, bcst],
                     [baccs[oc], bacc])
            for oc in range(NDV, KD):
                P.act(acc[:, oc, 0:n], src[:, oc, c0:c0 + n], AF.Identity, [bsrc, bcst], [baccs[oc], bacc],
                      bias=cv[:, 0, oc:oc + 1], scale=wdw_sb[:, oc, 0:1])
                P.act(ctmps[oc][1][:, 0:n], src[:, oc, c0 + 1:c0 + 1 + n], AF.Identity, [bsrc, bcst], [bcts[oc][1]],
                      scale=wdw_sb[:, oc, 1:2])
            for k in range(1, 31):
                for oc in range(NDV):
                    P.stt("dve", acc[:, oc, 0:n], src[:, oc, c0 + k:c0 + k + n], wdw_sb[:, oc, k:k + 1],
                          acc[:, oc, 0:n], ALU.mult, ALU.add, [bsrc, bcst, baccs[oc]], [baccs[oc]])
                for oc in range(NDV, KD):
                    if k + 1 < 31:
                        P.act(ctmps[oc][(k + 1) % 2][:, 0:n], src[:, oc, c0 + k + 1:c0 + k + 1 + n], AF.Identity,
                              [bsrc, bcst], [bcts[oc][(k + 1) % 2]], scale=wdw_sb[:, oc, k + 1:k + 2])
                    P.tt("pool", acc[:, oc, 0:n], acc[:, oc, 0:n], ctmps[oc][k % 2][:, 0:n], ALU.add,
                         [bcts[oc][k % 2], baccs[oc]], [baccs[oc]])
            for oc in range(NDV):
                P.ts("dve", acc[:, oc, 0:n], acc[:, oc, 0:n], cv[:, 0, oc:oc + 1], ALU.add,
                     [baccs[oc], bcst], [baccs[oc]])
            P.op("dve", lambda e: e.memset(jn[:], 0.0), baccs, [bacc])
            P.act(C.sq[:, :, 0:n], acc[:, :, 0:n], AF.Copy, [bacc], [C.b_sq])
            for k in range(KD):
                P.mm(ps_m[:, 0:n], C.ones[:], C.sq[:, k, 0:n], k == 0, k == KD - 1, [C.b_ones, C.b_sq], [bpm])
            P.tt("dve", acc[:, :, 0:n], acc[:, :, 0:n],
                 view(ps_m[:, 0:n], [list(ps_m[:].ap[0]), [0, KD], [1, n]]), ALU.subtract, [bacc, bpm], [bacc])
            norm_mod(P, C, acc[:, :, 0:n], n, cv[:, 1, :], cv[:, 2, :], bcst, hg[:, :, 0:n], bacc, bhg, func=AF.Silu)
            xc0 = TE + c0 if isctx else HALO + c0
            P.dma("sp", xg[:, :, 0:n], xT[:, :, xc0:xc0 + n], (), [bx])
            for oc in range(KD):
                py, by_ = ps_y[yi % 2], bpy[yi % 2]
                yi += 1
                for k in range(KD):
                    P.mm(py[:, 0:n], w2_sb[:, k, oc * 128:(oc + 1) * 128], hg[:, k, 0:n], k == 0, k == KD - 1,
                         [bw2, bhg], [by_])
                P.stt("dve", xg[:, oc, 0:n], py[:, 0:n], M["g"][isctx][:, oc:oc + 1], xg[:, oc, 0:n],
                      ALU.mult, ALU.add, [by_, M["buf"], bx], [bx])
                P.ts("dve", xg[:, oc, 0:n], xg[:, oc, 0:n], gb[:, isctx, oc:oc + 1], ALU.add, [bx, bcst], [bx])
            bo = P.buf("out")
            oc0 = TL + c0 if isctx else c0
            P.dma("sp", xo[:, :, oc0:oc0 + n], xg[:, :, 0:n], [bx], [bo])
            outs.append(bo)
        P.wait_all("sp", outs)
    return None


def win_masks():
    kk = np.arange(128)[:, None, None]
    r = np.arange(6)[None, :, None]
    qq = np.arange(512)[None, None, :]
    return (np.abs(128 * (r - 1) + kk - qq) <= 128).astype(np.float32).astype(NPBF)


def gather_kv_win(kTs, vPs):
    out = []
    for c in range(NCORES):
        q = c % 4
        zk = np.zeros_like(kTs[c][:, :, 0:128])
        zv = np.zeros_like(vPs[c][:, 0:1])
        kprev = kTs[c - 1][:, :, TL - 128:TL] if q > 0 else zk
        knext = kTs[c + 1][:, :, 0:128] if q < 3 else zk
        vprev = vPs[c - 1][:, 15:16] if q > 0 else zv
        vnext = vPs[c + 1][:, 0:1] if q < 3 else zv
        kT = np.concatenate([kTs[c][:, :, TL:TT], kprev, kTs[c][:, :, 0:TL], knext], axis=2)
        vP = np.concatenate([vPs[c][:, 16:18], vprev, vPs[c][:, 0:16], vnext], axis=1)
        out.append((np.ascontiguousarray(kT), np.ascontiguousarray(vP)))
    return out


def pre_inputs(i, xs, mods, wqkv, qg, kg, nq, nk, cos, sin, norm1_g):
    ident = np.eye(128, dtype=np.float32).astype(NPBF)
    gvec = np.concatenate([np.tile(qg, nq), np.tile(kg, nk)]).astype(np.float32)
    gvec = np.ascontiguousarray(np.broadcast_to(gvec[None, :], (128, gvec.size)))
    ng1 = colv(norm1_g)
    w = wl(wqkv)
    return [{"xT": xs[c], "modT": mods[c // 4][i], "ng": ng1, "wqkv": w, "gvec": gvec,
             "cs": cs_for_core(cos, sin, c % 4), "ident": ident} for c in range(NCORES)], ng1


def layer_win(i, xs, mods, inp, cos, sin):
    maps, ng1 = pre_inputs(i, xs, mods, inp["swa_w_qkv"][0], inp["swa_q_g"][0], inp["swa_k_g"][0], 16, 4,
                           cos, sin, inp["norm1_g"][i])
    res = run(prog("pre_gqa", build_pre_gqa, "gqa"), maps)
    kv = gather_kv_win([r["kT"] for r in res], [r["vP"] for r in res])
    wo = np.ascontiguousarray(inp["swa_w_o"][0].reshape(16, 64, D).transpose(1, 0, 2))
    masks = win_masks()
    sink = np.ascontiguousarray(inp["swa_sink"][0].reshape(1, 16))
    maps = [{"xT": xs[c], "modT": mods[c // 4][i], "ng": ng1, "qT": res[c]["qT"], "kT": kv[c][0], "vP": kv[c][1],
             "wo": wo, "masks": masks, "sink": sink} for c in range(NCORES)]
    res2 = run(prog("att", build_att, "win", False), maps)
    return [r["xo"] for r in res2]


def layer_diff(i, xs, mods, inp, cos, sin):
    maps, ng1 = pre_inputs(i, xs, mods, inp["diff_w_qkv"][0], inp["diff_q_g"][0], inp["diff_k_g"][0], 16, 16,
                           cos, sin, inp["norm1_g"][i])
    res = run(prog("pre_gqa", build_pre_gqa, "diff"), maps)
    kv = gather_kv_dense([r["kT"] for r in res], [r["vP"] for r in res])
    kv = [(k, np.ascontiguousarray(v.transpose(0, 2, 1, 3))) for (k, v) in kv]
    wo = wl(inp["diff_w_o"][0])
    lamv = np.ascontiguousarray(np.stack([inp["diff_lam_q1"][0], inp["diff_lam_k1"][0], inp["diff_lam_q2"][0],
                                          inp["diff_lam_k2"][0]])[None])
    slg = np.ascontiguousarray(inp["diff_subln_g"][0].reshape(128, 1))
    lam_init = 0.8 - 0.6 * math.exp(-0.3 * i)
    maps = [{"xT": xs[c], "modT": mods[c // 4][i], "ng": ng1, "qT": res[c]["qT"], "kT": kv[c // 4][0],
             "vP": kv[c // 4][1], "wo": wo, "lamv": lamv, "slg": slg} for c in range(NCORES)]
    res2 = run(prog("att_diff", build_att_diff, lam_init), maps)
    return [r["xo"] for r in res2]


def layer_conv(i, xs, mods, inp):
    ng1 = colv(inp["norm1_g"][i])
    w1 = wl(inp["conv_w_pw1"][0])
    b1 = colv(inp["conv_b_pw1"][0])
    wdw = np.ascontiguousarray(inp["conv_w_dw"][0].T.reshape(KD, 128, 31).transpose(1, 0, 2))
    cvec = np.ascontiguousarray(np.stack([colv(inp["conv_b_dw"][0]), colv(inp["conv_ln_g"][0]),
                                          colv(inp["conv_ln_b"][0]), colv(inp["conv_b_pw2"][0])], axis=1))
    w2 = wl(inp["conv_w_pw2"][0])
    maps = []
    for c in range(NCORES):
        q = c % 4
        z = np.zeros((128, KD, HALO), np.float32)
        left = xs[c - 1][:, :, TL - HALO:TL] if q > 0 else z
        right = xs[c + 1][:, :, 0:HALO] if q < 3 else z
        xe = np.ascontiguousarray(np.concatenate([left, xs[c][:, :, 0:TL], right, xs[c][:, :, TL:TT]], axis=2))
        hv = np.ascontiguousarray(np.broadcast_to(np.array([[float(q > 0), float(q < 3)]], np.float32), (128, 2)))
        maps.append({"xT": xe, "modT": mods[c // 4][i], "ng": ng1, "w1": w1, "b1": b1, "wdw": wdw, "cvec": cvec,
                     "w2": w2, "hv": hv})
    res = run(prog("conv", build_conv), maps)
    return [r["xo"] for r in res]


def kernel(**inputs):
    inp = {k: np.asarray(v, dtype=np.float32) for k, v in inputs.items()}
    cos, sin = rope_tables()
    mods = run_mod(inp["c"], inp["c_ctx"], inp["mod_w"], inp["mod_b"])
    xs = shard_x(inp["x"], inp["ctx"])
    xs = layer_gqa_dense(0, xs, mods, inp, True, cos, sin)
    xs = layer_mlp(0, xs, mods, inp, True)
    xs = layer_conv(1, xs, mods, inp)
    xs = layer_mlp(1, xs, mods, inp, True)
    xs = layer_diff(2, xs, mods, inp, cos, sin)
    xs = layer_mlp(2, xs, mods, inp, True)
    xs = layer_win(3, xs, mods, inp, cos, sin)
    xs = layer_mlp(3, xs, mods, inp, False)
    return unshard_x(xs)


GROUPS4 = [[0, 1, 2, 3], [4, 5, 6, 7]]


def build_fused(stop_after=99):
    nc = new_nc()
    I = {}
    step = [0]

    class Stop(Exception):
        pass

    def chk():
        step[0] += 1
        if step[0] > stop_after:
            raise Stop()

    def inp(name, shape, dt=F32):
        I[name] = din(nc, name, shape, dt)
        return I[name]

    xT_in = inp("xT", [128, KD, TT])
    inp("cT", [128, KD, 2])
    inp("mod_w", [DEPTH, 128, KD, 1536])
    inp("mod_b", [DEPTH, 128, 12])
    inp("ng1", [DEPTH, 128, KD])
    inp("ng2", [DEPTH, 128, KD])
    inp("wu", [DEPTH, 128, KD, DFF])
    inp("wd", [DEPTH, 128, 32, D])
    inp("cs", [128, 16, 2, 32])
    inp("ident", [128, 128], BF16)
    inp("gqa_wqkv", [128, KD, 1536]); inp("gqa_gvec", [128, 1280]); inp("gqa_wo", [64, 16, D])
    inp("swa_wqkv", [128, KD, 1536]); inp("swa_gvec", [128, 1280]); inp("swa_wo", [64, 16, D])
    inp("masks", [128, 6, 512], BF16); inp("sink", [1, 16]); inp("selv", [128, 2, 4])
    inp("diff_wqkv", [128, KD, 3072]); inp("diff_gvec", [128, 2048]); inp("diff_wo", [128, 8, D])
    inp("lamv", [1, 4, 64]); inp("slg", [128, 1])
    inp("w1", [128, KD, 2 * D]); inp("b1", [128, 16]); inp("wdw", [128, KD, 31]); inp("cvec", [128, 4, KD])
    inp("w2", [128, KD, D]); inp("hv", [128, 2])
    xo = dout(nc, "xo", [128, KD, TT])
    modT = dint(nc, "modT_i", [DEPTH, 128, 48, 2])
    xA = dint(nc, "xA", [128, KD, TT])
    xB = dint(nc, "xB", [128, KD, TT])
    qT = dint(nc, "qT_i", [128, KD, TT], BF16)
    kloc = dint(nc, "kloc", [4, 128, TT], BF16)
    vloc = dint(nc, "vloc", [2, 128, 9, 4, 65], BF16)
    kall = dint(nc, "kall", [4, 512, TT], BF16)
    vall = dint(nc, "vall", [2, 512, 9, 4, 65], BF16)
    kloc2 = dint(nc, "kloc2", [8, 128, TT], BF16)
    vloc2 = dint(nc, "vloc2", [8, 128, 18, 128], BF16)
    kall2 = dint(nc, "kall2", [8, 512, TT], BF16)
    vall2 = dint(nc, "vall2", [8, 512, 18, 128], BF16)
    xe_loc = dint(nc, "xe_loc", [128, KD * 2 * HALO])
    xe_all = dint(nc, "xe_all", [512, KD * 2 * HALO])
    xext = dint(nc, "xext", [128, KD, TE + TC])
    with contextlib.ExitStack() as st:
        P = Prog(nc, st)
        P.init_arena()
        try:
            _fused_body(P, I, chk, modT, xT_in, xA, xB, xo, qT, kloc, vloc, kall, vall, kloc2, vloc2, kall2, vall2,
                        xe_loc, xe_all, xext)
        except Stop:
            P.phase()
            stg = P.sbuf("stg", [128, KD, 512], F32)
            bs_ = P.buf("stg")
            for c0 in range(0, TT, 512):
                n = min(512, TT - c0)
                P.dma("sp", stg[:, :, 0:n], xT_in[:, :, c0:c0 + n], (), [bs_])
                P.dma("sp", xo[:, :, c0:c0 + n], stg[:, :, 0:n], [bs_], [P.buf("o")])
        P.barrier()
        P.emit()
    return nc


def _fused_body(P, I, chk, modT, xT_in, xA, xB, xo, qT, kloc, vloc, kall, vall, kloc2, vloc2, kall2, vall2,
                xe_loc, xe_all, xext):
    if True:
        chk()
        modloc = dint(P.nc, "modloc", [128, DEPTH * 24])
        modall = dint(P.nc, "modall", [512, DEPTH * 24])
        build_mod(P, {"cT": I["cT"], "mod_w": I["mod_w"], "mod_b": I["mod_b"], "modloc": modloc})
        P.coll("AllGather", GROUPS4, [(modloc, modall)])
        P.phase()
        mg = P.sbuf("mg", [128, 4, DEPTH, 12, 2], F32)
        bmg = P.buf("mg")
        for r in range(4):
            P.dma("sp", mg[:, r].rearrange("p a b c -> p (a b c)"), modall[r * 128:(r + 1) * 128, :], (), [bmg])
        for l in range(DEPTH):
            for r in range(4):
                P.dma("sp", modT[l][:, r * 12:(r + 1) * 12, :], mg[:, r, l], [bmg], [P.buf("modT")])

        kview = kloc.rearrange("g p t -> p g t")
        vtiles = [vloc[t // 9, :, t % 9] for t in range(18)]

        def cc_gqa():
            P.coll("AllGather", GROUPS4, [(kloc[g], kall[g]) for g in range(4)] +
                   [(vloc[hf].rearrange("p t g d -> p (t g d)"), vall[hf].rearrange("p t g d -> p (t g d)"))
                    for hf in range(2)])

        def mlp(l, xin, xout, need_ctx):
            build_mlp(P, {"xT": xin, "modT": modT[l], "ng": I["ng2"][l], "wu": I["wu"][l], "wd": I["wd"][l],
                          "xo": xout}, need_ctx)

        chk()
        build_pre_gqa(P, {"xT": xT_in, "modT": modT[0], "ng": I["ng1"][0], "wqkv": I["gqa_wqkv"],
                          "gvec": I["gqa_gvec"], "cs": I["cs"], "ident": I["ident"], "qT": qT, "kT": kview,
                          "vP_tiles": vtiles}, "gqa")
        chk()
        cc_gqa()
        build_att(P, {"xT": xT_in, "modT": modT[0], "ng": I["ng1"][0], "qT": qT, "kloc": kloc, "vloc": vloc,
                      "kall": kall, "vall": vall, "wo": I["gqa_wo"], "xo": xA}, "dense", True)
        chk()
        mlp(0, xA, xB, True)
        chk()
        P.phase()
        edge = P.sbuf("edge", [128, KD, 2, HALO], F32)
        bed = P.buf("edge")
        P.dma("sp", edge[:, :, 0, :], xB[:, :, 0:HALO], (), [bed])
        P.dma("sp", edge[:, :, 1, :], xB[:, :, TL - HALO:TL], (), [bed])
        P.dma("sp", xe_loc, edge[:].rearrange("p a b c -> p (a b c)"), [bed], [P.buf("xe")])
        P.coll("AllGather", GROUPS4, [(xe_loc, xe_all)])
        P.phase()
        ea = P.sbuf("ea", [128, 4, KD, 2, HALO], F32)
        sv = P.sbuf("sv2", [128, 2, 4], F32)
        hl = P.sbuf("hl", [128, 2, KD, HALO], F32)
        bea, bsv, bhl = P.buf("ea"), P.buf("sv2"), P.buf("hl")
        for r in range(4):
            P.dma("sp", ea[:, r].rearrange("p a b c -> p (a b c)"), xe_all[r * 128:(r + 1) * 128, :], (), [bea])
        P.dma("sp", sv[:], I["selv"], (), [bsv])
        for side in range(2):
            src_i = 1 - side
            P.ts("dve", hl[:, side], ea[:, 0, :, src_i, :], sv[:, side, 0:1], ALU.mult, [bea, bsv], [bhl])
            for r in range(1, 4):
                P.stt("dve", hl[:, side], ea[:, r, :, src_i, :], sv[:, side, r:r + 1], hl[:, side], ALU.mult, ALU.add,
                      [bea, bsv, bhl], [bhl])
        bxe = P.buf("xext")
        P.dma("sp", xext[:, :, 0:HALO], hl[:, 0], [bhl], [bxe])
        P.dma("sp", xext[:, :, HALO + TL:TE], hl[:, 1], [bhl], [bxe])
        stage = P.sbuf("stage", [128, KD, 512], F32)
        bst = P.buf("stage")
        for c0 in range(0, TT, 512):
            n = min(512, TT - c0)
            P.dma("sp", stage[:, :, 0:n], xB[:, :, c0:c0 + n], (), [bst])
            d0 = HALO + c0 if c0 < TL else TE + (c0 - TL)
            P.dma("sp", xext[:, :, d0:d0 + n], stage[:, :, 0:n], [bst], [bxe])
        chk()
        build_conv(P, {"xT": xext, "modT": modT[1], "ng": I["ng1"][1], "w1": I["w1"], "b1": I["b1"],
                       "wdw": I["wdw"], "cvec": I["cvec"], "w2": I["w2"], "hv": I["hv"], "xo": xA})
        chk()
        mlp(1, xA, xB, True)
        chk()
        build_pre_gqa(P, {"xT": xB, "modT": modT[2], "ng": I["ng1"][2], "wqkv": I["diff_wqkv"],
                          "gvec": I["diff_gvec"], "cs": I["cs"], "ident": I["ident"], "qT": qT,
                          "kT": kloc2.rearrange("g p t -> p g t"),
                          "vP_tiles": [vloc2.rearrange("h p t d -> p h t d")[:, :, t, :] for t in range(18)]}, "diff")
        chk()
        P.coll("AllGather", GROUPS4, [(kloc2[h], kall2[h]) for h in range(8)] +
               [(vloc2[h].rearrange("p t d -> p (t d)"), vall2[h].rearrange("p t d -> p (t d)")) for h in range(8)])
        build_att_diff(P, {"xT": xB, "modT": modT[2], "ng": I["ng1"][2], "qT": qT, "kloc": kloc2, "vloc": vloc2,
                           "kall": kall2, "vall": vall2, "wo": I["diff_wo"], "lamv": I["lamv"], "slg": I["slg"],
                           "xo": xA}, 0.8 - 0.6 * math.exp(-0.3 * 2))
        chk()
        mlp(2, xA, xB, True)
        chk()
        build_pre_gqa(P, {"xT": xB, "modT": modT[3], "ng": I["ng1"][3], "wqkv": I["swa_wqkv"],
                          "gvec": I["swa_gvec"], "cs": I["cs"], "ident": I["ident"], "qT": qT, "kT": kview,
                          "vP_tiles": vtiles}, "gqa")
        cc_gqa()
        build_att(P, {"xT": xB, "modT": modT[3], "ng": I["ng1"][3], "qT": qT, "kloc": kloc, "vloc": vloc,
                      "kall": kall, "vall": vall, "wo": I["swa_wo"], "masks": I["masks"], "sink": I["sink"],
                      "selv": I["selv"], "xo": xA}, "win", False)
        chk()
        mlp(3, xA, xo, False)


def kernel(**inputs):
    inp = {k: np.asarray(v, dtype=np.float32) for k, v in inputs.items()}
    cos, sin = rope_tables()
    xs = shard_x(inp["x"], inp["ctx"])
    ident = np.eye(128, dtype=np.float32).astype(NPBF)

    def gv(qg, kg, nq, nk):
        g = np.concatenate([np.tile(qg, nq), np.tile(kg, nk)]).astype(np.float32)
        return np.ascontiguousarray(np.broadcast_to(g[None, :], (128, g.size)))

    def wo64(w):
        return np.ascontiguousarray(w.reshape(16, 64, D).transpose(1, 0, 2))

    shared = {
        "ng1": np.stack([colv(inp["norm1_g"][l]) for l in range(DEPTH)]),
        "ng2": np.stack([colv(inp["norm2_g"][l]) for l in range(DEPTH)]),
        "wu": np.stack([wl(inp["mlp_up"][l]) for l in range(DEPTH)]),
        "wd": np.stack([wl(inp["mlp_down"][l]) for l in range(DEPTH)]),
        "ident": ident,
        "gqa_wqkv": wl(inp["gqa_w_qkv"][0]), "gqa_gvec": gv(inp["gqa_q_g"][0], inp["gqa_k_g"][0], 16, 4),
        "gqa_wo": wo64(inp["gqa_w_o"][0]),
        "swa_wqkv": wl(inp["swa_w_qkv"][0]), "swa_gvec": gv(inp["swa_q_g"][0], inp["swa_k_g"][0], 16, 4),
        "swa_wo": wo64(inp["swa_w_o"][0]),
        "masks": win_masks(), "sink": np.ascontiguousarray(inp["swa_sink"][0].reshape(1, 16)),
        "diff_wqkv": wl(inp["diff_w_qkv"][0]), "diff_gvec": gv(inp["diff_q_g"][0], inp["diff_k_g"][0], 16, 16),
        "diff_wo": wl(inp["diff_w_o"][0]),
        "lamv": np.ascontiguousarray(np.stack([inp["diff_lam_q1"][0], inp["diff_lam_k1"][0], inp["diff_lam_q2"][0],
                                               inp["diff_lam_k2"][0]])[None]),
        "slg": np.ascontiguousarray(inp["diff_subln_g"][0].reshape(128, 1)),
        "w1": wl(inp["conv_w_pw1"][0]), "b1": colv(inp["conv_b_pw1"][0]),
        "wdw": np.ascontiguousarray(inp["conv_w_dw"][0].T.reshape(KD, 128, 31).transpose(1, 0, 2)),
        "cvec": np.ascontiguousarray(np.stack([colv(inp["conv_b_dw"][0]), colv(inp["conv_ln_g"][0]),
                                               colv(inp["conv_ln_b"][0]), colv(inp["conv_b_pw2"][0])], axis=1)),
        "w2": wl(inp["conv_w_pw2"][0]),
    }
    maps = []
    for c in range(NCORES):
        b, q = c // 4, c % 4
        selv = np.zeros((128, 2, 4), np.float32)
        if q > 0:
            selv[:, 0, q - 1] = 1.0
        if q < 3:
            selv[:, 1, q + 1] = 1.0
        m = dict(shared)
        mc = slice(q * 1536, (q + 1) * 1536)
        m.update({"mod_w": np.ascontiguousarray(inp["mod_w"][:, :, mc].reshape(DEPTH, KD, 128, 1536).transpose(0, 2, 1, 3)),
                  "mod_b": np.ascontiguousarray(inp["mod_b"][:, mc].reshape(DEPTH, 12, 128).transpose(0, 2, 1))})
        m.update({"xT": xs[c], "cT": np.ascontiguousarray(np.stack([colv(inp["c"][b]), colv(inp["c_ctx"])], axis=-1)),
                  "cs": cs_for_core(cos, sin, q), "selv": selv,
                  "hv": np.ascontiguousarray(np.broadcast_to(np.array([[float(q > 0), float(q < 3)]], np.float32),
                                                             (128, 2)))})
        maps.append(m)
    nc = prog("fused", build_fused)
    res = run(nc, maps)
    return unshard_x([r["xo"] for r in res])
```

```python
import contextlib
import math
import numpy as np
import ml_dtypes
import concourse.bass as bass
import concourse.mybir as mybir
from concourse.bass_utils import run_bass_kernel_spmd

F32 = mybir.dt.float32
BF16 = mybir.dt.bfloat16
ALU = mybir.AluOpType
AF = mybir.ActivationFunctionType
AX = mybir.AxisListType
NPBF = ml_dtypes.bfloat16

NCORES = 8
D = 1024
KD = 8
TL = 2048
TC = 256
TT = TL + TC
SEQ = 8192
DFF = 4096
EPS = 1e-6
DEPTH = 4

EPOCH = 16000
NDMA = 20
ENGS = ("pe", "act", "dve", "pool", "sp")
SAME_ENGINE_SYNC = {"pe": False, "act": True, "dve": True, "pool": True, "sp": True}


class Buf:
    __slots__ = ("name", "w", "r")

    def __init__(self, name):
        self.name = name
        self.w = None
        self.r = {}


class Prog:
    def __init__(self, nc, stack):
        self.nc = nc
        self.stack = stack
        self.streams = {e: [] for e in ENGS}
        self.cnt = {e: 0 for e in ENGS}
        self.dcnt = {e: 0 for e in ENGS}
        self.known = {e: {} for e in ENGS}
        self.psem = {e: [] for e in ENGS}
        self.dsem = {e: [stack.enter_context(nc.semaphore(f"d_{e}_{i}")) for i in range(NDMA)]
                     for e in ("sp", "act", "pool")}
        self.nbuf = 0

    def init_arena(self, sb_bytes=204 * 1024):
        self.sb_cap = sb_bytes
        self.ar_f32 = self.stack.enter_context(self.nc.sbuf_tensor("arena", [128, sb_bytes // 4], F32))[:]
        self.ar_bf = self.ar_f32.bitcast(BF16)
        self.ps_f32 = self.stack.enter_context(self.nc.psum_tensor("psarena", [128, 4096], F32))[:]
        self.ps_bf = self.ps_f32.bitcast(BF16)
        self.sb_off = 0
        self.ps_off = 0
        self.extra = []

    @staticmethod
    def _carve(base, off_elems, shape):
        dims = [[base.ap[0][0], shape[0]]]
        rev, st_ = [], 1
        for n_ in reversed(shape[1:]):
            rev.append([st_, n_])
            st_ *= n_
        return bass.AP(base.tensor, base.offset + off_elems, dims + list(reversed(rev)))

    def barrier(self):
        toks = list(self.extra)
        for e in ENGS:
            n = self.cnt[e]
            if n > 0:
                ep = (n - 1) // EPOCH
                toks.append((("p", e, ep), self.psem[e][ep], (n - 1) % EPOCH + 1))
        for q, sems in self.dsem.items():
            for slot in range(NDMA):
                c = (self.dcnt[q] - slot + NDMA - 1) // NDMA
                if c > 0:
                    toks.append((("d", q, slot), sems[slot], 16 * c))
        for e in ENGS:
            waits = self._waits(e, (), (), toks)
            self.streams[e].append((waits, None, None, 0))

    def phase(self):
        self.barrier()
        self.sb_off = 0
        self.ps_off = 0

    def coll(self, kind, groups, pairs):
        self.barrier()
        if not hasattr(self, "ccsem"):
            self.ccsem = self.stack.enter_context(self.nc.semaphore("ccsem"))
            self.ccn = 0
        for (src, dst) in pairs:
            self.ccn += 1
            tok = (("c",), self.ccsem, self.ccn)

            def fn(e, src=src, dst=dst):
                return e.collective_compute(kind, ALU.bypass, replica_groups=groups, ins=[src], outs=[dst])
            self.streams["pool"].append(([], fn, tok, 1))
        self.extra = [(("c",), self.ccsem, self.ccn)]
        self.barrier()

    def buf(self, name=None):
        self.nbuf += 1
        return Buf(name or f"b{self.nbuf}")

    def bufs(self, n, name="b"):
        return [self.buf(f"{name}{i}") for i in range(n)]

    def sbuf(self, name, shape, dtype):
        size = 4 if dtype == F32 else 2
        nbytes = size * int(np.prod(shape[1:]))
        off = (self.sb_off + 31) // 32 * 32
        self.sb_off = off + nbytes
        assert self.sb_off <= self.sb_cap, f"SBUF arena overflow at {name}: {self.sb_off}"
        return self._carve(self.ar_f32 if dtype == F32 else self.ar_bf, off // size, list(shape))

    def psum(self, name, shape, dtype):
        size = 4 if dtype == F32 else 2
        nbytes = (size * int(np.prod(shape[1:])) + 2047) // 2048 * 2048
        off = self.ps_off
        self.ps_off = off + nbytes
        assert self.ps_off <= 16384, f"PSUM arena overflow at {name}"
        return self._carve(self.ps_f32 if dtype == F32 else self.ps_bf, off // size, list(shape))

    def _waits(self, eng, reads, writes, extra=()):
        waits = []
        kn = self.known[eng]

        def need(tok):
            if tok is None:
                return
            key, _, val = tok
            if key[0] == "p" and key[1] == eng and not SAME_ENGINE_SYNC[eng]:
                return
            if kn.get(key, 0) >= val:
                return
            kn[key] = val
            waits.append(tok)

        for b in reads:
            need(b.w)
        for b in writes:
            need(b.w)
            for t in b.r.values():
                need(t)
        for t in extra:
            need(t)
        return waits

    def _commit(self, tok, reads, writes):
        key = tok[0]
        for b in reads:
            b.r[key] = tok
        for b in writes:
            b.w = tok
            b.r = {}

    def op(self, eng, fn, reads=(), writes=()):
        waits = self._waits(eng, reads, writes)
        n = self.cnt[eng]
        self.cnt[eng] += 1
        ep = n // EPOCH
        while len(self.psem[eng]) <= ep:
            self.psem[eng].append(self.stack.enter_context(self.nc.semaphore(f"p_{eng}_{len(self.psem[eng])}")))
        tok = (("p", eng, ep), self.psem[eng][ep], n % EPOCH + 1)
        self.streams[eng].append((waits, fn, tok, 1))
        self._commit(tok, reads, writes)
        return tok

    def dma(self, q, out, in_, reads=(), writes=()):
        j = self.dcnt[q]
        self.dcnt[q] += 1
        slot, rnd = j % NDMA, j // NDMA
        sem = self.dsem[q][slot]
        key = ("d", q, slot)
        extra = [(key, sem, 16 * rnd)] if rnd > 0 else []
        waits = self._waits(q, reads, writes, extra)
        tok = (key, sem, 16 * (rnd + 1))
        self.streams[q].append((waits, lambda e: e.dma_start(out=out, in_=in_), tok, 16))
        self._commit(tok, reads, writes)
        return tok

    def wait_all(self, eng, bufs):
        waits = self._waits(eng, (), bufs)
        self.streams[eng].append((waits, None, None, 0))

    def emit(self):
        nc = self.nc
        with nc.Block() as block:
            def run(stream):
                def body(e):
                    for waits, fn, tok, inc in stream:
                        for (_, sem, val) in waits:
                            e.wait_ge(sem, val)
                        if fn is not None:
                            fn(e).then_inc(tok[1], inc)
                return body

            block.sync(run(self.streams["sp"]))
            block.scalar(run(self.streams["act"]))
            block.vector(run(self.streams["dve"]))
            block.gpsimd(run(self.streams["pool"]))
            block.tensor(run(self.streams["pe"]))

    def mm(self, out, lhsT, rhs, start, stop, r, w):
        return self.op("pe", lambda e: e.matmul(out, lhsT=lhsT, rhs=rhs, start=start, stop=stop), r, w)

    def tr(self, out, in_, ident, r, w):
        return self.op("pe", lambda e: e.transpose(out, in_, ident), r, w)

    def act(self, out, in_, func, r, w, bias=None, scale=None):
        kw = {}
        if bias is not None:
            kw["bias"] = bias
        if scale is not None:
            kw["scale"] = scale
        return self.op("act", lambda e: e.activation(out=out, in_=in_, func=func, **kw), r, w)

    def tt(self, eng, out, in0, in1, op, r, w):
        return self.op(eng, lambda e: e.tensor_tensor(out=out, in0=in0, in1=in1, op=op), r, w)

    def ts(self, eng, out, in0, s1, op0, r, w, s2=None, op1=None):
        if op1 is None:
            return self.op(eng, lambda e: e.tensor_scalar(out=out, in0=in0, scalar1=s1, scalar2=None, op0=op0), r, w)
        return self.op(eng, lambda e: e.tensor_scalar(out=out, in0=in0, scalar1=s1, scalar2=s2, op0=op0, op1=op1), r, w)

    def stt(self, eng, out, in0, scalar, in1, op0, op1, r, w):
        return self.op(eng, lambda e: e.scalar_tensor_tensor(out=out, in0=in0, scalar=scalar, in1=in1,
                                                             op0=op0, op1=op1), r, w)

    def copy(self, eng, out, in_, r, w):
        if eng == "act":
            return self.act(out, in_, AF.Copy, r, w)
        return self.op(eng, lambda e: e.tensor_copy(out=out, in_=in_), r, w)

    def recip(self, out, in_, r, w):
        return self.op("dve", lambda e: e.reciprocal(out=out, in_=in_), r, w)

    def reduce(self, out, in_, op, r, w):
        return self.op("dve", lambda e: e.tensor_reduce(out=out, in_=in_, axis=AX.X, op=op), r, w)

    def memset(self, eng, ap, val, w):
        return self.op(eng, lambda e: e.memset(ap, val), (), w)


def view(ap, dims):
    return bass.AP(ap.tensor, ap.offset, dims)


def new_nc():
    return bass.Bass("TRN2", target_bir_lowering=False)


def din(nc, name, shape, dt=F32):
    if isinstance(nc, dict):
        ap = nc[name]
        assert tuple(ap.shape) == tuple(shape), (name, tuple(ap.shape), tuple(shape))
        return ap
    return nc.dram_tensor(name, list(shape), dt, kind="ExternalInput").ap()


def dout(nc, name, shape, dt=F32):
    if isinstance(nc, dict):
        return din(nc, name, shape, dt)
    return nc.dram_tensor(name, list(shape), dt, kind="ExternalOutput").ap()


def dint(nc, name, shape, dt=F32):
    return nc.dram_tensor(name, list(shape), dt, kind="Internal").ap()


class NormCtx:
    def __init__(self, P, tag=""):
        self.P = P
        self.ones = P.sbuf("nm_ones" + tag, [128, 128], BF16)
        self.sq = P.sbuf("nm_sq" + tag, [128, KD, 512], BF16)
        self.rs = P.sbuf("nm_rs" + tag, [128, 512], F32)
        self.tmp = P.sbuf("nm_tmp" + tag, [128, KD, 512], F32)
        self.ps = P.psum("nm_ps" + tag, [128, 512], F32)
        self.b_ones = P.buf("nm_ones")
        self.b_sq = P.buf("nm_sq")
        self.b_rs = P.buf("nm_rs")
        self.b_tmp = P.buf("nm_tmp")
        self.b_ps = P.buf("nm_ps")
        P.memset("dve", self.ones[:], 1.0 / 1024.0, [self.b_ones])


def norm_mod(P, C, xg, n, a, b, bmod, hg, bx, bh, func=AF.Identity):
    P.act(C.sq[:, :, 0:n], xg, AF.Square, [bx], [C.b_sq])
    for k in range(KD):
        P.mm(C.ps[:, 0:n], C.ones[:], C.sq[:, k, 0:n], k == 0, k == KD - 1, [C.b_ones, C.b_sq], [C.b_ps])
    P.act(C.rs[:, 0:n], C.ps[:, 0:n], AF.Sqrt, [C.b_ps], [C.b_rs], bias=EPS, scale=1.0)
    P.recip(C.rs[:, 0:n], C.rs[:, 0:n], [C.b_rs], [C.b_rs])
    for k in range(KD):
        P.stt("dve", C.tmp[:, k, 0:n], xg[:, k, :], a[:, k:k + 1], C.rs[:, 0:n], ALU.mult, ALU.mult,
              [bx, C.b_rs, bmod], [C.b_tmp])
    for k in range(KD):
        P.act(hg[:, k, :], C.tmp[:, k, 0:n], func, [C.b_tmp, bmod], [bh], bias=b[:, k:k + 1], scale=1.0)


def load_mod(P, modT_d, ng_d, which):
    mod = P.sbuf("mod_sb", [128, 48, 2], F32)
    ng = P.sbuf("ng_sb", [128, KD], F32)
    aa = P.sbuf("mod_a", [128, 2, KD], F32)
    bb = P.sbuf("mod_b", [128, 2, KD], F32)
    gg = P.sbuf("mod_g", [128, 2, KD], F32)
    bm = P.buf("mod")
    P.dma("sp", mod[:], modT_d, (), [bm])
    P.dma("sp", ng[:], ng_d, (), [bm])
    base = which * 24
    for j in range(2):
        P.stt("dve", aa[:, j, :], mod[:, base + 8:base + 16, j], 1.0, ng[:], ALU.add, ALU.mult, [bm], [bm])
        P.copy("dve", bb[:, j, :], mod[:, base:base + 8, j], [bm], [bm])
        P.copy("dve", gg[:, j, :], mod[:, base + 16:base + 24, j], [bm], [bm])
    return {"a": [aa[:, 0, :], aa[:, 1, :]], "b": [bb[:, 0, :], bb[:, 1, :]],
            "g": [gg[:, 0, :], gg[:, 1, :]], "buf": bm}


PIPE = 4
GROUPS = [(0, 512, 0), (512, 512, 0), (1024, 512, 0), (1536, 512, 0), (2048, 256, 1)]


def build_mod(P, T):
    nc = T
    NV = 2
    NCH = 12
    cT = din(nc, "cT", [128, KD, NV])
    mw = din(nc, "mod_w", [DEPTH, 128, KD, NCH * 128])
    mb = din(nc, "mod_b", [DEPTH, 128, NCH])
    out = dout(nc, "modloc", [128, DEPTH * NCH * NV])
    P.phase()
    c_sb = P.sbuf("c_sb", [128, KD, NV], F32)
    s_sb = P.sbuf("s_sb", [128, KD, NV], F32)
    s_bf = P.sbuf("s_bf", [128, KD, NV], BF16)
    mb_sb = P.sbuf("mb_sb", [128, DEPTH, NCH], F32)
    res = P.sbuf("res", [128, DEPTH, NCH, NV], F32)
    wch = [P.sbuf(f"wch{i}", [128, KD, NCH * 128], BF16) for i in range(2)]
    bw = P.bufs(2, "wch")
    ps = [P.psum(f"ps{i}", [128, 512], F32) for i in range(2)]
    bps = P.bufs(2, "ps")
    bc, bs, bmb, bres = P.buf("c"), P.buf("s"), P.buf("mb"), P.buf("res")
    P.dma("sp", c_sb[:], cT, (), [bc])
    for l in range(DEPTH):
        P.dma("sp", mb_sb[:, l, :], mb[l], (), [bmb])
    P.act(s_sb[:], c_sb[:], AF.Sigmoid, [bc], [bs])
    P.tt("dve", s_sb[:], s_sb[:], c_sb[:], ALU.mult, [bc, bs], [bs])
    P.copy("dve", s_bf[:], s_sb[:], [bs], [bs])
    for l in range(DEPTH):
        w = wch[l % 2]
        for k in range(0, KD, 2):
            P.dma("pool", w[:, k:k + 2, :], mw[l, :, k:k + 2, :], (), [bw[l % 2]])
        pt = ps[l % 2]
        for oc in range(NCH):
            for k in range(KD):
                P.mm(pt[:, oc * NV:(oc + 1) * NV], w[:, k, oc * 128:(oc + 1) * 128], s_bf[:, k, :],
                     k == 0, k == KD - 1, [bw[l % 2], bs], [bps[l % 2]])
        P.tt("dve", res[:, l, :, :], pt[:, 0:NCH * NV].rearrange("p (a b) -> p a b", b=NV),
             view(mb_sb[:, l, :], [list(mb_sb[:].ap[0]), [1, NCH], [0, NV]]),
             ALU.add, [bps[l % 2], bmb], [bres])
    P.dma("sp", out, res[:].rearrange("p a b c -> p (a b c)"), [bres], [P.buf("out")])
    return None


def build_pre_gqa(P, T, kind="gqa"):
    nc = T
    diff = kind == "diff"
    NH = 16
    NKV = 16 if diff else 4
    NKC = 8 if diff else 4
    NQK = NH + NKV
    WQK = NQK * 64
    NV_, DV, DVP = (8, 128, 128) if diff else (4, 64, 65)
    WTOT = WQK + NV_ * DV
    NB = WTOT // 512
    xT = din(nc, "xT", [128, KD, TT])
    modT = din(nc, "modT", [128, 48, 2])
    ng = din(nc, "ng", [128, KD])
    wqkv = din(nc, "wqkv", [128, KD, WTOT])
    gvec = din(nc, "gvec", [128, NQK * 64])
    cs = din(nc, "cs", [128, 16, 2, 32])
    ident_d = din(nc, "ident", [128, 128], BF16)
    qT_o = dout(nc, "qT", [128, KD, TT], BF16)
    kT_o = dout(nc, "kT", [128, NKC, TT], BF16)
    vP_tiles = T["vP_tiles"]
    with contextlib.nullcontext():
        P.phase()
        C = NormCtx(P)
        M = load_mod(P, modT, ng, 0)
        w_sb = P.sbuf("w_sb", [128, KD, WTOT], BF16)
        g_sb = P.sbuf("g_sb", [128, NQK * 64], F32)
        cs_sb = P.sbuf("cs_sb", [128, 16, 2, 32], F32)
        ident = P.sbuf("ident_sb", [128, 128], BF16)
        qst = [P.sbuf(f"qst{i}", [128, KD, 128], BF16) for i in range(2)]
        kst = [P.sbuf(f"kst{i}", [128, NKC, 128], BF16) for i in range(2)]
        vst = [P.sbuf(f"vst{i}", [128, NV_, DVP], BF16) for i in range(2)]
        xg = [P.sbuf("xg0", [128, KD, 512], F32)] * 2 if diff else [P.sbuf(f"xg{i}", [128, KD, 512], F32) for i in range(2)]
        hg = P.sbuf("hg", [128, KD, 512], BF16)
        sq = P.sbuf("sq", [128, NQK * 64], F32)
        ss = P.sbuf("ss", [128, NQK], F32)
        qk = P.sbuf("qk", [128, NQK * 64], F32)
        r1 = P.sbuf("r1", [128, NQK * 32], F32)
        r2 = P.sbuf("r2", [128, NQK * 32], F32)
        r3 = P.sbuf("r3", [128, NQK * 32], F32)
        r4 = P.sbuf("r4", [128, NQK * 32], F32)
        qkb = P.sbuf("qkb", [128, NQK * 64], BF16)
        kdup = P.sbuf("kdup", [128, 4, 2, 64], BF16)
        ps_qkv = P.psum("ps_qkv", [128, WTOT], F32)
        ps_qT = P.psum("ps_qT", [128, 1024], BF16)
        ps_kT = ps_qT if diff else P.psum("ps_kT", [128, 1024], BF16)
        bw, bg, bcs, bid = P.buf("w"), P.buf("g"), P.buf("cs"), P.buf("id")
        bqst, bkst, bvst = P.bufs(2, "qst"), P.bufs(2, "kst"), P.bufs(2, "vst")
        outs = []
        ti = 0
        bxg = [P.buf("xg")] * 2 if diff else P.bufs(2, "xg")
        bhg, bsq, bss, bqk, bqkb, bkd = P.buf("hg"), P.buf("sq"), P.buf("ss"), P.buf("qk"), P.buf("qkb"), P.buf("kd")
        br = P.bufs(4, "r")
        bpq, bpqT = P.buf("psqkv"), P.buf("psqT")
        bpkT = bpqT if diff else P.buf("pskT")
        for k in range(KD):
            P.dma("pool", w_sb[:, k, :], wqkv[:, k, :], (), [bw])
        P.dma("sp", g_sb[:], gvec, (), [bg])
        P.dma("sp", cs_sb[:], cs, (), [bcs])
        P.dma("sp", ident[:], ident_d, (), [bid])
        if not diff:
            for i2 in range(2):
                P.memset("pool", vst[i2][:, :, 64:65], 1.0, [bvst[i2]])
        pst = list(qk[:].ap[0])
        for gi, (c0, n, isctx) in enumerate(GROUPS):
            x_t = xg[gi % 2]
            bx = bxg[gi % 2]
            P.dma("sp", x_t[:, :, 0:n], xT[:, :, c0:c0 + n], (), [bx])
            norm_mod(P, C, x_t[:, :, 0:n], n, M["a"][isctx], M["b"][isctx], M["buf"], hg[:, :, 0:n], bx, bhg)
            for tl in range(n // 128):
                t = c0 // 128 + tl
                tc = slice(c0 + tl * 128, c0 + tl * 128 + 128)
                q_s, k_s, v_s = qst[ti % 2], kst[ti % 2], vst[ti % 2]
                bqT, bkT, bvP = bqst[ti % 2], bkst[ti % 2], bvst[ti % 2]
                ti += 1
                for nb in range(NB):
                    for k in range(KD):
                        P.mm(ps_qkv[:, nb * 512:(nb + 1) * 512], hg[:, k, tl * 128:(tl + 1) * 128],
                             w_sb[:, k, nb * 512:(nb + 1) * 512], k == 0, k == KD - 1, [bhg, bw], [bpq])
                for a0 in range(0, WQK, 512):
                    a1 = min(a0 + 512, WQK)
                    P.act(sq[:, a0:a1], ps_qkv[:, a0:a1], AF.Square, [bpq], [bsq])
                P.reduce(ss[:], sq[:].rearrange("p (h d) -> p h d", d=64), ALU.add, [bsq], [bss])
                P.act(ss[:], ss[:], AF.Sqrt, [bss], [bss], bias=EPS, scale=1.0 / 64.0)
                P.recip(ss[:], ss[:], [bss], [bss])
                for h0 in range(0, NQK, 8):
                    h1 = min(h0 + 8, NQK)
                    P.tt("dve", qk[:, h0 * 64:h1 * 64].rearrange("p (h d) -> p h d", d=64),
                         ps_qkv[:, h0 * 64:h1 * 64].rearrange("p (h d) -> p h d", d=64),
                         view(ss[:, h0:h1], [list(ss[:].ap[0]), [1, h1 - h0], [0, 64]]),
                         ALU.mult, [bpq, bss], [bqk])
                for v0 in range(0, NV_ * DV, 512):
                    v1 = min(v0 + 512, NV_ * DV)
                    P.act(v_s[:, v0 // DV:v1 // DV, 0:DV],
                          ps_qkv[:, WQK + v0:WQK + v1].rearrange("p (h d) -> p h d", d=DV), AF.Copy, [bpq], [bvP])
                if isctx:
                    P.tt("dve", qkb[:], qk[:], g_sb[:], ALU.mult, [bqk, bg], [bqkb])
                else:
                    P.tt("dve", qk[:], qk[:], g_sb[:], ALU.mult, [bqk, bg], [bqk])
                    ev = view(qk[:], [pst, [64, NQK], [2, 32]])
                    od = view(qk[:, 1:2], [pst, [64, NQK], [2, 32]])
                    cosv = view(cs_sb[:, t, 0, :], [list(cs_sb[:].ap[0]), [0, NQK], [1, 32]])
                    sinv = view(cs_sb[:, t, 1, :], [list(cs_sb[:].ap[0]), [0, NQK], [1, 32]])
                    rv = [x[:].rearrange("p (h d) -> p h d", d=32) for x in (r1, r2, r3, r4)]
                    P.tt("dve", rv[0], ev, cosv, ALU.mult, [bqk, bcs], [br[0]])
                    P.tt("dve", rv[1], od, sinv, ALU.mult, [bqk, bcs], [br[1]])
                    P.tt("dve", rv[2], ev, sinv, ALU.mult, [bqk, bcs], [br[2]])
                    P.tt("dve", rv[3], od, cosv, ALU.mult, [bqk, bcs], [br[3]])
                    pb = list(qkb[:].ap[0])
                    evo = view(qkb[:], [pb, [64, NQK], [2, 32]])
                    odo = view(qkb[:, 1:2], [pb, [64, NQK], [2, 32]])
                    P.tt("dve", evo, rv[0], rv[1], ALU.subtract, [br[0], br[1]], [bqkb])
                    P.tt("dve", odo, rv[2], rv[3], ALU.add, [br[2], br[3]], [bqkb])
                if not diff:
                    kv_ = qkb[:, 1024:1280].rearrange("p (h d) -> p h d", d=64)
                    P.copy("act", kdup[:, :, 0, :], kv_, [bqkb], [bkd])
                    P.copy("act", kdup[:, :, 1, :], kv_, [bqkb], [bkd])
                for j in range(8):
                    P.tr(ps_qT[:, j * 128:(j + 1) * 128], qkb[:, j * 128:(j + 1) * 128], ident[:], [bqkb, bid], [bpqT])
                P.copy("dve", q_s[:], ps_qT[:].rearrange("p (j t) -> p j t", t=128), [bpqT], [bqT])
                for g in range(NKC):
                    src = qkb[:, 1024 + g * 128:1024 + (g + 1) * 128] if diff else \
                        kdup[:, g, :, :].rearrange("p a d -> p (a d)")
                    P.tr(ps_kT[:, g * 128:(g + 1) * 128], src, ident[:], [bqkb if diff else bkd, bid], [bpkT])
                P.copy("act", k_s[:], ps_kT[:, 0:NKC * 128].rearrange("p (j t) -> p j t", t=128),
                       [bpkT], [bkT])
                bo = P.buf("out")
                P.dma("sp", qT_o[:, :, tc], q_s[:], [bqT], [bo])
                P.dma("sp", kT_o[:, :, tc], k_s[:], [bkT], [bo])
                P.dma("sp", vP_tiles[t], v_s[:], [bvP], [bo])
                outs.append(bo)
        P.wait_all("sp", outs)
    return None


def build_att(P, T, mode, need_ctx):
    nc = T
    NH, NKV = 16, 4
    NKT = 66 if mode == "dense" else 20
    xT = din(nc, "xT", [128, KD, TT])
    modT = din(nc, "modT", [128, 48, 2])
    ng = din(nc, "ng", [128, KD])
    qT = din(nc, "qT", [128, KD, TT], BF16)
    kloc = din(nc, "kloc", [NKV, 128, TT], BF16).rearrange("g p t -> p g t")
    vloc5 = din(nc, "vloc", [2, 128, 9, NKV, 65], BF16)
    kall = din(nc, "kall", [NKV, 4 * 128, TT], BF16)
    vall5 = din(nc, "vall", [2, 4 * 128, 9, NKV, 65], BF16)
    wo = din(nc, "wo", [64, NH, D])
    if mode == "win":
        masks_d = din(nc, "masks", [128, 6, 512], BF16)
        sink_d = din(nc, "sink", [1, NH])
        selv = din(nc, "selv", [128, 2, 4])
    xo = dout(nc, "xo", [128, KD, TT])
    with contextlib.nullcontext():
        P.phase()
        M = load_mod(P, modT, ng, 0)
        kT_sb = P.sbuf("kT_sb", [128, NKV, NKT * 128], BF16)
        vP_sb = P.sbuf("vP_sb", [128, NKT, NKV, 65], BF16)
        wo_sb = P.sbuf("wo_sb", [64, NH, D], BF16)
        qg = [P.sbuf(f"qg{i}", [128, KD, 512], BF16) for i in range(2)]
        xg = [P.sbuf("xg0", [128, KD, 512], F32)] * 2
        ao = P.sbuf("ao", [64, NH, 512], BF16)
        NP_ = 2 * PIPE + 4
        pt = [P.sbuf(f"pt{i}", [128, 512], BF16) for i in range(NP_)]
        o_sb = P.sbuf("o_sb", [65, 512], F32)
        onesf = P.sbuf("onesf", [65, 64], F32)
        NS = 4
        ps_s = [P.psum(f"ps_s{i}", [128, 512], F32) for i in range(NS)]
        ps_o = [P.psum(f"ps_o{i}", [128, 512], F32) for i in range(2)]
        ps_rb = P.psum("ps_rb", [128, 512], F32)
        ps_y = [P.psum("ps_y0", [128, 512], F32)] * 2
        bk, bv, bwo, bones = P.buf("k"), P.buf("v"), P.buf("wo"), P.buf("ones")
        bqg = P.bufs(2, "qg")
        bxg = [P.buf("xg")] * 2
        bao, bosb = P.buf("ao"), P.buf("osb")
        bpt = P.bufs(NP_, "pt")
        bps, bpo, bprb, bpy = P.bufs(NS, "pss"), P.bufs(2, "pso"), P.buf("psrb"), [P.buf("psy")] * 2
        kall4 = kall.rearrange("g (r p) t -> r p g t", p=128)

        def vcopy(dst0, r, t0, t1):
            for hf in range(2):
                lo, hi = max(t0, 9 * hf), min(t1, 9 * hf + 9)
                if lo < hi:
                    src = vloc5[hf] if r is None else vall5[hf, r * 128:(r + 1) * 128]
                    P.dma("sp", vP_sb[:, dst0 + lo - t0:dst0 + hi - t0], src[:, lo - 9 * hf:hi - 9 * hf], (), [bv])

        if mode == "dense":
            P.dma("sp", kT_sb[:, :, 0:TC], kloc[:, :, TL:TT], (), [bk])
            vcopy(0, None, 16, 18)
            for r in range(4):
                for g in range(NKV):
                    P.dma("sp", kT_sb[:, g, TC + r * TL:TC + (r + 1) * TL], kall4[r, :, g, 0:TL], (), [bk])
                vcopy(2 + 16 * r, r, 0, 16)
        else:
            sv = P.sbuf("sv", [128, 2, 4], F32)
            kc_ = P.sbuf("kc_", [128, 2, 4, NKV, 128], BF16)
            vc_ = P.sbuf("vc_", [128, 2, 4, NKV, 65], BF16)
            bsv, bkc = P.buf("sv"), P.buf("kc")
            P.dma("sp", sv[:], selv, (), [bsv])
            P.dma("sp", kT_sb[:, :, 0:TC], kloc[:, :, TL:TT], (), [bk])
            P.dma("sp", kT_sb[:, :, 3 * 128:3 * 128 + TL], kloc[:, :, 0:TL], (), [bk])
            vcopy(0, None, 16, 18)
            vcopy(3, None, 0, 16)
            for r in range(4):
                P.dma("sp", kc_[:, 0, r], kall4[r, :, :, TL - 128:TL], (), [bkc])
                P.dma("sp", kc_[:, 1, r], kall4[r, :, :, 0:128], (), [bkc])
                P.dma("sp", vc_[:, 0, r], vall5[1, r * 128:(r + 1) * 128, 6], (), [bkc])
                P.dma("sp", vc_[:, 1, r], vall5[0, r * 128:(r + 1) * 128, 0], (), [bkc])
            for side, (kslot, vslot) in enumerate(((2 * 128, 2), (19 * 128, 19))):
                kd_ = kT_sb[:, :, kslot:kslot + 128]
                vd_ = vP_sb[:, vslot]
                P.ts("dve", kd_, kc_[:, side, 0], sv[:, side, 0:1], ALU.mult, [bkc, bsv], [bk])
                P.ts("dve", vd_, vc_[:, side, 0], sv[:, side, 0:1], ALU.mult, [bkc, bsv], [bv])
                for r in range(1, 4):
                    P.stt("dve", kd_, kc_[:, side, r], sv[:, side, r:r + 1], kd_, ALU.mult, ALU.add, [bkc, bsv, bk], [bk])
                    P.stt("dve", vd_, vc_[:, side, r], sv[:, side, r:r + 1], vd_, ALU.mult, ALU.add, [bkc, bsv, bv], [bv])
        for h in range(0, NH, 4):
            P.dma("pool", wo_sb[:, h:h + 4, :], wo[:, h:h + 4, :], (), [bwo])
        P.memset("dve", onesf[:], 1.0, [bones])
        if mode == "win":
            mk = P.sbuf("mk", [128, 6, 512], BF16)
            snk = P.sbuf("snk", [65, NH], F32)
            bmk, bsn = P.buf("mk"), P.buf("snk")
            P.dma("sp", mk[:], masks_d, (), [bmk])
            P.dma("sp", snk[64:65, :], sink_d, (), [bsn])
            P.act(snk[64:65, :], snk[64:65, :], AF.Exp, [bsn], [bsn])
        groups = GROUPS if need_ctx else GROUPS[:4]
        si = 0
        oi = 0
        yi = 0
        outs = []
        P.dma("sp", qg[0][:, :, 0:groups[0][1]], qT[:, :, 0:groups[0][1]], (), [bqg[0]])
        for gi, (c0, n, isctx) in enumerate(groups):
            q_t, x_t = qg[gi % 2], xg[gi % 2]
            bq, bx = bqg[gi % 2], bxg[gi % 2]
            if gi + 1 < len(groups):
                c1, n1, _ = groups[gi + 1]
                P.dma("sp", qg[(gi + 1) % 2][:, :, 0:n1], qT[:, :, c1:c1 + n1], (), [bqg[(gi + 1) % 2]])
            P.dma("sp", x_t[:, :, 0:n], xT[:, :, c0:c0 + n], (), [bx])
            if isctx:
                ktl = [(0, None), (1, None)]
            elif mode == "dense":
                ktl = [(kt, None) for kt in range(NKT)]
            else:
                ktl = [(0, None), (1, None)] + [(2 + 4 * gi + r, r) for r in range(6)]
            units = [(2 * j + hp, ki, kt, mr) for j in range(NH // 2) for ki, (kt, mr) in enumerate(ktl)
                     for hp in range(2)]
            pend = []

            def stage1(u):
                h, ki, kt, mr = u
                j, hp, g = h // 2, h % 2, h // 4
                i_ = stage1.si
                stage1.si += 1
                s_ps, bs_ = ps_s[i_ % NS], bps[i_ % NS]
                p_t, bp_ = pt[i_ % NP_], bpt[i_ % NP_]
                P.mm(s_ps[:, 0:n], kT_sb[hp * 64:(hp + 1) * 64, g, kt * 128:(kt + 1) * 128],
                     q_t[hp * 64:(hp + 1) * 64, j, 0:n], True, True, [bk, bq], [bs_])
                P.act(p_t[:, 0:n], s_ps[:, 0:n], AF.Exp, [bs_], [bp_], scale=0.125)
                if mr is not None:
                    P.tt("dve", p_t[:, 0:n], p_t[:, 0:n], mk[:, mr, 0:n], ALU.mult, [bp_, bmk], [bp_])
                return p_t, bp_

            stage1.si = si

            def stage2(u, p_t, bp_):
                h, ki, kt, mr = u
                g = h // 4
                po, bo_ = ps_o[h % 2], bpo[h % 2]
                P.mm(po[0:65, 0:n], vP_sb[:, kt, g, :], p_t[:, 0:n], ki == 0, ki == len(ktl) - 1, [bv, bp_], [bo_])
                if ki != len(ktl) - 1:
                    return
                P.copy("act", o_sb[:, 0:n], po[0:65, 0:n], [bo_], [bosb])
                if mode == "win" and not isctx:
                    P.ts("dve", o_sb[64:65, 0:n], o_sb[64:65, 0:n], snk[64:65, h:h + 1], ALU.add, [bosb, bsn], [bosb])
                P.recip(o_sb[64:65, 0:n], o_sb[64:65, 0:n], [bosb], [bosb])
                P.mm(ps_rb[0:64, 0:n], onesf[64:65, :], o_sb[64:65, 0:n], True, True, [bones, bosb], [bprb])
                P.tt("dve", ao[:, h, 0:n], o_sb[0:64, 0:n], ps_rb[0:64, 0:n], ALU.mult, [bosb, bprb], [bao])

            PP = 2 * PIPE
            for idx in range(0, len(units) + PP, 2):
                for i2 in (idx, idx + 1):
                    if i2 < len(units):
                        pend.append(stage1(units[i2]))
                for i2 in (idx - PP, idx - PP + 1):
                    if 0 <= i2 < len(units):
                        stage2(units[i2], *pend[i2])
            si = stage1.si
            for oc in range(KD):
                y_ps, by_ = ps_y[yi % 2], bpy[yi % 2]
                yi += 1
                for h in range(NH):
                    P.mm(y_ps[:, 0:n], wo_sb[:, h, oc * 128:(oc + 1) * 128], ao[:, h, 0:n], h == 0, h == NH - 1,
                         [bwo, bao], [by_])
                P.stt("dve", x_t[:, oc, 0:n], y_ps[:, 0:n], M["g"][isctx][:, oc:oc + 1], x_t[:, oc, 0:n],
                      ALU.mult, ALU.add, [by_, M["buf"], bx], [bx])
            bo = P.buf("out")
            P.dma("sp", xo[:, :, c0:c0 + n], x_t[:, :, 0:n], [bx], [bo])
            outs.append(bo)
        if not need_ctx:
            x_t, bx = xg[0], bxg[0]
            bo = P.buf("outc")
            P.dma("sp", x_t[:, :, 0:TC], xT[:, :, TL:TT], (), [bx])
            P.dma("sp", xo[:, :, TL:TT], x_t[:, :, 0:TC], [bx], [bo])
            outs.append(bo)
        P.wait_all("sp", outs)
    return None


def build_mlp(P, T, need_ctx):
    nc = T
    xT = din(nc, "xT", [128, KD, TT])
    modT = din(nc, "modT", [128, 48, 2])
    ng = din(nc, "ng", [128, KD])
    wu = din(nc, "wu", [128, KD, DFF])
    wd = din(nc, "wd", [128, 32, D])
    xo = dout(nc, "xo", [128, KD, TT])
    with contextlib.nullcontext():
        P.phase()
        C = NormCtx(P)
        M = load_mod(P, modT, ng, 1)
        wu_sb = P.sbuf("wu_sb", [128, KD, DFF], BF16)
        wd_sb = P.sbuf("wd_sb", [128, 32, D], BF16)
        N = 256
        xg = [P.sbuf(f"xg{i}", [128, KD, N], F32) for i in range(2)]
        hg = P.sbuf("hg", [128, KD, N], BF16)
        aT = P.sbuf("aT", [128, 32, N], BF16)
        rl = [P.sbuf(f"rl{i}", [128, N], F32) for i in range(3)]
        ps_u = [P.psum(f"ps_u{i}", [128, 512], F32) for i in range(3)]
        ps_d = [P.psum(f"ps_d{i}", [128, 512], F32) for i in range(2)]
        bwu, bwd = P.bufs(KD, "wu"), P.bufs(8, "wd")
        bxg, bhg, baT = P.bufs(2, "xg"), P.buf("hg"), P.bufs(32, "aT")
        brl, bpu, bpd = P.bufs(3, "rl"), P.bufs(3, "psu"), P.bufs(2, "psd")
        for c in range(8):
            P.dma("pool", wu_sb[:, :, c * 512:(c + 1) * 512], wu[:, :, c * 512:(c + 1) * 512], (), [bwu[c]])
        for c in range(8):
            P.dma("pool", wd_sb[:, c * 4:(c + 1) * 4, :], wd[:, c * 4:(c + 1) * 4, :], (), [bwd[c]])
        ncol = TT if need_ctx else TL
        ui = 0
        di = 0
        outs = []
        cols = list(range(0, ncol, N))

        def load_norm(gi):
            c0_ = cols[gi]
            ic = 1 if c0_ >= TL else 0
            P.dma("sp", xg[gi % 2][:], xT[:, :, c0_:c0_ + N], (), [bxg[gi % 2]])
            norm_mod(P, C, xg[gi % 2][:], N, M["a"][ic], M["b"][ic], M["buf"], hg[:], bxg[gi % 2], bhg)

        load_norm(0)
        for gi, c0 in enumerate(cols):
            isctx = 1 if c0 >= TL else 0
            x_t, bx = xg[gi % 2], bxg[gi % 2]
            for fc in range(32):
                u_ps, bu_ = ps_u[ui % 3], bpu[ui % 3]
                r_t, br_ = rl[ui % 3], brl[ui % 3]
                ui += 1
                for k in range(KD):
                    P.mm(u_ps[:, 0:N], wu_sb[:, k, fc * 128:(fc + 1) * 128], hg[:, k, :], k == 0, k == KD - 1,
                         [bwu[fc // 4], bhg], [bu_])
                P.act(r_t[:], u_ps[:, 0:N], AF.Relu, [bu_], [br_])
                P.tt("dve" if fc % 2 == 0 else "pool", aT[:, fc, :], r_t[:], r_t[:], ALU.mult, [br_], [baT[fc]])
            if gi + 1 < len(cols):
                load_norm(gi + 1)
            for oc in range(KD):
                d_ps, bd_ = ps_d[di % 2], bpd[di % 2]
                di += 1
                for fc in range(32):
                    P.mm(d_ps[:, 0:N], wd_sb[:, fc, oc * 128:(oc + 1) * 128], aT[:, fc, :], fc == 0, fc == 31,
                         [bwd[fc // 4], baT[fc]], [bd_])
                P.stt("dve", x_t[:, oc, :], d_ps[:, 0:N], M["g"][isctx][:, oc:oc + 1], x_t[:, oc, :],
                      ALU.mult, ALU.add, [bd_, M["buf"], bx], [bx])
            bo = P.buf("out")
            P.dma("sp", xo[:, :, c0:c0 + N], x_t[:], [bx], [bo])
            outs.append(bo)
        if not need_ctx:
            x_t, bx = xg[0], bxg[0]
            bo = P.buf("outc")
            P.dma("sp", x_t[:], xT[:, :, TL:TT], (), [bx])
            P.dma("sp", xo[:, :, TL:TT], x_t[:], [bx], [bo])
            outs.append(bo)
        P.wait_all("sp", outs)
    return None


_PROGS = {}


def prog(name, builder, *args):
    key = (name,) + args
    if key not in _PROGS:
        _PROGS[key] = builder(*args)
    return _PROGS[key]


def run(nc, in_maps):
    res = run_bass_kernel_spmd(nc, in_maps, core_ids=list(range(NCORES)))
    return res.results


def fm(a):
    T, F = a.shape
    return np.ascontiguousarray(a.T.reshape(F // 128, 128, T).transpose(1, 0, 2))


def fm_inv(a):
    p, k, t = a.shape
    return np.ascontiguousarray(a.transpose(1, 0, 2).reshape(k * p, t).T)


def wl(w):
    fin, o = w.shape
    return np.ascontiguousarray(w.reshape(fin // 128, 128, o).transpose(1, 0, 2))


def colv(v):
    return np.ascontiguousarray(v.reshape(-1, 128).T)


def rope_tables():
    rows = SEQ // 64
    row = np.repeat(np.arange(rows, dtype=np.float32), 64)
    col = np.tile(np.arange(64, dtype=np.float32), rows)
    half = 32
    inv = (1.0 / np.power(np.float32(10000.0), np.arange(0, half, 2, dtype=np.float32) / half)).astype(np.float32)
    ang = np.concatenate([row[:, None] * inv, col[:, None] * inv], axis=-1).astype(np.float32)
    return np.cos(ang).astype(np.float32), np.sin(ang).astype(np.float32)


def cs_for_core(cos, sin, q):
    c = cos[q * TL:(q + 1) * TL].reshape(16, 128, 32)
    s = sin[q * TL:(q + 1) * TL].reshape(16, 128, 32)
    return np.ascontiguousarray(np.stack([c, s], axis=2).transpose(1, 0, 2, 3))


def run_mod(c, c_ctx, mod_w, mod_b):
    nc = prog("mod", build_mod)
    cT = np.ascontiguousarray(np.stack([colv(c[0]), colv(c[1]), colv(c_ctx)], axis=-1))
    maps = []
    for core in range(NCORES):
        cols = slice(core * 768, (core + 1) * 768)
        mw = np.ascontiguousarray(mod_w[:, :, cols].reshape(DEPTH, KD, 128, 768).transpose(0, 2, 1, 3))
        mb = np.ascontiguousarray(mod_b[:, cols].reshape(DEPTH, 6, 128).transpose(0, 2, 1))
        maps.append({"cT": cT, "mod_w": mw, "mod_b": mb})
    res = run(nc, maps)
    full = np.concatenate([r["modT"] for r in res], axis=2)
    return [np.ascontiguousarray(full[:, :, :, [b, 2]]) for b in range(2)]


def gather_kv_dense(kTs, vPs):
    out = []
    for b in range(2):
        cores = [b * 4 + q for q in range(4)]
        kT = np.concatenate([kTs[cores[0]][:, :, TL:TT]] + [kTs[c][:, :, 0:TL] for c in cores], axis=2)
        vP = np.concatenate([vPs[cores[0]][:, 16:18]] + [vPs[c][:, 0:16] for c in cores], axis=1)
        out.append((np.ascontiguousarray(kT), np.ascontiguousarray(vP)))
    return out


def layer_gqa_dense(i, xs, mods, inp, need_ctx, cos, sin):
    ident = np.eye(128, dtype=np.float32).astype(NPBF)
    wqkv = wl(inp["gqa_w_qkv"][0])
    gvec = np.concatenate([np.tile(inp["gqa_q_g"][0], 16), np.tile(inp["gqa_k_g"][0], 4)]).astype(np.float32)
    gvec = np.ascontiguousarray(np.broadcast_to(gvec[None, :], (128, gvec.size)))
    ng1 = colv(inp["norm1_g"][i])
    nc = prog("pre_gqa", build_pre_gqa, "gqa")
    maps = [{"xT": xs[c], "modT": mods[c // 4][i], "ng": ng1, "wqkv": wqkv, "gvec": gvec,
             "cs": cs_for_core(cos, sin, c % 4), "ident": ident} for c in range(NCORES)]
    res = run(nc, maps)
    kv = gather_kv_dense([r["kT"] for r in res], [r["vP"] for r in res])
    wo = np.ascontiguousarray(inp["gqa_w_o"][0].reshape(16, 64, D).transpose(1, 0, 2))
    nc = prog("att", build_att, "dense", need_ctx)
    maps = [{"xT": xs[c], "modT": mods[c // 4][i], "ng": ng1, "qT": res[c]["qT"], "kT": kv[c // 4][0],
             "vP": kv[c // 4][1], "wo": wo} for c in range(NCORES)]
    res2 = run(nc, maps)
    return [r["xo"] for r in res2]


def layer_mlp(i, xs, mods, inp, need_ctx):
    nc = prog("mlp", build_mlp, need_ctx)
    wu = wl(inp["mlp_up"][i])
    wd = wl(inp["mlp_down"][i])
    ng2 = colv(inp["norm2_g"][i])
    maps = [{"xT": xs[c], "modT": mods[c // 4][i], "ng": ng2, "wu": wu, "wd": wd} for c in range(NCORES)]
    res = run(nc, maps)
    return [r["xo"] for r in res]


def shard_x(x, ctx):
    xs = []
    for c in range(NCORES):
        b, q = c // 4, c % 4
        xs.append(fm(np.concatenate([x[b, q * TL:(q + 1) * TL], ctx[b]], axis=0)))
    return xs


def unshard_x(xs):
    out = np.empty((2, SEQ, D), np.float32)
    for c in range(NCORES):
        b, q = c // 4, c % 4
        out[b, q * TL:(q + 1) * TL] = fm_inv(xs[c])[0:TL]
    return out


def build_att_diff(P, T, lam_init):
    nc = T
    NHD, NKT = 8, 66
    xT = din(nc, "xT", [128, KD, TT])
    modT = din(nc, "modT", [128, 48, 2])
    ng = din(nc, "ng", [128, KD])
    qT = din(nc, "qT", [128, KD, TT], BF16)
    kloc = din(nc, "kloc", [NHD, 128, TT], BF16).rearrange("g p t -> p g t")
    vloc = din(nc, "vloc", [NHD, 128, 18, 128], BF16).rearrange("h p t d -> p h t d")
    kall = din(nc, "kall", [NHD, 4 * 128, TT], BF16)
    vall = din(nc, "vall", [NHD, 4 * 128, 18, 128], BF16)
    kall4 = kall.rearrange("g (r p) t -> r p g t", p=128)
    vall4 = vall.rearrange("h (r p) t d -> r p h t d", p=128)
    wo = din(nc, "wo", [128, NHD, D])
    lamv = din(nc, "lamv", [1, 4, 64])
    slg = din(nc, "slg", [128, 1])
    xo = dout(nc, "xo", [128, KD, TT])
    with contextlib.nullcontext():
        P.phase()
        M = load_mod(P, modT, ng, 0)
        qT_sb = P.sbuf("qT_sb", [128, KD, TT], BF16)
        aoT = P.sbuf("aoT", [128, NHD, TT], BF16)
        kh = [P.sbuf(f"kh{i}", [128, NKT * 128], BF16) for i in range(2)]
        vh = [P.sbuf(f"vh{i}", [128, NKT, 128], BF16) for i in range(2)]
        wo_sb = P.sbuf("wo_sb", [128, NHD, D], BF16)
        xg = P.sbuf("xg", [128, KD, 512], F32)
        NP_ = 2 * PIPE + 4
        pt = [P.sbuf(f"pt{i}", [128, 512], BF16) for i in range(NP_)]
        o_sb = [P.sbuf(f"o_sb{i}", [128, 512], F32) for i in range(2)]
        l_sb = P.sbuf("l_sb", [2, 512], F32)
        accl = [P.sbuf(f"accl{i}", [128, 512], F32) for i in range(2)]
        accb = [P.sbuf(f"accb{i}", [128, 512], BF16) for i in range(2)]
        baccl, baccb = P.bufs(2, "accl"), P.bufs(2, "accb")
        od = P.sbuf("od", [128, 512], F32)
        sqb = P.sbuf("sqb", [128, 512], BF16)
        rs = P.sbuf("rs", [128, 512], F32)
        onesb = P.sbuf("onesb", [128, 128], BF16)
        sel = P.sbuf("sel", [128, 2, 2], BF16)
        self_f = P.sbuf("self_f", [2, 2, 128], F32)
        lam_sb = P.sbuf("lam_sb", [1, 4, 64], F32)
        lam_t = P.sbuf("lam_t", [1, 8], F32)
        sg = P.sbuf("sg", [128, 1], F32)
        ps_s = [P.psum(f"ps_s{i}", [128, 512], F32) for i in range(4)]
        ps_o = [P.psum(f"ps_o{i}", [128, 512], F32) for i in range(2)]
        ps_l = P.psum("ps_l", [128, 512], F32)
        ps_rb = [P.psum("ps_rb0", [128, 512], F32)] * 2
        ps_y = ps_rb[0]
        bq, bao, bwo, bx = P.buf("q"), P.buf("ao"), P.buf("wo"), P.buf("x")
        bkh, bvh = P.bufs(2, "kh"), P.bufs(2, "vh")
        bpt, bosb = P.bufs(NP_, "pt"), P.bufs(2, "osb")
        blsb, bod, bsqb, brs, bcst, blam = P.buf("lsb"), P.buf("od"), P.buf("sqb"), P.buf("rs"), P.buf("cst"), P.buf("lam")
        bps, bpo, bpl = P.bufs(4, "pss"), P.bufs(2, "pso"), P.buf("psl")
        bprb = [P.buf("psrb")] * 2
        bpy = bprb[0]
        for k in range(KD):
            P.dma("sp", qT_sb[:, k, :], qT[:, k, :], (), [bq])
        for h in range(0, NHD, 2):
            P.dma("pool", wo_sb[:, h:h + 2, :], wo[:, h:h + 2, :], (), [bwo])
        P.dma("sp", lam_sb[:], lamv, (), [blam])
        P.dma("sp", sg[:], slg, (), [bcst])
        P.memset("dve", onesb[:], 1.0 / 128.0, [bcst])
        P.memset("dve", sel[:], 0.0, [bcst])
        P.memset("dve", sel[:, 0, 0:1], 1.0, [bcst])
        P.memset("dve", sel[:, 1, 1:2], 1.0, [bcst])
        P.memset("dve", self_f[:], 0.0, [bcst])
        P.memset("dve", self_f[0:1, 0, :], 1.0, [bcst])
        P.memset("dve", self_f[0:2, 1, :], 1.0, [bcst])
        P.memset("dve", self_f[0:1, 1, :], 0.0, [bcst])
        P.ts("dve", sg[:], sg[:], 1.0 - lam_init, ALU.mult, [bcst], [bcst])
        P.tt("dve", lam_sb[:, 0, :], lam_sb[:, 0, :], lam_sb[:, 1, :], ALU.mult, [blam], [blam])
        P.tt("dve", lam_sb[:, 2, :], lam_sb[:, 2, :], lam_sb[:, 3, :], ALU.mult, [blam], [blam])
        P.reduce(lam_t[:, 0:1], lam_sb[:, 0, :], ALU.add, [blam], [blam])
        P.reduce(lam_t[:, 1:2], lam_sb[:, 2, :], ALU.add, [blam], [blam])
        P.act(lam_t[:, 2:4], lam_t[:, 0:2], AF.Exp, [blam], [blam])
        P.tt("dve", lam_t[:, 4:5], lam_t[:, 3:4], lam_t[:, 2:3], ALU.subtract, [blam], [blam])
        P.ts("dve", lam_t[:, 5:6], lam_t[:, 4:5], -lam_init, ALU.add, [blam], [blam])
        nlam = P.sbuf("nlam", [128, 1], F32)
        bnl = P.buf("nlam")
        P.mm(ps_y[:, 0:1], self_f[0:1, 0, :], lam_t[0:1, 5:6], True, True, [bcst, blam], [bpy])
        P.copy("dve", nlam[:], ps_y[:, 0:1], [bpy], [bnl])
        si = 0
        for h in range(NHD):
            k_t, v_t = kh[h % 2], vh[h % 2]
            bk, bv = bkh[h % 2], bvh[h % 2]
            P.dma("sp", k_t[:, 0:TC], kloc[:, h, TL:TT], (), [bk])
            P.dma("sp", v_t[:, 0:2, :], vloc[:, h, 16:18, :], (), [bv])
            for r in range(4):
                P.dma("sp", k_t[:, TC + r * TL:TC + (r + 1) * TL], kall4[r, :, h, 0:TL], (), [bk])
                P.dma("sp", v_t[:, 2 + 16 * r:2 + 16 * (r + 1), :], vall4[r, :, h, 0:16, :], (), [bv])
            for gi, (c0, n, isctx) in enumerate(GROUPS):
                ktl = [0, 1] if isctx else list(range(NKT))
                units = [(c, ki, kt) for ki, kt in enumerate(ktl) for c in range(2)]
                pend = []

                def stage1(u):
                    c, ki, kt = u
                    i_ = stage1.si
                    stage1.si += 1
                    s_ps, bs_ = ps_s[i_ % 4], bps[i_ % 4]
                    p_t, bp_ = pt[i_ % NP_], bpt[i_ % NP_]
                    P.mm(s_ps[:, 0:n], k_t[c * 64:(c + 1) * 64, kt * 128:(kt + 1) * 128],
                         qT_sb[c * 64:(c + 1) * 64, h, c0:c0 + n], True, True, [bk, bq], [bs_])
                    P.act(p_t[:, 0:n], s_ps[:, 0:n], AF.Exp, [bs_], [bp_], scale=0.125)
                    return p_t, bp_

                stage1.si = si

                def stage2(u, p_t, bp_):
                    c, ki, kt = u
                    po, bo_ = ps_o[c], bpo[c]
                    P.mm(po[:, 0:n], v_t[:, kt, :], p_t[:, 0:n], ki == 0, ki == len(ktl) - 1, [bv, bp_], [bo_])
                    if ki % 4 == 0:
                        P.mm(ps_l[0:2, 0:n], sel[:, c, :], p_t[:, 0:n], c == 0 and ki == 0, False, [bcst, bp_], [bpl])
                    elif ki == 1:
                        P.copy("dve", accl[c][:, 0:n], p_t[:, 0:n], [bp_], [baccl[c]])
                    else:
                        P.tt("dve", accl[c][:, 0:n], accl[c][:, 0:n], p_t[:, 0:n], ALU.add, [bp_, baccl[c]], [baccl[c]])
                    if ki == len(ktl) - 1:
                        P.copy("dve", accb[c][:, 0:n], accl[c][:, 0:n], [baccl[c]], [baccb[c]])
                        P.mm(ps_l[0:2, 0:n], sel[:, c, :], accb[c][:, 0:n], False, c == 1, [bcst, baccb[c]], [bpl])
                        P.copy("act", o_sb[c][:, 0:n], po[:, 0:n], [bo_], [bosb[c]])

                PP = 2 * PIPE
                for idx in range(0, len(units) + PP, 2):
                    for i2 in (idx, idx + 1):
                        if i2 < len(units):
                            pend.append(stage1(units[i2]))
                    for i2 in (idx - PP, idx - PP + 1):
                        if 0 <= i2 < len(units):
                            stage2(units[i2], *pend[i2])
                si = stage1.si
                P.copy("act", l_sb[:, 0:n], ps_l[0:2, 0:n], [bpl], [blsb])
                P.recip(l_sb[:, 0:n], l_sb[:, 0:n], [blsb], [blsb])
                for c in range(2):
                    P.mm(ps_rb[c][:, 0:n], self_f[:, c, :], l_sb[:, 0:n], True, True, [bcst, blsb], [bprb[c]])
                    P.tt("dve", o_sb[c][:, 0:n], o_sb[c][:, 0:n], ps_rb[c][:, 0:n], ALU.mult, [bosb[c], bprb[c]],
                         [bosb[c]])
                P.stt("dve", od[:, 0:n], o_sb[1][:, 0:n], nlam[:, 0:1], o_sb[0][:, 0:n], ALU.mult, ALU.add,
                      [bosb[0], bosb[1], bnl], [bod])
                P.act(sqb[:, 0:n], od[:, 0:n], AF.Square, [bod], [bsqb])
                P.mm(ps_rb[0][:, 0:n], onesb[:], sqb[:, 0:n], True, True, [bcst, bsqb], [bprb[0]])
                P.act(rs[:, 0:n], ps_rb[0][:, 0:n], AF.Sqrt, [bprb[0]], [brs], bias=EPS, scale=1.0)
                P.recip(rs[:, 0:n], rs[:, 0:n], [brs], [brs])
                P.stt("dve", aoT[:, h, c0:c0 + n], od[:, 0:n], sg[:, 0:1], rs[:, 0:n], ALU.mult, ALU.mult,
                      [bod, bcst, brs], [bao])
        outs = []
        for gi, (c0, n, isctx) in enumerate(GROUPS):
            P.dma("sp", xg[:, :, 0:n], xT[:, :, c0:c0 + n], (), [bx])
            for oc in range(KD):
                for h in range(NHD):
                    P.mm(ps_y[:, 0:n], wo_sb[:, h, oc * 128:(oc + 1) * 128], aoT[:, h, c0:c0 + n], h == 0,
                         h == NHD - 1, [bwo, bao], [bpy])
                P.stt("dve", xg[:, oc, 0:n], ps_y[:, 0:n], M["g"][isctx][:, oc:oc + 1], xg[:, oc, 0:n],
                      ALU.mult, ALU.add, [bpy, M["buf"], bx], [bx])
            bo = P.buf("out")
            P.dma("sp", xo[:, :, c0:c0 + n], xg[:, :, 0:n], [bx], [bo])
            outs.append(bo)
        P.wait_all("sp", outs)
    return None


HALO = 15
TE = TL + 2 * HALO
TCE = TC + 2 * HALO


def build_conv(P, T):
    nc = T
    xT = din(nc, "xT", [128, KD, TE + TC])
    modT = din(nc, "modT", [128, 48, 2])
    ng = din(nc, "ng", [128, KD])
    w1 = din(nc, "w1", [128, KD, 2 * D])
    b1 = din(nc, "b1", [128, 16])
    wdw = din(nc, "wdw", [128, KD, 31])
    cvec = din(nc, "cvec", [128, 4, KD])
    w2 = din(nc, "w2", [128, KD, D])
    hv = din(nc, "hv", [128, 2])
    xo = dout(nc, "xo", [128, KD, TT])
    with contextlib.nullcontext():
        P.phase()
        C = NormCtx(P)
        M = load_mod(P, modT, ng, 0)
        w1_sb = P.sbuf("w1_sb", [128, KD, 2 * D], BF16)
        w2_sb = P.sbuf("w2_sb", [128, KD, D], BF16)
        b1_sb = P.sbuf("b1_sb", [128, 16], F32)
        wdw_sb = P.sbuf("wdw_sb", [128, KD, 31], F32)
        cv = P.sbuf("cv", [128, 4, KD], F32)
        gb = P.sbuf("gb", [128, 2, KD], F32)
        hv_sb = P.sbuf("hv_sb", [128, 2], F32)
        uT = P.sbuf("uT", [128, KD, TE], F32)
        uc = P.sbuf("uc", [128, KD, TCE], F32)
        xg = P.sbuf("xg", [128, KD, 512], F32)
        hg = P.sbuf("hg", [128, KD, 512], BF16)
        sg_ = [P.sbuf(f"sg{i}", [128, 512], F32) for i in range(2)]
        N2 = 256
        acc = P.sbuf("acc", [128, KD, N2], F32)
        ctmps = {oc: [P.sbuf(f"ctmp{oc}_{i}", [128, N2], F32) for i in range(2)] for oc in range(5, KD)}
        bcts = {oc: P.bufs(2, f"ctmp{oc}") for oc in range(5, KD)}
        jn = P.sbuf("jn", [128, 1], F32)
        baccs = P.bufs(KD, "accs")
        ps_a = [P.psum(f"ps_a{i}", [128, 512], F32) for i in range(2)]
        ps_g = [P.psum(f"ps_g{i}", [128, 512], F32) for i in range(2)]
        ps_m = P.psum("ps_m", [128, 512], F32)
        ps_y = [P.psum(f"ps_y{i}", [128, 512], F32) for i in range(2)]
        bw1, bw2, bcst, bhv = P.buf("w1"), P.buf("w2"), P.buf("cst"), P.buf("hv")
        bu, buc, bx, bhg, bacc = P.buf("u"), P.buf("uc"), P.buf("x"), P.buf("hg"), P.buf("acc")
        bsg, bpa, bpg, bpm, bpy = P.bufs(2, "sg"), P.bufs(2, "psa"), P.bufs(2, "psg"), P.buf("psm"), P.bufs(2, "psy")
        for k in range(KD):
            P.dma("pool", w1_sb[:, k, :], w1[:, k, :], (), [bw1])
        for k in range(0, KD, 2):
            P.dma("pool", w2_sb[:, k:k + 2, :], w2[:, k:k + 2, :], (), [bw2])
        P.dma("sp", b1_sb[:], b1, (), [bcst])
        P.dma("sp", wdw_sb[:], wdw, (), [bcst])
        P.dma("sp", cv[:], cvec, (), [bcst])
        P.dma("sp", hv_sb[:], hv, (), [bhv])
        for j in range(2):
            P.tt("dve", gb[:, j, :], M["g"][j], cv[:, 3, :], ALU.mult, [M["buf"], bcst], [bcst])
        P.memset("pool", uc[:], 0.0, [buc])
        groups1 = [(0, 512, 0), (512, 512, 0), (1024, 512, 0), (1536, 512, 0), (2048, 2 * HALO, 0), (TE, TC, 1)]
        ai = 0
        for (c0, n, isctx) in groups1:
            P.dma("sp", xg[:, :, 0:n], xT[:, :, c0:c0 + n], (), [bx])
            norm_mod(P, C, xg[:, :, 0:n], n, M["a"][isctx], M["b"][isctx], M["buf"], hg[:, :, 0:n], bx, bhg)
            for oc in range(KD):
                pa, pg = ps_a[ai % 2], ps_g[ai % 2]
                bpa_, bpg_ = bpa[ai % 2], bpg[ai % 2]
                s_t, bs_ = sg_[ai % 2], bsg[ai % 2]
                ai += 1
                for k in range(KD):
                    P.mm(pa[:, 0:n], w1_sb[:, k, oc * 128:(oc + 1) * 128], hg[:, k, 0:n], k == 0, k == KD - 1,
                         [bw1, bhg], [bpa_])
                for k in range(KD):
                    P.mm(pg[:, 0:n], w1_sb[:, k, D + oc * 128:D + (oc + 1) * 128], hg[:, k, 0:n], k == 0, k == KD - 1,
                         [bw1, bhg], [bpg_])
                P.act(s_t[:, 0:n], pg[:, 0:n], AF.Sigmoid, [bpg_, bcst], [bs_], bias=b1_sb[:, 8 + oc:9 + oc], scale=1.0)
                if isctx:
                    dst, bd = uc[:, oc, HALO:HALO + TC], buc
                else:
                    dst, bd = uT[:, oc, c0:c0 + n], bu
                P.stt("dve", dst, pa[:, 0:n], b1_sb[:, oc:oc + 1], s_t[:, 0:n], ALU.add, ALU.mult,
                      [bpa_, bcst, bs_], [bd])
        for oc in range(KD):
            P.ts("dve", uT[:, oc, 0:HALO], uT[:, oc, 0:HALO], hv_sb[:, 0:1], ALU.mult, [bu, bhv], [bu])
            P.ts("dve", uT[:, oc, HALO + TL:TE], uT[:, oc, HALO + TL:TE], hv_sb[:, 1:2], ALU.mult, [bu, bhv], [bu])
        outs = []
        yi = 0
        groups2 = [(c0, N2, 0) for c0 in range(0, TL, N2)] + [(0, TC, 1)]
        for gi, (c0, n, isctx) in enumerate(groups2):
            src, bsrc = (uc, buc) if isctx else (uT, bu)
            NDV = 5
            for oc in range(NDV):
                P.ts("dve", acc[:, oc, 0:n], src[:, oc, c0:c0 + n], wdw_sb[:, oc, 0:1], ALU.mult, [bsrc, bcst],
                     [baccs[oc], bacc])
            for oc in range(NDV, KD):
                P.act(acc[:, oc, 0:n], src[:, oc, c0:c0 + n], AF.Identity, [bsrc, bcst], [baccs[oc], bacc],
                      bias=cv[:, 0, oc:oc + 1], scale=wdw_sb[:, oc, 0:1])
                P.act(ctmps[oc][1][:, 0:n], src[:, oc, c0 + 1:c0 + 1 + n], AF.Identity, [bsrc, bcst], [bcts[oc][1]],
                      scale=wdw_sb[:, oc, 1:2])
            for k in range(1, 31):
                for oc in range(NDV):
                    P.stt("dve", acc[:, oc, 0:n], src[:, oc, c0 + k:c0 + k + n], wdw_sb[:, oc, k:k + 1],
                          acc[:, oc, 0:n], ALU.mult, ALU.add, [bsrc, bcst, baccs[oc]], [baccs[oc]])
                for oc in range(NDV, KD):
                    if k + 1 < 31:
                        P.act(ctmps[oc][(k + 1) % 2][:, 0:n], src[:, oc, c0 + k + 1:c0 + k + 1 + n], AF.Identity,
                              [bsrc, bcst], [bcts[oc][(k + 1) % 2]], scale=wdw_sb[:, oc, k + 1:k + 2])
                    P.tt("pool", acc[:, oc, 0:n], acc[:, oc, 0:n], ctmps[oc][k % 2][:, 0:n], ALU.add,
                         [bcts[oc][k % 2], baccs[oc]], [baccs[oc]])
            for oc in range(NDV):
                P.ts("dve", acc[:, oc, 0:n], acc[:, oc, 0:n], cv[:, 0, oc:oc + 1], ALU.add,
                     [baccs[oc], bcst], [baccs[oc]])
            P.op("dve", lambda e: e.memset(jn[:], 0.0), baccs, [bacc])
            P.act(C.sq[:, :, 0:n], acc[:, :, 0:n], AF.Copy, [bacc], [C.b_sq])
            for k in range(KD):
                P.mm(ps_m[:, 0:n], C.ones[:], C.sq[:, k, 0:n], k == 0, k == KD - 1, [C.b_ones, C.b_sq], [bpm])
            P.tt("dve", acc[:, :, 0:n], acc[:, :, 0:n],
                 view(ps_m[:, 0:n], [list(ps_m[:].ap[0]), [0, KD], [1, n]]), ALU.subtract, [bacc, bpm], [bacc])
            norm_mod(P, C, acc[:, :, 0:n], n, cv[:, 1, :], cv[:, 2, :], bcst, hg[:, :, 0:n], bacc, bhg, func=AF.Silu)
            xc0 = TE + c0 if isctx else HALO + c0
            P.dma("sp", xg[:, :, 0:n], xT[:, :, xc0:xc0 + n], (), [bx])
            for oc in range(KD):
                py, by_ = ps_y[yi % 2], bpy[yi % 2]
                yi += 1
                for k in range(KD):
                    P.mm(py[:, 0:n], w2_sb[:, k, oc * 128:(oc + 1) * 128], hg[:, k, 0:n], k == 0, k == KD - 1,
                         [bw2, bhg], [by_])
                P.stt("dve", xg[:, oc, 0:n], py[:, 0:n], M["g"][isctx][:, oc:oc + 1], xg[:, oc, 0:n],
                      ALU.mult, ALU.add, [by_, M["buf"], bx], [bx])
                P.ts("dve", xg[:, oc, 0:n], xg[:, oc, 0:n], gb[:, isctx, oc:oc + 1], ALU.add, [bx, bcst], [bx])
            bo = P.buf("out")
            oc0 = TL + c0 if isctx else c0
            P.dma("sp", xo[:, :, oc0:oc0 + n], xg[:, :, 0:n], [bx], [bo])
            outs.append(bo)
        P.wait_all("sp", outs)
    return None


def win_masks():
    kk = np.arange(128)[:, None, None]
    r = np.arange(6)[None, :, None]
    qq = np.arange(512)[None, None, :]
    return (np.abs(128 * (r - 1) + kk - qq) <= 128).astype(np.float32).astype(NPBF)


def gather_kv_win(kTs, vPs):
    out = []
    for c in range(NCORES):
        q = c % 4
        zk = np.zeros_like(kTs[c][:, :, 0:128])
        zv = np.zeros_like(vPs[c][:, 0:1])
        kprev = kTs[c - 1][:, :, TL - 128:TL] if q > 0 else zk
        knext = kTs[c + 1][:, :, 0:128] if q < 3 else zk
        vprev = vPs[c - 1][:, 15:16] if q > 0 else zv
        vnext = vPs[c + 1][:, 0:1] if q < 3 else zv
        kT = np.concatenate([kTs[c][:, :, TL:TT], kprev, kTs[c][:, :, 0:TL], knext], axis=2)
        vP = np.concatenate([vPs[c][:, 16:18], vprev, vPs[c][:, 0:16], vnext], axis=1)
        out.append((np.ascontiguousarray(kT), np.ascontiguousarray(vP)))
    return out


def pre_inputs(i, xs, mods, wqkv, qg, kg, nq, nk, cos, sin, norm1_g):
    ident = np.eye(128, dtype=np.float32).astype(NPBF)
    gvec = np.concatenate([np.tile(qg, nq), np.tile(kg, nk)]).astype(np.float32)
    gvec = np.ascontiguousarray(np.broadcast_to(gvec[None, :], (128, gvec.size)))
    ng1 = colv(norm1_g)
    w = wl(wqkv)
    return [{"xT": xs[c], "modT": mods[c // 4][i], "ng": ng1, "wqkv": w, "gvec": gvec,
             "cs": cs_for_core(cos, sin, c % 4), "ident": ident} for c in range(NCORES)], ng1


def layer_win(i, xs, mods, inp, cos, sin):
    maps, ng1 = pre_inputs(i, xs, mods, inp["swa_w_qkv"][0], inp["swa_q_g"][0], inp["swa_k_g"][0], 16, 4,
                           cos, sin, inp["norm1_g"][i])
    res = run(prog("pre_gqa", build_pre_gqa, "gqa"), maps)
    kv = gather_kv_win([r["kT"] for r in res], [r["vP"] for r in res])
    wo = np.ascontiguousarray(inp["swa_w_o"][0].reshape(16, 64, D).transpose(1, 0, 2))
    masks = win_masks()
    sink = np.ascontiguousarray(inp["swa_sink"][0].reshape(1, 16))
    maps = [{"xT": xs[c], "modT": mods[c // 4][i], "ng": ng1, "qT": res[c]["qT"], "kT": kv[c][0], "vP": kv[c][1],
             "wo": wo, "masks": masks, "sink": sink} for c in range(NCORES)]
    res2 = run(prog("att", build_att, "win", False), maps)
    return [r["xo"] for r in res2]


def layer_diff(i, xs, mods, inp, cos, sin):
    maps, ng1 = pre_inputs(i, xs, mods, inp["diff_w_qkv"][0], inp["diff_q_g"][0], inp["diff_k_g"][0], 16, 16,
                           cos, sin, inp["norm1_g"][i])
    res = run(prog("pre_gqa", build_pre_gqa, "diff"), maps)
    kv = gather_kv_dense([r["kT"] for r in res], [r["vP"] for r in res])
    kv = [(k, np.ascontiguousarray(v.transpose(0, 2, 1, 3))) for (k, v) in kv]
    wo = wl(inp["diff_w_o"][0])
    lamv = np.ascontiguousarray(np.stack([inp["diff_lam_q1"][0], inp["diff_lam_k1"][0], inp["diff_lam_q2"][0],
                                          inp["diff_lam_k2"][0]])[None])
    slg = np.ascontiguousarray(inp["diff_subln_g"][0].reshape(128, 1))
    lam_init = 0.8 - 0.6 * math.exp(-0.3 * i)
    maps = [{"xT": xs[c], "modT": mods[c // 4][i], "ng": ng1, "qT": res[c]["qT"], "kT": kv[c // 4][0],
             "vP": kv[c // 4][1], "wo": wo, "lamv": lamv, "slg": slg} for c in range(NCORES)]
    res2 = run(prog("att_diff", build_att_diff, lam_init), maps)
    return [r["xo"] for r in res2]


def layer_conv(i, xs, mods, inp):
    ng1 = colv(inp["norm1_g"][i])
    w1 = wl(inp["conv_w_pw1"][0])
    b1 = colv(inp["conv_b_pw1"][0])
    wdw = np.ascontiguousarray(inp["conv_w_dw"][0].T.reshape(KD, 128, 31).transpose(1, 0, 2))
    cvec = np.ascontiguousarray(np.stack([colv(inp["conv_b_dw"][0]), colv(inp["conv_ln_g"][0]),
                                          colv(inp["conv_ln_b"][0]), colv(inp["conv_b_pw2"][0])], axis=1))
    w2 = wl(inp["conv_w_pw2"][0])
    maps = []
    for c in range(NCORES):
        q = c % 4
        z = np.zeros((128, KD, HALO), np.float32)
        left = xs[c - 1][:, :, TL - HALO:TL] if q > 0 else z
        right = xs[c + 1][:, :, 0:HALO] if q < 3 else z
        xe = np.ascontiguousarray(np.concatenate([left, xs[c][:, :, 0:TL], right, xs[c][:, :, TL:TT]], axis=2))
        hv = np.ascontiguousarray(np.broadcast_to(np.array([[float(q > 0), float(q < 3)]], np.float32), (128, 2)))
        maps.append({"xT": xe, "modT": mods[c // 4][i], "ng": ng1, "w1": w1, "b1": b1, "wdw": wdw, "cvec": cvec,
                     "w2": w2, "hv": hv})
    res = run(prog("conv", build_conv), maps)
    return [r["xo"] for r in res]


def kernel(**inputs):
    inp = {k: np.asarray(v, dtype=np.float32) for k, v in inputs.items()}
    cos, sin = rope_tables()
    mods = run_mod(inp["c"], inp["c_ctx"], inp["mod_w"], inp["mod_b"])
    xs = shard_x(inp["x"], inp["ctx"])
    xs = layer_gqa_dense(0, xs, mods, inp, True, cos, sin)
    xs = layer_mlp(0, xs, mods, inp, True)
    xs = layer_conv(1, xs, mods, inp)
    xs = layer_mlp(1, xs, mods, inp, True)
    xs = layer_diff(2, xs, mods, inp, cos, sin)
    xs = layer_mlp(2, xs, mods, inp, True)
    xs = layer_win(3, xs, mods, inp, cos, sin)
    xs = layer_mlp(3, xs, mods, inp, False)
    return unshard_x(xs)


GROUPS4 = [[0, 1, 2, 3], [4, 5, 6, 7]]


def build_fused(stop_after=99):
    nc = new_nc()
    I = {}
    step = [0]

    class Stop(Exception):
        pass

    def chk():
        step[0] += 1
        if step[0] > stop_after:
            raise Stop()

    def inp(name, shape, dt=F32):
        I[name] = din(nc, name, shape, dt)
        return I[name]

    xT_in = inp("xT", [128, KD, TT])
    inp("cT", [128, KD, 2])
    inp("mod_w", [DEPTH, 128, KD, 1536])
    inp("mod_b", [DEPTH, 128, 12])
    inp("ng1", [DEPTH, 128, KD])
    inp("ng2", [DEPTH, 128, KD])
    inp("wu", [DEPTH, 128, KD, DFF])
    inp("wd", [DEPTH, 128, 32, D])
    inp("cs", [128, 16, 2, 32])
    inp("ident", [128, 128], BF16)
    inp("gqa_wqkv", [128, KD, 1536]); inp("gqa_gvec", [128, 1280]); inp("gqa_wo", [64, 16, D])
    inp("swa_wqkv", [128, KD, 1536]); inp("swa_gvec", [128, 1280]); inp("swa_wo", [64, 16, D])
    inp("masks", [128, 6, 512], BF16); inp("sink", [1, 16]); inp("selv", [128, 2, 4])
    inp("diff_wqkv", [128, KD, 3072]); inp("diff_gvec", [128, 2048]); inp("diff_wo", [128, 8, D])
    inp("lamv", [1, 4, 64]); inp("slg", [128, 1])
    inp("w1", [128, KD, 2 * D]); inp("b1", [128, 16]); inp("wdw", [128, KD, 31]); inp("cvec", [128, 4, KD])
    inp("w2", [128, KD, D]); inp("hv", [128, 2])
    xo = dout(nc, "xo", [128, KD, TT])
    modT = dint(nc, "modT_i", [DEPTH, 128, 48, 2])
    xA = dint(nc, "xA", [128, KD, TT])
    xB = dint(nc, "xB", [128, KD, TT])
    qT = dint(nc, "qT_i", [128, KD, TT], BF16)
    kloc = dint(nc, "kloc", [4, 128, TT], BF16)
    vloc = dint(nc, "vloc", [2, 128, 9, 4, 65], BF16)
    kall = dint(nc, "kall", [4, 512, TT], BF16)
    vall = dint(nc, "vall", [2, 512, 9, 4, 65], BF16)
    kloc2 = dint(nc, "kloc2", [8, 128, TT], BF16)
    vloc2 = dint(nc, "vloc2", [8, 128, 18, 128], BF16)
    kall2 = dint(nc, "kall2", [8, 512, TT], BF16)
    vall2 = dint(nc, "vall2", [8, 512, 18, 128], BF16)
    xe_loc = dint(nc, "xe_loc", [128, KD * 2 * HALO])
    xe_all = dint(nc, "xe_all", [512, KD * 2 * HALO])
    xext = dint(nc, "xext", [128, KD, TE + TC])
    with contextlib.ExitStack() as st:
        P = Prog(nc, st)
        P.init_arena()
        try:
            _fused_body(P, I, chk, modT, xT_in, xA, xB, xo, qT, kloc, vloc, kall, vall, kloc2, vloc2, kall2, vall2,
                        xe_loc, xe_all, xext)
        except Stop:
            P.phase()
            stg = P.sbuf("stg", [128, KD, 512], F32)
            bs_ = P.buf("stg")
            for c0 in range(0, TT, 512):
                n = min(512, TT - c0)
                P.dma("sp", stg[:, :, 0:n], xT_in[:, :, c0:c0 + n], (), [bs_])
                P.dma("sp", xo[:, :, c0:c0 + n], stg[:, :, 0:n], [bs_], [P.buf("o")])
        P.barrier()
        P.emit()
    return nc


def _fused_body(P, I, chk, modT, xT_in, xA, xB, xo, qT, kloc, vloc, kall, vall, kloc2, vloc2, kall2, vall2,
                xe_loc, xe_all, xext):
    if True:
        chk()
        modloc = dint(P.nc, "modloc", [128, DEPTH * 24])
        modall = dint(P.nc, "modall", [512, DEPTH * 24])
        build_mod(P, {"cT": I["cT"], "mod_w": I["mod_w"], "mod_b": I["mod_b"], "modloc": modloc})
        P.coll("AllGather", GROUPS4, [(modloc, modall)])
        P.phase()
        mg = P.sbuf("mg", [128, 4, DEPTH, 12, 2], F32)
        bmg = P.buf("mg")
        for r in range(4):
            P.dma("sp", mg[:, r].rearrange("p a b c -> p (a b c)"), modall[r * 128:(r + 1) * 128, :], (), [bmg])
        for l in range(DEPTH):
            for r in range(4):
                P.dma("sp", modT[l][:, r * 12:(r + 1) * 12, :], mg[:, r, l], [bmg], [P.buf("modT")])

        kview = kloc.rearrange("g p t -> p g t")
        vtiles = [vloc[t // 9, :, t % 9] for t in range(18)]

        def cc_gqa():
            P.coll("AllGather", GROUPS4, [(kloc[g], kall[g]) for g in range(4)] +
                   [(vloc[hf].rearrange("p t g d -> p (t g d)"), vall[hf].rearrange("p t g d -> p (t g d)"))
                    for hf in range(2)])

        def mlp(l, xin, xout, need_ctx):
            build_mlp(P, {"xT": xin, "modT": modT[l], "ng": I["ng2"][l], "wu": I["wu"][l], "wd": I["wd"][l],
                          "xo": xout}, need_ctx)

        chk()
        build_pre_gqa(P, {"xT": xT_in, "modT": modT[0], "ng": I["ng1"][0], "wqkv": I["gqa_wqkv"],
                          "gvec": I["gqa_gvec"], "cs": I["cs"], "ident": I["ident"], "qT": qT, "kT": kview,
                          "vP_tiles": vtiles}, "gqa")
        chk()
        cc_gqa()
        build_att(P, {"xT": xT_in, "modT": modT[0], "ng": I["ng1"][0], "qT": qT, "kloc": kloc, "vloc": vloc,
                      "kall": kall, "vall": vall, "wo": I["gqa_wo"], "xo": xA}, "dense", True)
        chk()
        mlp(0, xA, xB, True)
        chk()
        P.phase()
        edge = P.sbuf("edge", [128, KD, 2, HALO], F32)
        bed = P.buf("edge")
        P.dma("sp", edge[:, :, 0, :], xB[:, :, 0:HALO], (), [bed])
        P.dma("sp", edge[:, :, 1, :], xB[:, :, TL - HALO:TL], (), [bed])
        P.dma("sp", xe_loc, edge[:].rearrange("p a b c -> p (a b c)"), [bed], [P.buf("xe")])
        P.coll("AllGather", GROUPS4, [(xe_loc, xe_all)])
        P.phase()
        ea = P.sbuf("ea", [128, 4, KD, 2, HALO], F32)
        sv = P.sbuf("sv2", [128, 2, 4], F32)
        hl = P.sbuf("hl", [128, 2, KD, HALO], F32)
        bea, bsv, bhl = P.buf("ea"), P.buf("sv2"), P.buf("hl")
        for r in range(4):
            P.dma("sp", ea[:, r].rearrange("p a b c -> p (a b c)"), xe_all[r * 128:(r + 1) * 128, :], (), [bea])
        P.dma("sp", sv[:], I["selv"], (), [bsv])
        for side in range(2):
            src_i = 1 - side
            P.ts("dve", hl[:, side], ea[:, 0, :, src_i, :], sv[:, side, 0:1], ALU.mult, [bea, bsv], [bhl])
            for r in range(1, 4):
                P.stt("dve", hl[:, side], ea[:, r, :, src_i, :], sv[:, side, r:r + 1], hl[:, side], ALU.mult, ALU.add,
                      [bea, bsv, bhl], [bhl])
        bxe = P.buf("xext")
        P.dma("sp", xext[:, :, 0:HALO], hl[:, 0], [bhl], [bxe])
        P.dma("sp", xext[:, :, HALO + TL:TE], hl[:, 1], [bhl], [bxe])
        stage = P.sbuf("stage", [128, KD, 512], F32)
        bst = P.buf("stage")
        for c0 in range(0, TT, 512):
            n = min(512, TT - c0)
            P.dma("sp", stage[:, :, 0:n], xB[:, :, c0:c0 + n], (), [bst])
            d0 = HALO + c0 if c0 < TL else TE + (c0 - TL)
            P.dma("sp", xext[:, :, d0:d0 + n], stage[:, :, 0:n], [bst], [bxe])
        chk()
        build_conv(P, {"xT": xext, "modT": modT[1], "ng": I["ng1"][1], "w1": I["w1"], "b1": I["b1"],
                       "wdw": I["wdw"], "cvec": I["cvec"], "w2": I["w2"], "hv": I["hv"], "xo": xA})
        chk()
        mlp(1, xA, xB, True)
        chk()
        build_pre_gqa(P, {"xT": xB, "modT": modT[2], "ng": I["ng1"][2], "wqkv": I["diff_wqkv"],
                          "gvec": I["diff_gvec"], "cs": I["cs"], "ident": I["ident"], "qT": qT,
                          "kT": kloc2.rearrange("g p t -> p g t"),
                          "vP_tiles": [vloc2.rearrange("h p t d -> p h t d")[:, :, t, :] for t in range(18)]}, "diff")
        chk()
        P.coll("AllGather", GROUPS4, [(kloc2[h], kall2[h]) for h in range(8)] +
               [(vloc2[h].rearrange("p t d -> p (t d)"), vall2[h].rearrange("p t d -> p (t d)")) for h in range(8)])
        build_att_diff(P, {"xT": xB, "modT": modT[2], "ng": I["ng1"][2], "qT": qT, "kloc": kloc2, "vloc": vloc2,
                           "kall": kall2, "vall": vall2, "wo": I["diff_wo"], "lamv": I["lamv"], "slg": I["slg"],
                           "xo": xA}, 0.8 - 0.6 * math.exp(-0.3 * 2))
        chk()
        mlp(2, xA, xB, True)
        chk()
        build_pre_gqa(P, {"xT": xB, "modT": modT[3], "ng": I["ng1"][3], "wqkv": I["swa_wqkv"],
                          "gvec": I["swa_gvec"], "cs": I["cs"], "ident": I["ident"], "qT": qT, "kT": kview,
                          "vP_tiles": vtiles}, "gqa")
        cc_gqa()
        build_att(P, {"xT": xB, "modT": modT[3], "ng": I["ng1"][3], "qT": qT, "kloc": kloc, "vloc": vloc,
                      "kall": kall, "vall": vall, "wo": I["swa_wo"], "masks": I["masks"], "sink": I["sink"],
                      "selv": I["selv"], "xo": xA}, "win", False)
        chk()
        mlp(3, xA, xo, False)


def kernel(**inputs):
    inp = {k: np.asarray(v, dtype=np.float32) for k, v in inputs.items()}
    cos, sin = rope_tables()
    xs = shard_x(inp["x"], inp["ctx"])
    ident = np.eye(128, dtype=np.float32).astype(NPBF)

    def gv(qg, kg, nq, nk):
        g = np.concatenate([np.tile(qg, nq), np.tile(kg, nk)]).astype(np.float32)
        return np.ascontiguousarray(np.broadcast_to(g[None, :], (128, g.size)))

    def wo64(w):
        return np.ascontiguousarray(w.reshape(16, 64, D).transpose(1, 0, 2))

    shared = {
        "ng1": np.stack([colv(inp["norm1_g"][l]) for l in range(DEPTH)]),
        "ng2": np.stack([colv(inp["norm2_g"][l]) for l in range(DEPTH)]),
        "wu": np.stack([wl(inp["mlp_up"][l]) for l in range(DEPTH)]),
        "wd": np.stack([wl(inp["mlp_down"][l]) for l in range(DEPTH)]),
        "ident": ident,
        "gqa_wqkv": wl(inp["gqa_w_qkv"][0]), "gqa_gvec": gv(inp["gqa_q_g"][0], inp["gqa_k_g"][0], 16, 4),
        "gqa_wo": wo64(inp["gqa_w_o"][0]),
        "swa_wqkv": wl(inp["swa_w_qkv"][0]), "swa_gvec": gv(inp["swa_q_g"][0], inp["swa_k_g"][0], 16, 4),
        "swa_wo": wo64(inp["swa_w_o"][0]),
        "masks": win_masks(), "sink": np.ascontiguousarray(inp["swa_sink"][0].reshape(1, 16)),
        "diff_wqkv": wl(inp["diff_w_qkv"][0]), "diff_gvec": gv(inp["diff_q_g"][0], inp["diff_k_g"][0], 16, 16),
        "diff_wo": wl(inp["diff_w_o"][0]),
        "lamv": np.ascontiguousarray(np.stack([inp["diff_lam_q1"][0], inp["diff_lam_k1"][0], inp["diff_lam_q2"][0],
                                               inp["diff_lam_k2"][0]])[None]),
        "slg": np.ascontiguousarray(inp["diff_subln_g"][0].reshape(128, 1)),
        "w1": wl(inp["conv_w_pw1"][0]), "b1": colv(inp["conv_b_pw1"][0]),
        "wdw": np.ascontiguousarray(inp["conv_w_dw"][0].T.reshape(KD, 128, 31).transpose(1, 0, 2)),
        "cvec": np.ascontiguousarray(np.stack([colv(inp["conv_b_dw"][0]), colv(inp["conv_ln_g"][0]),
                                               colv(inp["conv_ln_b"][0]), colv(inp["conv_b_pw2"][0])], axis=1)),
        "w2": wl(inp["conv_w_pw2"][0]),
    }
    maps = []
    for c in range(NCORES):
        b, q = c // 4, c % 4
        selv = np.zeros((128, 2, 4), np.float32)
        if q > 0:
            selv[:, 0, q - 1] = 1.0
        if q < 3:
            selv[:, 1, q + 1] = 1.0
        m = dict(shared)
        mc = slice(q * 1536, (q + 1) * 1536)
        m.update({"mod_w": np.ascontiguousarray(inp["mod_w"][:, :, mc].reshape(DEPTH, KD, 128, 1536).transpose(0, 2, 1, 3)),
                  "mod_b": np.ascontiguousarray(inp["mod_b"][:, mc].reshape(DEPTH, 12, 128).transpose(0, 2, 1))})
        m.update({"xT": xs[c], "cT": np.ascontiguousarray(np.stack([colv(inp["c"][b]), colv(inp["c_ctx"])], axis=-1)),
                  "cs": cs_for_core(cos, sin, q), "selv": selv,
                  "hv": np.ascontiguousarray(np.broadcast_to(np.array([[float(q > 0), float(q < 3)]], np.float32),
                                                             (128, 2)))})
        maps.append(m)
    nc = prog("fused", build_fused)
    res = run(nc, maps)
    return unshard_x([r["xo"] for r in res])
```

```python
import contextlib
import math
import numpy as np
import ml_dtypes
import concourse.bass as bass
import concourse.mybir as mybir
from concourse.bass_utils import run_bass_kernel_spmd

F32 = mybir.dt.float32
BF16 = mybir.dt.bfloat16
ALU = mybir.AluOpType
AF = mybir.ActivationFunctionType
AX = mybir.AxisListType
NPBF = ml_dtypes.bfloat16

NCORES = 8
D = 1024
KD = 8
TL = 2048
TC = 256
TT = TL + TC
SEQ = 8192
DFF = 4096
EPS = 1e-6
DEPTH = 4

EPOCH = 16000
NDMA = 20
ENGS = ("pe", "act", "dve", "pool", "sp")
SAME_ENGINE_SYNC = {"pe": False, "act": True, "dve": True, "pool": True, "sp": True}


class Buf:
    __slots__ = ("name", "w", "r")

    def __init__(self, name):
        self.name = name
        self.w = None
        self.r = {}


class Prog:
    def __init__(self, nc, stack):
        self.nc = nc
        self.stack = stack
        self.streams = {e: [] for e in ENGS}
        self.cnt = {e: 0 for e in ENGS}
        self.dcnt = {e: 0 for e in ENGS}
        self.known = {e: {} for e in ENGS}
        self.psem = {e: [] for e in ENGS}
        self.dsem = {e: [stack.enter_context(nc.semaphore(f"d_{e}_{i}")) for i in range(NDMA)]
                     for e in ("sp", "act", "pool")}
        self.nbuf = 0

    def init_arena(self, sb_bytes=204 * 1024):
        self.sb_cap = sb_bytes
        self.ar_f32 = self.stack.enter_context(self.nc.sbuf_tensor("arena", [128, sb_bytes // 4], F32))[:]
        self.ar_bf = self.ar_f32.bitcast(BF16)
        self.ps_f32 = self.stack.enter_context(self.nc.psum_tensor("psarena", [128, 4096], F32))[:]
        self.ps_bf = self.ps_f32.bitcast(BF16)
        self.sb_off = 0
        self.ps_off = 0
        self.extra = []

    @staticmethod
    def _carve(base, off_elems, shape):
        dims = [[base.ap[0][0], shape[0]]]
        rev, st_ = [], 1
        for n_ in reversed(shape[1:]):
            rev.append([st_, n_])
            st_ *= n_
        return bass.AP(base.tensor, base.offset + off_elems, dims + list(reversed(rev)))

    def barrier(self):
        toks = list(self.extra)
        for e in ENGS:
            n = self.cnt[e]
            if n > 0:
                ep = (n - 1) // EPOCH
                toks.append((("p", e, ep), self.psem[e][ep], (n - 1) % EPOCH + 1))
        for q, sems in self.dsem.items():
            for slot in range(NDMA):
                c = (self.dcnt[q] - slot + NDMA - 1) // NDMA
                if c > 0:
                    toks.append((("d", q, slot), sems[slot], 16 * c))
        for e in ENGS:
            waits = self._waits(e, (), (), toks)
            self.streams[e].append((waits, None, None, 0))

    def phase(self):
        self.barrier()
        self.sb_off = 0
        self.ps_off = 0

    def coll(self, kind, groups, pairs):
        self.barrier()
        if not hasattr(self, "ccsem"):
            self.ccsem = self.stack.enter_context(self.nc.semaphore("ccsem"))
            self.ccn = 0
        for (src, dst) in pairs:
            self.ccn += 1
            tok = (("c",), self.ccsem, self.ccn)

            def fn(e, src=src, dst=dst):
                return e.collective_compute(kind, ALU.bypass, replica_groups=groups, ins=[src], outs=[dst])
            self.streams["pool"].append(([], fn, tok, 1))
        self.extra = [(("c",), self.ccsem, self.ccn)]
        self.barrier()

    def buf(self, name=None):
        self.nbuf += 1
        return Buf(name or f"b{self.nbuf}")

    def bufs(self, n, name="b"):
        return [self.buf(f"{name}{i}") for i in range(n)]

    def sbuf(self, name, shape, dtype):
        size = 4 if dtype == F32 else 2
        nbytes = size * int(np.prod(shape[1:]))
        off = (self.sb_off + 31) // 32 * 32
        self.sb_off = off + nbytes
        assert self.sb_off <= self.sb_cap, f"SBUF arena overflow at {name}: {self.sb_off}"
        return self._carve(self.ar_f32 if dtype == F32 else self.ar_bf, off // size, list(shape))

    def psum(self, name, shape, dtype):
        size = 4 if dtype == F32 else 2
        nbytes = (size * int(np.prod(shape[1:])) + 2047) // 2048 * 2048
        off = self.ps_off
        self.ps_off = off + nbytes
        assert self.ps_off <= 16384, f"PSUM arena overflow at {name}"
        return self._carve(self.ps_f32 if dtype == F32 else self.ps_bf, off // size, list(shape))

    def _waits(self, eng, reads, writes, extra=()):
        waits = []
        kn = self.known[eng]

        def need(tok):
            if tok is None:
                return
            key, _, val = tok
            if key[0] == "p" and key[1] == eng and not SAME_ENGINE_SYNC[eng]:
                return
            if kn.get(key, 0) >= val:
                return
            kn[key] = val
            waits.append(tok)

        for b in reads:
            need(b.w)
        for b in writes:
            need(b.w)
            for t in b.r.values():
                need(t)
        for t in extra:
            need(t)
        return waits

    def _commit(self, tok, reads, writes):
        key = tok[0]
        for b in reads:
            b.r[key] = tok
        for b in writes:
            b.w = tok
            b.r = {}

    def op(self, eng, fn, reads=(), writes=()):
        waits = self._waits(eng, reads, writes)
        n = self.cnt[eng]
        self.cnt[eng] += 1
        ep = n // EPOCH
        while len(self.psem[eng]) <= ep:
            self.psem[eng].append(self.stack.enter_context(self.nc.semaphore(f"p_{eng}_{len(self.psem[eng])}")))
        tok = (("p", eng, ep), self.psem[eng][ep], n % EPOCH + 1)
        self.streams[eng].append((waits, fn, tok, 1))
        self._commit(tok, reads, writes)
        return tok

    def dma(self, q, out, in_, reads=(), writes=()):
        j = self.dcnt[q]
        self.dcnt[q] += 1
        slot, rnd = j % NDMA, j // NDMA
        sem = self.dsem[q][slot]
        key = ("d", q, slot)
        extra = [(key, sem, 16 * rnd)] if rnd > 0 else []
        waits = self._waits(q, reads, writes, extra)
        tok = (key, sem, 16 * (rnd + 1))
        self.streams[q].append((waits, lambda e: e.dma_start(out=out, in_=in_), tok, 16))
        self._commit(tok, reads, writes)
        return tok

    def wait_all(self, eng, bufs):
        waits = self._waits(eng, (), bufs)
        self.streams[eng].append((waits, None, None, 0))

    def emit(self):
        nc = self.nc
        with nc.Block() as block:
            def run(stream):
                def body(e):
                    for waits, fn, tok, inc in stream:
                        for (_, sem, val) in waits:
                            e.wait_ge(sem, val)
                        if fn is not None:
                            fn(e).then_inc(tok[1], inc)
                return body

            block.sync(run(self.streams["sp"]))
            block.scalar(run(self.streams["act"]))
            block.vector(run(self.streams["dve"]))
            block.gpsimd(run(self.streams["pool"]))
            block.tensor(run(self.streams["pe"]))

    def mm(self, out, lhsT, rhs, start, stop, r, w):
        return self.op("pe", lambda e: e.matmul(out, lhsT=lhsT, rhs=rhs, start=start, stop=stop), r, w)

    def tr(self, out, in_, ident, r, w):
        return self.op("pe", lambda e: e.transpose(out, in_, ident), r, w)

    def act(self, out, in_, func, r, w, bias=None, scale=None):
        kw = {}
        if bias is not None:
            kw["bias"] = bias
        if scale is not None:
            kw["scale"] = scale
        return self.op("act", lambda e: e.activation(out=out, in_=in_, func=func, **kw), r, w)

    def tt(self, eng, out, in0, in1, op, r, w):
        return self.op(eng, lambda e: e.tensor_tensor(out=out, in0=in0, in1=in1, op=op), r, w)

    def ts(self, eng, out, in0, s1, op0, r, w, s2=None, op1=None):
        if op1 is None:
            return self.op(eng, lambda e: e.tensor_scalar(out=out, in0=in0, scalar1=s1, scalar2=None, op0=op0), r, w)
        return self.op(eng, lambda e: e.tensor_scalar(out=out, in0=in0, scalar1=s1, scalar2=s2, op0=op0, op1=op1), r, w)

    def stt(self, eng, out, in0, scalar, in1, op0, op1, r, w):
        return self.op(eng, lambda e: e.scalar_tensor_tensor(out=out, in0=in0, scalar=scalar, in1=in1,
                                                             op0=op0, op1=op1), r, w)

    def copy(self, eng, out, in_, r, w):
        if eng == "act":
            return self.act(out, in_, AF.Copy, r, w)
        return self.op(eng, lambda e: e.tensor_copy(out=out, in_=in_), r, w)

    def recip(self, out, in_, r, w):
        return self.op("dve", lambda e: e.reciprocal(out=out, in_=in_), r, w)

    def reduce(self, out, in_, op, r, w):
        return self.op("dve", lambda e: e.tensor_reduce(out=out, in_=in_, axis=AX.X, op=op), r, w)

    def memset(self, eng, ap, val, w):
        return self.op(eng, lambda e: e.memset(ap, val), (), w)


def view(ap, dims):
    return bass.AP(ap.tensor, ap.offset, dims)


def new_nc():
    return bass.Bass("TRN2", target_bir_lowering=False)


def din(nc, name, shape, dt=F32):
    if isinstance(nc, dict):
        ap = nc[name]
        assert tuple(ap.shape) == tuple(shape), (name, tuple(ap.shape), tuple(shape))
        return ap
    return nc.dram_tensor(name, list(shape), dt, kind="ExternalInput").ap()


def dout(nc, name, shape, dt=F32):
    if isinstance(nc, dict):
        return din(nc, name, shape, dt)
    return nc.dram_tensor(name, list(shape), dt, kind="ExternalOutput").ap()


def dint(nc, name, shape, dt=F32):
    return nc.dram_tensor(name, list(shape), dt, kind="Internal").ap()


class NormCtx:
    def __init__(self, P, tag=""):
        self.P = P
        self.ones = P.sbuf("nm_ones" + tag, [128, 128], BF16)
        self.sq = P.sbuf("nm_sq" + tag, [128, KD, 512], BF16)
        self.rs = P.sbuf("nm_rs" + tag, [128, 512], F32)
        self.tmp = P.sbuf("nm_tmp" + tag, [128, KD, 512], F32)
        self.ps = P.psum("nm_ps" + tag, [128, 512], F32)
        self.b_ones = P.buf("nm_ones")
        self.b_sq = P.buf("nm_sq")
        self.b_rs = P.buf("nm_rs")
        self.b_tmp = P.buf("nm_tmp")
        self.b_ps = P.buf("nm_ps")
        P.memset("dve", self.ones[:], 1.0 / 1024.0, [self.b_ones])


def norm_mod(P, C, xg, n, a, b, bmod, hg, bx, bh, func=AF.Identity):
    P.act(C.sq[:, :, 0:n], xg, AF.Square, [bx], [C.b_sq])
    for k in range(KD):
        P.mm(C.ps[:, 0:n], C.ones[:], C.sq[:, k, 0:n], k == 0, k == KD - 1, [C.b_ones, C.b_sq], [C.b_ps])
    P.act(C.rs[:, 0:n], C.ps[:, 0:n], AF.Sqrt, [C.b_ps], [C.b_rs], bias=EPS, scale=1.0)
    P.recip(C.rs[:, 0:n], C.rs[:, 0:n], [C.b_rs], [C.b_rs])
    for k in range(KD):
        P.stt("dve", C.tmp[:, k, 0:n], xg[:, k, :], a[:, k:k + 1], C.rs[:, 0:n], ALU.mult, ALU.mult,
              [bx, C.b_rs, bmod], [C.b_tmp])
    for k in range(KD):
        P.act(hg[:, k, :], C.tmp[:, k, 0:n], func, [C.b_tmp, bmod], [bh], bias=b[:, k:k + 1], scale=1.0)


def load_mod(P, modT_d, ng_d, which):
    mod = P.sbuf("mod_sb", [128, 48, 2], F32)
    ng = P.sbuf("ng_sb", [128, KD], F32)
    aa = P.sbuf("mod_a", [128, 2, KD], F32)
    bb = P.sbuf("mod_b", [128, 2, KD], F32)
    gg = P.sbuf("mod_g", [128, 2, KD], F32)
    bm = P.buf("mod")
    P.dma("sp", mod[:], modT_d, (), [bm])
    P.dma("sp", ng[:], ng_d, (), [bm])
    base = which * 24
    for j in range(2):
        P.stt("dve", aa[:, j, :], mod[:, base + 8:base + 16, j], 1.0, ng[:], ALU.add, ALU.mult, [bm], [bm])
        P.copy("dve", bb[:, j, :], mod[:, base:base + 8, j], [bm], [bm])
        P.copy("dve", gg[:, j, :], mod[:, base + 16:base + 24, j], [bm], [bm])
    return {"a": [aa[:, 0, :], aa[:, 1, :]], "b": [bb[:, 0, :], bb[:, 1, :]],
            "g": [gg[:, 0, :], gg[:, 1, :]], "buf": bm}


PIPE = 4
GROUPS = [(0, 512, 0), (512, 512, 0), (1024, 512, 0), (1536, 512, 0), (2048, 256, 1)]


def build_mod(P, T):
    nc = T
    NV = 2
    NCH = 12
    cT = din(nc, "cT", [128, KD, NV])
    mw = din(nc, "mod_w", [DEPTH, 128, KD, NCH * 128])
    mb = din(nc, "mod_b", [DEPTH, 128, NCH])
    out = dout(nc, "modloc", [128, DEPTH * NCH * NV])
    P.phase()
    c_sb = P.sbuf("c_sb", [128, KD, NV], F32)
    s_sb = P.sbuf("s_sb", [128, KD, NV], F32)
    s_bf = P.sbuf("s_bf", [128, KD, NV], BF16)
    mb_sb = P.sbuf("mb_sb", [128, DEPTH, NCH], F32)
    res = P.sbuf("res", [128, DEPTH, NCH, NV], F32)
    wch = [P.sbuf(f"wch{i}", [128, KD, NCH * 128], BF16) for i in range(2)]
    bw = P.bufs(2, "wch")
    ps = [P.psum(f"ps{i}", [128, 512], F32) for i in range(2)]
    bps = P.bufs(2, "ps")
    bc, bs, bmb, bres = P.buf("c"), P.buf("s"), P.buf("mb"), P.buf("res")
    P.dma("sp", c_sb[:], cT, (), [bc])
    for l in range(DEPTH):
        P.dma("sp", mb_sb[:, l, :], mb[l], (), [bmb])
    P.act(s_sb[:], c_sb[:], AF.Sigmoid, [bc], [bs])
    P.tt("dve", s_sb[:], s_sb[:], c_sb[:], ALU.mult, [bc, bs], [bs])
    P.copy("dve", s_bf[:], s_sb[:], [bs], [bs])
    for l in range(DEPTH):
        w = wch[l % 2]
        for k in range(0, KD, 2):
            P.dma("pool", w[:, k:k + 2, :], mw[l, :, k:k + 2, :], (), [bw[l % 2]])
        pt = ps[l % 2]
        for oc in range(NCH):
            for k in range(KD):
                P.mm(pt[:, oc * NV:(oc + 1) * NV], w[:, k, oc * 128:(oc + 1) * 128], s_bf[:, k, :],
                     k == 0, k == KD - 1, [bw[l % 2], bs], [bps[l % 2]])
        P.tt("dve", res[:, l, :, :], pt[:, 0:NCH * NV].rearrange("p (a b) -> p a b", b=NV),
             view(mb_sb[:, l, :], [list(mb_sb[:].ap[0]), [1, NCH], [0, NV]]),
             ALU.add, [bps[l % 2], bmb], [bres])
    P.dma("sp", out, res[:].rearrange("p a b c -> p (a b c)"), [bres], [P.buf("out")])
    return None


def build_pre_gqa(P, T, kind="gqa"):
    nc = T
    diff = kind == "diff"
    NH = 16
    NKV = 16 if diff else 4
    NKC = 8 if diff else 4
    NQK = NH + NKV
    WQK = NQK * 64
    NV_, DV, DVP = (8, 128, 128) if diff else (4, 64, 65)
    WTOT = WQK + NV_ * DV
    NB = WTOT // 512
    xT = din(nc, "xT", [128, KD, TT])
    modT = din(nc, "modT", [128, 48, 2])
    ng = din(nc, "ng", [128, KD])
    wqkv = din(nc, "wqkv", [128, KD, WTOT])
    gvec = din(nc, "gvec", [128, NQK * 64])
    cs = din(nc, "cs", [128, 16, 2, 32])
    ident_d = din(nc, "ident", [128, 128], BF16)
    qT_o = dout(nc, "qT", [128, KD, TT], BF16)
    kT_o = dout(nc, "kT", [128, NKC, TT], BF16)
    vP_tiles = T["vP_tiles"]
    with contextlib.nullcontext():
        P.phase()
        C = NormCtx(P)
        M = load_mod(P, modT, ng, 0)
        w_sb = P.sbuf("w_sb", [128, KD, WTOT], BF16)
        g_sb = P.sbuf("g_sb", [128, NQK * 64], F32)
        cs_sb = P.sbuf("cs_sb", [128, 16, 2, 32], F32)
        ident = P.sbuf("ident_sb", [128, 128], BF16)
        qst = [P.sbuf(f"qst{i}", [128, KD, 128], BF16) for i in range(2)]
        kst = [P.sbuf(f"kst{i}", [128, NKC, 128], BF16) for i in range(2)]
        vst = [P.sbuf(f"vst{i}", [128, NV_, DVP], BF16) for i in range(2)]
        xg = [P.sbuf("xg0", [128, KD, 512], F32)] * 2 if diff else [P.sbuf(f"xg{i}", [128, KD, 512], F32) for i in range(2)]
        hg = P.sbuf("hg", [128, KD, 512], BF16)
        sq = P.sbuf("sq", [128, NQK * 64], F32)
        ss = P.sbuf("ss", [128, NQK], F32)
        qk = P.sbuf("qk", [128, NQK * 64], F32)
        r1 = P.sbuf("r1", [128, NQK * 32], F32)
        r2 = P.sbuf("r2", [128, NQK * 32], F32)
        r3 = P.sbuf("r3", [128, NQK * 32], F32)
        r4 = P.sbuf("r4", [128, NQK * 32], F32)
        qkb = P.sbuf("qkb", [128, NQK * 64], BF16)
        kdup = P.sbuf("kdup", [128, 4, 2, 64], BF16)
        ps_qkv = P.psum("ps_qkv", [128, WTOT], F32)
        ps_qT = P.psum("ps_qT", [128, 1024], BF16)
        ps_kT = ps_qT if diff else P.psum("ps_kT", [128, 1024], BF16)
        bw, bg, bcs, bid = P.buf("w"), P.buf("g"), P.buf("cs"), P.buf("id")
        bqst, bkst, bvst = P.bufs(2, "qst"), P.bufs(2, "kst"), P.bufs(2, "vst")
        outs = []
        ti = 0
        bxg = [P.buf("xg")] * 2 if diff else P.bufs(2, "xg")
        bhg, bsq, bss, bqk, bqkb, bkd = P.buf("hg"), P.buf("sq"), P.buf("ss"), P.buf("qk"), P.buf("qkb"), P.buf("kd")
        br = P.bufs(4, "r")
        bpq, bpqT = P.buf("psqkv"), P.buf("psqT")
        bpkT = bpqT if diff else P.buf("pskT")
        for k in range(KD):
            P.dma("pool", w_sb[:, k, :], wqkv[:, k, :], (), [bw])
        P.dma("sp", g_sb[:], gvec, (), [bg])
        P.dma("sp", cs_sb[:], cs, (), [bcs])
        P.dma("sp", ident[:], ident_d, (), [bid])
        if not diff:
            for i2 in range(2):
                P.memset("pool", vst[i2][:, :, 64:65], 1.0, [bvst[i2]])
        pst = list(qk[:].ap[0])
        for gi, (c0, n, isctx) in enumerate(GROUPS):
            x_t = xg[gi % 2]
            bx = bxg[gi % 2]
            P.dma("sp", x_t[:, :, 0:n], xT[:, :, c0:c0 + n], (), [bx])
            norm_mod(P, C, x_t[:, :, 0:n], n, M["a"][isctx], M["b"][isctx], M["buf"], hg[:, :, 0:n], bx, bhg)
            for tl in range(n // 128):
                t = c0 // 128 + tl
                tc = slice(c0 + tl * 128, c0 + tl * 128 + 128)
                q_s, k_s, v_s = qst[ti % 2], kst[ti % 2], vst[ti % 2]
                bqT, bkT, bvP = bqst[ti % 2], bkst[ti % 2], bvst[ti % 2]
                ti += 1
                for nb in range(NB):
                    for k in range(KD):
                        P.mm(ps_qkv[:, nb * 512:(nb + 1) * 512], hg[:, k, tl * 128:(tl + 1) * 128],
                             w_sb[:, k, nb * 512:(nb + 1) * 512], k == 0, k == KD - 1, [bhg, bw], [bpq])
                for a0 in range(0, WQK, 512):
                    a1 = min(a0 + 512, WQK)
                    P.act(sq[:, a0:a1], ps_qkv[:, a0:a1], AF.Square, [bpq], [bsq])
                P.reduce(ss[:], sq[:].rearrange("p (h d) -> p h d", d=64), ALU.add, [bsq], [bss])
                P.act(ss[:], ss[:], AF.Sqrt, [bss], [bss], bias=EPS, scale=1.0 / 64.0)
                P.recip(ss[:], ss[:], [bss], [bss])
                for h0 in range(0, NQK, 8):
                    h1 = min(h0 + 8, NQK)
                    P.tt("dve", qk[:, h0 * 64:h1 * 64].rearrange("p (h d) -> p h d", d=64),
                         ps_qkv[:, h0 * 64:h1 * 64].rearrange("p (h d) -> p h d", d=64),
                         view(ss[:, h0:h1], [list(ss[:].ap[0]), [1, h1 - h0], [0, 64]]),
                         ALU.mult, [bpq, bss], [bqk])
                for v0 in range(0, NV_ * DV, 512):
                    v1 = min(v0 + 512, NV_ * DV)
                    P.act(v_s[:, v0 // DV:v1 // DV, 0:DV],
                          ps_qkv[:, WQK + v0:WQK + v1].rearrange("p (h d) -> p h d", d=DV), AF.Copy, [bpq], [bvP])
                if isctx:
                    P.tt("dve", qkb[:], qk[:], g_sb[:], ALU.mult, [bqk, bg], [bqkb])
                else:
                    P.tt("dve", qk[:], qk[:], g_sb[:], ALU.mult, [bqk, bg], [bqk])
                    ev = view(qk[:], [pst, [64, NQK], [2, 32]])
                    od = view(qk[:, 1:2], [pst, [64, NQK], [2, 32]])
                    cosv = view(cs_sb[:, t, 0, :], [list(cs_sb[:].ap[0]), [0, NQK], [1, 32]])
                    sinv = view(cs_sb[:, t, 1, :], [list(cs_sb[:].ap[0]), [0, NQK], [1, 32]])
                    rv = [x[:].rearrange("p (h d) -> p h d", d=32) for x in (r1, r2, r3, r4)]
                    P.tt("dve", rv[0], ev, cosv, ALU.mult, [bqk, bcs], [br[0]])
                    P.tt("dve", rv[1], od, sinv, ALU.mult, [bqk, bcs], [br[1]])
                    P.tt("dve", rv[2], ev, sinv, ALU.mult, [bqk, bcs], [br[2]])
                    P.tt("dve", rv[3], od, cosv, ALU.mult, [bqk, bcs], [br[3]])
                    pb = list(qkb[:].ap[0])
                    evo = view(qkb[:], [pb, [64, NQK], [2, 32]])
                    odo = view(qkb[:, 1:2], [pb, [64, NQK], [2, 32]])
                    P.tt("dve", evo, rv[0], rv[1], ALU.subtract, [br[0], br[1]], [bqkb])
                    P.tt("dve", odo, rv[2], rv[3], ALU.add, [br[2], br[3]], [bqkb])
                if not diff:
                    kv_ = qkb[:, 1024:1280].rearrange("p (h d) -> p h d", d=64)
                    P.copy("act", kdup[:, :, 0, :], kv_, [bqkb], [bkd])
                    P.copy("act", kdup[:, :, 1, :], kv_, [bqkb], [bkd])
                for j in range(8):
                    P.tr(ps_qT[:, j * 128:(j + 1) * 128], qkb[:, j * 128:(j + 1) * 128], ident[:], [bqkb, bid], [bpqT])
                P.copy("dve", q_s[:], ps_qT[:].rearrange("p (j t) -> p j t", t=128), [bpqT], [bqT])
                for g in range(NKC):
                    src = qkb[:, 1024 + g * 128:1024 + (g + 1) * 128] if diff else \
                        kdup[:, g, :, :].rearrange("p a d -> p (a d)")
                    P.tr(ps_kT[:, g * 128:(g + 1) * 128], src, ident[:], [bqkb if diff else bkd, bid], [bpkT])
                P.copy("act", k_s[:], ps_kT[:, 0:NKC * 128].rearrange("p (j t) -> p j t", t=128),
                       [bpkT], [bkT])
                bo = P.buf("out")
                P.dma("sp", qT_o[:, :, tc], q_s[:], [bqT], [bo])
                P.dma("sp", kT_o[:, :, tc], k_s[:], [bkT], [bo])
                P.dma("sp", vP_tiles[t], v_s[:], [bvP], [bo])
                outs.append(bo)
        P.wait_all("sp", outs)
    return None


def build_att(P, T, mode, need_ctx):
    nc = T
    NH, NKV = 16, 4
    NKT = 66 if mode == "dense" else 20
    xT = din(nc, "xT", [128, KD, TT])
    modT = din(nc, "modT", [128, 48, 2])
    ng = din(nc, "ng", [128, KD])
    qT = din(nc, "qT", [128, KD, TT], BF16)
    kloc = din(nc, "kloc", [NKV, 128, TT], BF16).rearrange("g p t -> p g t")
    vloc5 = din(nc, "vloc", [2, 128, 9, NKV, 65], BF16)
    kall = din(nc, "kall", [NKV, 4 * 128, TT], BF16)
    vall5 = din(nc, "vall", [2, 4 * 128, 9, NKV, 65], BF16)
    wo = din(nc, "wo", [64, NH, D])
    if mode == "win":
        masks_d = din(nc, "masks", [128, 6, 512], BF16)
        sink_d = din(nc, "sink", [1, NH])
        selv = din(nc, "selv", [128, 2, 4])
    xo = dout(nc, "xo", [128, KD, TT])
    with contextlib.nullcontext():
        P.phase()
        M = load_mod(P, modT, ng, 0)
        kT_sb = P.sbuf("kT_sb", [128, NKV, NKT * 128], BF16)
        vP_sb = P.sbuf("vP_sb", [128, NKT, NKV, 65], BF16)
        wo_sb = P.sbuf("wo_sb", [64, NH, D], BF16)
        qg = [P.sbuf(f"qg{i}", [128, KD, 512], BF16) for i in range(2)]
        xg = [P.sbuf("xg0", [128, KD, 512], F32)] * 2
        ao = P.sbuf("ao", [64, NH, 512], BF16)
        NP_ = 2 * PIPE + 4
        pt = [P.sbuf(f"pt{i}", [128, 512], BF16) for i in range(NP_)]
        o_sb = P.sbuf("o_sb", [65, 512], F32)
        onesf = P.sbuf("onesf", [65, 64], F32)
        NS = 4
        ps_s = [P.psum(f"ps_s{i}", [128, 512], F32) for i in range(NS)]
        ps_o = [P.psum(f"ps_o{i}", [128, 512], F32) for i in range(2)]
        ps_rb = P.psum("ps_rb", [128, 512], F32)
        ps_y = [P.psum("ps_y0", [128, 512], F32)] * 2
        bk, bv, bwo, bones = P.buf("k"), P.buf("v"), P.buf("wo"), P.buf("ones")
        bqg = P.bufs(2, "qg")
        bxg = [P.buf("xg")] * 2
        bao, bosb = P.buf("ao"), P.buf("osb")
        bpt = P.bufs(NP_, "pt")
        bps, bpo, bprb, bpy = P.bufs(NS, "pss"), P.bufs(2, "pso"), P.buf("psrb"), [P.buf("psy")] * 2
        kall4 = kall.rearrange("g (r p) t -> r p g t", p=128)

        def vcopy(dst0, r, t0, t1):
            for hf in range(2):
                lo, hi = max(t0, 9 * hf), min(t1, 9 * hf + 9)
                if lo < hi:
                    src = vloc5[hf] if r is None else vall5[hf, r * 128:(r + 1) * 128]
                    P.dma("sp", vP_sb[:, dst0 + lo - t0:dst0 + hi - t0], src[:, lo - 9 * hf:hi - 9 * hf], (), [bv])

        if mode == "dense":
            P.dma("sp", kT_sb[:, :, 0:TC], kloc[:, :, TL:TT], (), [bk])
            vcopy(0, None, 16, 18)
            for r in range(4):
                for g in range(NKV):
                    P.dma("sp", kT_sb[:, g, TC + r * TL:TC + (r + 1) * TL], kall4[r, :, g, 0:TL], (), [bk])
                vcopy(2 + 16 * r, r, 0, 16)
        else:
            sv = P.sbuf("sv", [128, 2, 4], F32)
            kc_ = P.sbuf("kc_", [128, 2, 4, NKV, 128], BF16)
            vc_ = P.sbuf("vc_", [128, 2, 4, NKV, 65], BF16)
            bsv, bkc = P.buf("sv"), P.buf("kc")
            P.dma("sp", sv[:], selv, (), [bsv])
            P.dma("sp", kT_sb[:, :, 0:TC], kloc[:, :, TL:TT], (), [bk])
            P.dma("sp", kT_sb[:, :, 3 * 128:3 * 128 + TL], kloc[:, :, 0:TL], (), [bk])
            vcopy(0, None, 16, 18)
            vcopy(3, None, 0, 16)
            for r in range(4):
                P.dma("sp", kc_[:, 0, r], kall4[r, :, :, TL - 128:TL], (), [bkc])
                P.dma("sp", kc_[:, 1, r], kall4[r, :, :, 0:128], (), [bkc])
                P.dma("sp", vc_[:, 0, r], vall5[1, r * 128:(r + 1) * 128, 6], (), [bkc])
                P.dma("sp", vc_[:, 1, r], vall5[0, r * 128:(r + 1) * 128, 0], (), [bkc])
            for side, (kslot, vslot) in enumerate(((2 * 128, 2), (19 * 128, 19))):
                kd_ = kT_sb[:, :, kslot:kslot + 128]
                vd_ = vP_sb[:, vslot]
                P.ts("dve", kd_, kc_[:, side, 0], sv[:, side, 0:1], ALU.mult, [bkc, bsv], [bk])
                P.ts("dve", vd_, vc_[:, side, 0], sv[:, side, 0:1], ALU.mult, [bkc, bsv], [bv])
                for r in range(1, 4):
                    P.stt("dve", kd_, kc_[:, side, r], sv[:, side, r:r + 1], kd_, ALU.mult, ALU.add, [bkc, bsv, bk], [bk])
                    P.stt("dve", vd_, vc_[:, side, r], sv[:, side, r:r + 1], vd_, ALU.mult, ALU.add, [bkc, bsv, bv], [bv])
        for h in range(0, NH, 4):
            P.dma("pool", wo_sb[:, h:h + 4, :], wo[:, h:h + 4, :], (), [bwo])
        P.memset("dve", onesf[:], 1.0, [bones])
        if mode == "win":
            mk = P.sbuf("mk", [128, 6, 512], BF16)
            snk = P.sbuf("snk", [65, NH], F32)
            bmk, bsn = P.buf("mk"), P.buf("snk")
            P.dma("sp", mk[:], masks_d, (), [bmk])
            P.dma("sp", snk[64:65, :], sink_d, (), [bsn])
            P.act(snk[64:65, :], snk[64:65, :], AF.Exp, [bsn], [bsn])
        groups = GROUPS if need_ctx else GROUPS[:4]
        si = 0
        oi = 0
        yi = 0
        outs = []
        P.dma("sp", qg[0][:, :, 0:groups[0][1]], qT[:, :, 0:groups[0][1]], (), [bqg[0]])
        for gi, (c0, n, isctx) in enumerate(groups):
            q_t, x_t = qg[gi % 2], xg[gi % 2]
            bq, bx = bqg[gi % 2], bxg[gi % 2]
            if gi + 1 < len(groups):
                c1, n1, _ = groups[gi + 1]
                P.dma("sp", qg[(gi + 1) % 2][:, :, 0:n1], qT[:, :, c1:c1 + n1], (), [bqg[(gi + 1) % 2]])
            P.dma("sp", x_t[:, :, 0:n], xT[:, :, c0:c0 + n], (), [bx])
            if isctx:
                ktl = [(0, None), (1, None)]
            elif mode == "dense":
                ktl = [(kt, None) for kt in range(NKT)]
            else:
                ktl = [(0, None), (1, None)] + [(2 + 4 * gi + r, r) for r in range(6)]
            units = [(2 * j + hp, ki, kt, mr) for j in range(NH // 2) for ki, (kt, mr) in enumerate(ktl)
                     for hp in range(2)]
            pend = []

            def stage1(u):
                h, ki, kt, mr = u
                j, hp, g = h // 2, h % 2, h // 4
                i_ = stage1.si
                stage1.si += 1
                s_ps, bs_ = ps_s[i_ % NS], bps[i_ % NS]
                p_t, bp_ = pt[i_ % NP_], bpt[i_ % NP_]
                P.mm(s_ps[:, 0:n], kT_sb[hp * 64:(hp + 1) * 64, g, kt * 128:(kt + 1) * 128],
                     q_t[hp * 64:(hp + 1) * 64, j, 0:n], True, True, [bk, bq], [bs_])
                P.act(p_t[:, 0:n], s_ps[:, 0:n], AF.Exp, [bs_], [bp_], scale=0.125)
                if mr is not None:
                    P.tt("dve", p_t[:, 0:n], p_t[:, 0:n], mk[:, mr, 0:n], ALU.mult, [bp_, bmk], [bp_])
                return p_t, bp_

            stage1.si = si

            def stage2(u, p_t, bp_):
                h, ki, kt, mr = u
                g = h // 4
                po, bo_ = ps_o[h % 2], bpo[h % 2]
                P.mm(po[0:65, 0:n], vP_sb[:, kt, g, :], p_t[:, 0:n], ki == 0, ki == len(ktl) - 1, [bv, bp_], [bo_])
                if ki != len(ktl) - 1:
                    return
                P.copy("act", o_sb[:, 0:n], po[0:65, 0:n], [bo_], [bosb])
                if mode == "win" and not isctx:
                    P.ts("dve", o_sb[64:65, 0:n], o_sb[64:65, 0:n], snk[64:65, h:h + 1], ALU.add, [bosb, bsn], [bosb])
                P.recip(o_sb[64:65, 0:n], o_sb[64:65, 0:n], [bosb], [bosb])
                P.mm(ps_rb[0:64, 0:n], onesf[64:65, :], o_sb[64:65, 0:n], True, True, [bones, bosb], [bprb])
                P.tt("dve", ao[:, h, 0:n], o_sb[0:64, 0:n], ps_rb[0:64, 0:n], ALU.mult, [bosb, bprb], [bao])

            PP = 2 * PIPE
            for idx in range(0, len(units) + PP, 2):
                for i2 in (idx, idx + 1):
                    if i2 < len(units):
                        pend.append(stage1(units[i2]))
                for i2 in (idx - PP, idx - PP + 1):
                    if 0 <= i2 < len(units):
                        stage2(units[i2], *pend[i2])
            si = stage1.si
            for oc in range(KD):
                y_ps, by_ = ps_y[yi % 2], bpy[yi % 2]
                yi += 1
                for h in range(NH):
                    P.mm(y_ps[:, 0:n], wo_sb[:, h, oc * 128:(oc + 1) * 128], ao[:, h, 0:n], h == 0, h == NH - 1,
                         [bwo, bao], [by_])
                P.stt("dve", x_t[:, oc, 0:n], y_ps[:, 0:n], M["g"][isctx][:, oc:oc + 1], x_t[:, oc, 0:n],
                      ALU.mult, ALU.add, [by_, M["buf"], bx], [bx])
            bo = P.buf("out")
            P.dma("sp", xo[:, :, c0:c0 + n], x_t[:, :, 0:n], [bx], [bo])
            outs.append(bo)
        if not need_ctx:
            x_t, bx = xg[0], bxg[0]
            bo = P.buf("outc")
            P.dma("sp", x_t[:, :, 0:TC], xT[:, :, TL:TT], (), [bx])
            P.dma("sp", xo[:, :, TL:TT], x_t[:, :, 0:TC], [bx], [bo])
            outs.append(bo)
        P.wait_all("sp", outs)
    return None


def build_mlp(P, T, need_ctx):
    nc = T
    xT = din(nc, "xT", [128, KD, TT])
    modT = din(nc, "modT", [128, 48, 2])
    ng = din(nc, "ng", [128, KD])
    wu = din(nc, "wu", [128, KD, DFF])
    wd = din(nc, "wd", [128, 32, D])
    xo = dout(nc, "xo", [128, KD, TT])
    with contextlib.nullcontext():
        P.phase()
        C = NormCtx(P)
        M = load_mod(P, modT, ng, 1)
        wu_sb = P.sbuf("wu_sb", [128, KD, DFF], BF16)
        wd_sb = P.sbuf("wd_sb", [128, 32, D], BF16)
        N = 256
        xg = [P.sbuf(f"xg{i}", [128, KD, N], F32) for i in range(2)]
        hg = P.sbuf("hg", [128, KD, N], BF16)
        aT = P.sbuf("aT", [128, 32, N], BF16)
        rl = [P.sbuf(f"rl{i}", [128, N], F32) for i in range(3)]
        ps_u = [P.psum(f"ps_u{i}", [128, 512], F32) for i in range(3)]
        ps_d = [P.psum(f"ps_d{i}", [128, 512], F32) for i in range(2)]
        bwu, bwd = P.bufs(KD, "wu"), P.bufs(8, "wd")
        bxg, bhg, baT = P.bufs(2, "xg"), P.buf("hg"), P.bufs(32, "aT")
        brl, bpu, bpd = P.bufs(3, "rl"), P.bufs(3, "psu"), P.bufs(2, "psd")
        for c in range(8):
            P.dma("pool", wu_sb[:, :, c * 512:(c + 1) * 512], wu[:, :, c * 512:(c + 1) * 512], (), [bwu[c]])
        for c in range(8):
            P.dma("pool", wd_sb[:, c * 4:(c + 1) * 4, :], wd[:, c * 4:(c + 1) * 4, :], (), [bwd[c]])
        ncol = TT if need_ctx else TL
        ui = 0
        di = 0
        outs = []
        cols = list(range(0, ncol, N))

        def load_norm(gi):
            c0_ = cols[gi]
            ic = 1 if c0_ >= TL else 0
            P.dma("sp", xg[gi % 2][:], xT[:, :, c0_:c0_ + N], (), [bxg[gi % 2]])
            norm_mod(P, C, xg[gi % 2][:], N, M["a"][ic], M["b"][ic], M["buf"], hg[:], bxg[gi % 2], bhg)

        load_norm(0)
        for gi, c0 in enumerate(cols):
            isctx = 1 if c0 >= TL else 0
            x_t, bx = xg[gi % 2], bxg[gi % 2]
            for fc in range(32):
                u_ps, bu_ = ps_u[ui % 3], bpu[ui % 3]
                r_t, br_ = rl[ui % 3], brl[ui % 3]
                ui += 1
                for k in range(KD):
                    P.mm(u_ps[:, 0:N], wu_sb[:, k, fc * 128:(fc + 1) * 128], hg[:, k, :], k == 0, k == KD - 1,
                         [bwu[fc // 4], bhg], [bu_])
                P.act(r_t[:], u_ps[:, 0:N], AF.Relu, [bu_], [br_])
                P.tt("dve" if fc % 2 == 0 else "pool", aT[:, fc, :], r_t[:], r_t[:], ALU.mult, [br_], [baT[fc]])
            if gi + 1 < len(cols):
                load_norm(gi + 1)
            for oc in range(KD):
                d_ps, bd_ = ps_d[di % 2], bpd[di % 2]
                di += 1
                for fc in range(32):
                    P.mm(d_ps[:, 0:N], wd_sb[:, fc, oc * 128:(oc + 1) * 128], aT[:, fc, :], fc == 0, fc == 31,
                         [bwd[fc // 4], baT[fc]], [bd_])
                P.stt("dve", x_t[:, oc, :], d_ps[:, 0:N], M["g"][isctx][:, oc:oc + 1], x_t[:, oc, :],
                      ALU.mult, ALU.add, [bd_, M["buf"], bx], [bx])
            bo = P.buf("out")
            P.dma("sp", xo[:, :, c0:c0 + N], x_t[:], [bx], [bo])
            outs.append(bo)
        if not need_ctx:
            x_t, bx = xg[0], bxg[0]
            bo = P.buf("outc")
            P.dma("sp", x_t[:], xT[:, :, TL:TT], (), [bx])
            P.dma("sp", xo[:, :, TL:TT], x_t[:], [bx], [bo])
            outs.append(bo)
        P.wait_all("sp", outs)
    return None


_PROGS = {}


def prog(name, builder, *args):
    key = (name,) + args
    if key not in _PROGS:
        _PROGS[key] = builder(*args)
    return _PROGS[key]


def run(nc, in_maps):
    res = run_bass_kernel_spmd(nc, in_maps, core_ids=list(range(NCORES)))
    return res.results


def fm(a):
    T, F = a.shape
    return np.ascontiguousarray(a.T.reshape(F // 128, 128, T).transpose(1, 0, 2))


def fm_inv(a):
    p, k, t = a.shape
    return np.ascontiguousarray(a.transpose(1, 0, 2).reshape(k * p, t).T)


def wl(w):
    fin, o = w.shape
    return np.ascontiguousarray(w.reshape(fin // 128, 128, o).transpose(1, 0, 2))


def colv(v):
    return np.ascontiguousarray(v.reshape(-1, 128).T)


def rope_tables():
    rows = SEQ // 64
    row = np.repeat(np.arange(rows, dtype=np.float32), 64)
    col = np.tile(np.arange(64, dtype=np.float32), rows)
    half = 32
    inv = (1.0 / np.power(np.float32(10000.0), np.arange(0, half, 2, dtype=np.float32) / half)).astype(np.float32)
    ang = np.concatenate([row[:, None] * inv, col[:, None] * inv], axis=-1).astype(np.float32)
    return np.cos(ang).astype(np.float32), np.sin(ang).astype(np.float32)


def cs_for_core(cos, sin, q):
    c = cos[q * TL:(q + 1) * TL].reshape(16, 128, 32)
    s = sin[q * TL:(q + 1) * TL].reshape(16, 128, 32)
    return np.ascontiguousarray(np.stack([c, s], axis=2).transpose(1, 0, 2, 3))


def run_mod(c, c_ctx, mod_w, mod_b):
    nc = prog("mod", build_mod)
    cT = np.ascontiguousarray(np.stack([colv(c[0]), colv(c[1]), colv(c_ctx)], axis=-1))
    maps = []
    for core in range(NCORES):
        cols = slice(core * 768, (core + 1) * 768)
        mw = np.ascontiguousarray(mod_w[:, :, cols].reshape(DEPTH, KD, 128, 768).transpose(0, 2, 1, 3))
        mb = np.ascontiguousarray(mod_b[:, cols].reshape(DEPTH, 6, 128).transpose(0, 2, 1))
        maps.append({"cT": cT, "mod_w": mw, "mod_b": mb})
    res = run(nc, maps)
    full = np.concatenate([r["modT"] for r in res], axis=2)
    return [np.ascontiguousarray(full[:, :, :, [b, 2]]) for b in range(2)]


def gather_kv_dense(kTs, vPs):
    out = []
    for b in range(2):
        cores = [b * 4 + q for q in range(4)]
        kT = np.concatenate([kTs[cores[0]][:, :, TL:TT]] + [kTs[c][:, :, 0:TL] for c in cores], axis=2)
        vP = np.concatenate([vPs[cores[0]][:, 16:18]] + [vPs[c][:, 0:16] for c in cores], axis=1)
        out.append((np.ascontiguousarray(kT), np.ascontiguousarray(vP)))
    return out


def layer_gqa_dense(i, xs, mods, inp, need_ctx, cos, sin):
    ident = np.eye(128, dtype=np.float32).astype(NPBF)
    wqkv = wl(inp["gqa_w_qkv"][0])
    gvec = np.concatenate([np.tile(inp["gqa_q_g"][0], 16), np.tile(inp["gqa_k_g"][0], 4)]).astype(np.float32)
    gvec = np.ascontiguousarray(np.broadcast_to(gvec[None, :], (128, gvec.size)))
    ng1 = colv(inp["norm1_g"][i])
    nc = prog("pre_gqa", build_pre_gqa, "gqa")
    maps = [{"xT": xs[c], "modT": mods[c // 4][i], "ng": ng1, "wqkv": wqkv, "gvec": gvec,
             "cs": cs_for_core(cos, sin, c % 4), "ident": ident} for c in range(NCORES)]
    res = run(nc, maps)
    kv = gather_kv_dense([r["kT"] for r in res], [r["vP"] for r in res])
    wo = np.ascontiguousarray(inp["gqa_w_o"][0].reshape(16, 64, D).transpose(1, 0, 2))
    nc = prog("att", build_att, "dense", need_ctx)
    maps = [{"xT": xs[c], "modT": mods[c // 4][i], "ng": ng1, "qT": res[c]["qT"], "kT": kv[c // 4][0],
             "vP": kv[c // 4][1], "wo": wo} for c in range(NCORES)]
    res2 = run(nc, maps)
    return [r["xo"] for r in res2]


def layer_mlp(i, xs, mods, inp, need_ctx):
    nc = prog("mlp", build_mlp, need_ctx)
    wu = wl(inp["mlp_up"][i])
    wd = wl(inp["mlp_down"][i])
    ng2 = colv(inp["norm2_g"][i])
    maps = [{"xT": xs[c], "modT": mods[c // 4][i], "ng": ng2, "wu": wu, "wd": wd} for c in range(NCORES)]
    res = run(nc, maps)
    return [r["xo"] for r in res]


def shard_x(x, ctx):
    xs = []
    for c in range(NCORES):
        b, q = c // 4, c % 4
        xs.append(fm(np.concatenate([x[b, q * TL:(q + 1) * TL], ctx[b]], axis=0)))
    return xs


def unshard_x(xs):
    out = np.empty((2, SEQ, D), np.float32)
    for c in range(NCORES):
        b, q = c // 4, c % 4
        out[b, q * TL:(q + 1) * TL] = fm_inv(xs[c])[0:TL]
    return out


def build_att_diff(P, T, lam_init):
    nc = T
    NHD, NKT = 8, 66
    xT = din(nc, "xT", [128, KD, TT])
    modT = din(nc, "modT", [128, 48, 2])
    ng = din(nc, "ng", [128, KD])
    qT = din(nc, "qT", [128, KD, TT], BF16)
    kloc = din(nc, "kloc", [NHD, 128, TT], BF16).rearrange("g p t -> p g t")
    vloc = din(nc, "vloc", [NHD, 128, 18, 128], BF16).rearrange("h p t d -> p h t d")
    kall = din(nc, "kall", [NHD, 4 * 128, TT], BF16)
    vall = din(nc, "vall", [NHD, 4 * 128, 18, 128], BF16)
    kall4 = kall.rearrange("g (r p) t -> r p g t", p=128)
    vall4 = vall.rearrange("h (r p) t d -> r p h t d", p=128)
    wo = din(nc, "wo", [128, NHD, D])
    lamv = din(nc, "lamv", [1, 4, 64])
    slg = din(nc, "slg", [128, 1])
    xo = dout(nc, "xo", [128, KD, TT])
    with contextlib.nullcontext():
        P.phase()
        M = load_mod(P, modT, ng, 0)
        qT_sb = P.sbuf("qT_sb", [128, KD, TT], BF16)
        aoT = P.sbuf("aoT", [128, NHD, TT], BF16)
        kh = [P.sbuf(f"kh{i}", [128, NKT * 128], BF16) for i in range(2)]
        vh = [P.sbuf(f"vh{i}", [128, NKT, 128], BF16) for i in range(2)]
        wo_sb = P.sbuf("wo_sb", [128, NHD, D], BF16)
        xg = P.sbuf("xg", [128, KD, 512], F32)
        NP_ = 2 * PIPE + 4
        pt = [P.sbuf(f"pt{i}", [128, 512], BF16) for i in range(NP_)]
        o_sb = [P.sbuf(f"o_sb{i}", [128, 512], F32) for i in range(2)]
        l_sb = P.sbuf("l_sb", [2, 512], F32)
        accl = [P.sbuf(f"accl{i}", [128, 512], F32) for i in range(2)]
        accb = [P.sbuf(f"accb{i}", [128, 512], BF16) for i in range(2)]
        baccl, baccb = P.bufs(2, "accl"), P.bufs(2, "accb")
        od = P.sbuf("od", [128, 512], F32)
        sqb = P.sbuf("sqb", [128, 512], BF16)
        rs = P.sbuf("rs", [128, 512], F32)
        onesb = P.sbuf("onesb", [128, 128], BF16)
        sel = P.sbuf("sel", [128, 2, 2], BF16)
        self_f = P.sbuf("self_f", [2, 2, 128], F32)
        lam_sb = P.sbuf("lam_sb", [1, 4, 64], F32)
        lam_t = P.sbuf("lam_t", [1, 8], F32)
        sg = P.sbuf("sg", [128, 1], F32)
        ps_s = [P.psum(f"ps_s{i}", [128, 512], F32) for i in range(4)]
        ps_o = [P.psum(f"ps_o{i}", [128, 512], F32) for i in range(2)]
        ps_l = P.psum("ps_l", [128, 512], F32)
        ps_rb = [P.psum("ps_rb0", [128, 512], F32)] * 2
        ps_y = ps_rb[0]
        bq, bao, bwo, bx = P.buf("q"), P.buf("ao"), P.buf("wo"), P.buf("x")
        bkh, bvh = P.bufs(2, "kh"), P.bufs(2, "vh")
        bpt, bosb = P.bufs(NP_, "pt"), P.bufs(2, "osb")
        blsb, bod, bsqb, brs, bcst, blam = P.buf("lsb"), P.buf("od"), P.buf("sqb"), P.buf("rs"), P.buf("cst"), P.buf("lam")
        bps, bpo, bpl = P.bufs(4, "pss"), P.bufs(2, "pso"), P.buf("psl")
        bprb = [P.buf("psrb")] * 2
        bpy = bprb[0]
        for k in range(KD):
            P.dma("sp", qT_sb[:, k, :], qT[:, k, :], (), [bq])
        for h in range(0, NHD, 2):
            P.dma("pool", wo_sb[:, h:h + 2, :], wo[:, h:h + 2, :], (), [bwo])
        P.dma("sp", lam_sb[:], lamv, (), [blam])
        P.dma("sp", sg[:], slg, (), [bcst])
        P.memset("dve", onesb[:], 1.0 / 128.0, [bcst])
        P.memset("dve", sel[:], 0.0, [bcst])
        P.memset("dve", sel[:, 0, 0:1], 1.0, [bcst])
        P.memset("dve", sel[:, 1, 1:2], 1.0, [bcst])
        P.memset("dve", self_f[:], 0.0, [bcst])
        P.memset("dve", self_f[0:1, 0, :], 1.0, [bcst])
        P.memset("dve", self_f[0:2, 1, :], 1.0, [bcst])
        P.memset("dve", self_f[0:1, 1, :], 0.0, [bcst])
        P.ts("dve", sg[:], sg[:], 1.0 - lam_init, ALU.mult, [bcst], [bcst])
        P.tt("dve", lam_sb[:, 0, :], lam_sb[:, 0, :], lam_sb[:, 1, :], ALU.mult, [blam], [blam])
        P.tt("dve", lam_sb[:, 2, :], lam_sb[:, 2, :], lam_sb[:, 3, :], ALU.mult, [blam], [blam])
        P.reduce(lam_t[:, 0:1], lam_sb[:, 0, :], ALU.add, [blam], [blam])
        P.reduce(lam_t[:, 1:2], lam_sb[:, 2, :], ALU.add, [blam], [blam])
        P.act(lam_t[:, 2:4], lam_t[:, 0:2], AF.Exp, [blam], [blam])
        P.tt("dve", lam_t[:, 4:5], lam_t[:, 3:4], lam_t[:, 2:3], ALU.subtract, [blam], [blam])
        P.ts("dve", lam_t[:, 5:6], lam_t[:, 4:5], -lam_init, ALU.add, [blam], [blam])
        nlam = P.sbuf("nlam", [128, 1], F32)
        bnl = P.buf("nlam")
        P.mm(ps_y[:, 0:1], self_f[0:1, 0, :], lam_t[0:1, 5:6], True, True, [bcst, blam], [bpy])
        P.copy("dve", nlam[:], ps_y[:, 0:1], [bpy], [bnl])
        si = 0
        for h in range(NHD):
            k_t, v_t = kh[h % 2], vh[h % 2]
            bk, bv = bkh[h % 2], bvh[h % 2]
            P.dma("sp", k_t[:, 0:TC], kloc[:, h, TL:TT], (), [bk])
            P.dma("sp", v_t[:, 0:2, :], vloc[:, h, 16:18, :], (), [bv])
            for r in range(4):
                P.dma("sp", k_t[:, TC + r * TL:TC + (r + 1) * TL], kall4[r, :, h, 0:TL], (), [bk])
                P.dma("sp", v_t[:, 2 + 16 * r:2 + 16 * (r + 1), :], vall4[r, :, h, 0:16, :], (), [bv])
            for gi, (c0, n, isctx) in enumerate(GROUPS):
                ktl = [0, 1] if isctx else list(range(NKT))
                units = [(c, ki, kt) for ki, kt in enumerate(ktl) for c in range(2)]
                pend = []

                def stage1(u):
                    c, ki, kt = u
                    i_ = stage1.si
                    stage1.si += 1
                    s_ps, bs_ = ps_s[i_ % 4], bps[i_ % 4]
                    p_t, bp_ = pt[i_ % NP_], bpt[i_ % NP_]
                    P.mm(s_ps[:, 0:n], k_t[c * 64:(c + 1) * 64, kt * 128:(kt + 1) * 128],
                         qT_sb[c * 64:(c + 1) * 64, h, c0:c0 + n], True, True, [bk, bq], [bs_])
                    P.act(p_t[:, 0:n], s_ps[:, 0:n], AF.Exp, [bs_], [bp_], scale=0.125)
                    return p_t, bp_

                stage1.si = si

                def stage2(u, p_t, bp_):
                    c, ki, kt = u
                    po, bo_ = ps_o[c], bpo[c]
                    P.mm(po[:, 0:n], v_t[:, kt, :], p_t[:, 0:n], ki == 0, ki == len(ktl) - 1, [bv, bp_], [bo_])
                    if ki % 4 == 0:
                        P.mm(ps_l[0:2, 0:n], sel[:, c, :], p_t[:, 0:n], c == 0 and ki == 0, False, [bcst, bp_], [bpl])
                    elif ki == 1:
                        P.copy("dve", accl[c][:, 0:n], p_t[:, 0:n], [bp_], [baccl[c]])
                    else:
                        P.tt("dve", accl[c][:, 0:n], accl[c][:, 0:n], p_t[:, 0:n], ALU.add, [bp_, baccl[c]], [baccl[c]])
                    if ki == len(ktl) - 1:
                        P.copy("dve", accb[c][:, 0:n], accl[c][:, 0:n], [baccl[c]], [baccb[c]])
                        P.mm(ps_l[0:2, 0:n], sel[:, c, :], accb[c][:, 0:n], False, c == 1, [bcst, baccb[c]], [bpl])
                        P.copy("act", o_sb[c][:, 0:n], po[:, 0:n], [bo_], [bosb[c]])

                PP = 2 * PIPE
                for idx in range(0, len(units) + PP, 2):
                    for i2 in (idx, idx + 1):
                        if i2 < len(units):
                            pend.append(stage1(units[i2]))
                    for i2 in (idx - PP, idx - PP + 1):
                        if 0 <= i2 < len(units):
                            stage2(units[i2], *pend[i2])
                si = stage1.si
                P.copy("act", l_sb[:, 0:n], ps_l[0:2, 0:n], [bpl], [blsb])
                P.recip(l_sb[:, 0:n], l_sb[:, 0:n], [blsb], [blsb])
                for c in range(2):
                    P.mm(ps_rb[c][:, 0:n], self_f[:, c, :], l_sb[:, 0:n], True, True, [bcst, blsb], [bprb[c]])
                    P.tt("dve", o_sb[c][:, 0:n], o_sb[c][:, 0:n], ps_rb[c][:, 0:n], ALU.mult, [bosb[c], bprb[c]],
                         [bosb[c]])
                P.stt("dve", od[:, 0:n], o_sb[1][:, 0:n], nlam[:, 0:1], o_sb[0][:, 0:n], ALU.mult, ALU.add,
                      [bosb[0], bosb[1], bnl], [bod])
                P.act(sqb[:, 0:n], od[:, 0:n], AF.Square, [bod], [bsqb])
                P.mm(ps_rb[0][:, 0:n], onesb[:], sqb[:, 0:n], True, True, [bcst, bsqb], [bprb[0]])
                P.act(rs[:, 0:n], ps_rb[0][:, 0:n], AF.Sqrt, [bprb[0]], [brs], bias=EPS, scale=1.0)
                P.recip(rs[:, 0:n], rs[:, 0:n], [brs], [brs])
                P.stt("dve", aoT[:, h, c0:c0 + n], od[:, 0:n], sg[:, 0:1], rs[:, 0:n], ALU.mult, ALU.mult,
                      [bod, bcst, brs], [bao])
        outs = []
        for gi, (c0, n, isctx) in enumerate(GROUPS):
            P.dma("sp", xg[:, :, 0:n], xT[:, :, c0:c0 + n], (), [bx])
            for oc in range(KD):
                for h in range(NHD):
                    P.mm(ps_y[:, 0:n], wo_sb[:, h, oc * 128:(oc + 1) * 128], aoT[:, h, c0:c0 + n], h == 0,
                         h == NHD - 1, [bwo, bao], [bpy])
                P.stt("dve", xg[:, oc, 0:n], ps_y[:, 0:n], M["g"][isctx][:, oc:oc + 1], xg[:, oc, 0:n],
                      ALU.mult, ALU.add, [bpy, M["buf"], bx], [bx])
            bo = P.buf("out")
            P.dma("sp", xo[:, :, c0:c0 + n], xg[:, :, 0:n], [bx], [bo])
            outs.append(bo)
        P.wait_all("sp", outs)
    return None


HALO = 15
TE = TL + 2 * HALO
TCE = TC + 2 * HALO


def build_conv(P, T):
    nc = T
    xT = din(nc, "xT", [128, KD, TE + TC])
    modT = din(nc, "modT", [128, 48, 2])
    ng = din(nc, "ng", [128, KD])
    w1 = din(nc, "w1", [128, KD, 2 * D])
    b1 = din(nc, "b1", [128, 16])
    wdw = din(nc, "wdw", [128, KD, 31])
    cvec = din(nc, "cvec", [128, 4, KD])
    w2 = din(nc, "w2", [128, KD, D])
    hv = din(nc, "hv", [128, 2])
    xo = dout(nc, "xo", [128, KD, TT])
    with contextlib.nullcontext():
        P.phase()
        C = NormCtx(P)
        M = load_mod(P, modT, ng, 0)
        w1_sb = P.sbuf("w1_sb", [128, KD, 2 * D], BF16)
        w2_sb = P.sbuf("w2_sb", [128, KD, D], BF16)
        b1_sb = P.sbuf("b1_sb", [128, 16], F32)
        wdw_sb = P.sbuf("wdw_sb", [128, KD, 31], F32)
        cv = P.sbuf("cv", [128, 4, KD], F32)
        gb = P.sbuf("gb", [128, 2, KD], F32)
        hv_sb = P.sbuf("hv_sb", [128, 2], F32)
        uT = P.sbuf("uT", [128, KD, TE], F32)
        uc = P.sbuf("uc", [128, KD, TCE], F32)
        xg = P.sbuf("xg", [128, KD, 512], F32)
        hg = P.sbuf("hg", [128, KD, 512], BF16)
        sg_ = [P.sbuf(f"sg{i}", [128, 512], F32) for i in range(2)]
        N2 = 256
        acc = P.sbuf("acc", [128, KD, N2], F32)
        ctmps = {oc: [P.sbuf(f"ctmp{oc}_{i}", [128, N2], F32) for i in range(2)] for oc in range(4, KD)}
        bcts = {oc: P.bufs(2, f"ctmp{oc}") for oc in range(4, KD)}
        jn = P.sbuf("jn", [128, 1], F32)
        baccs = P.bufs(KD, "accs")
        ps_a = [P.psum(f"ps_a{i}", [128, 512], F32) for i in range(2)]
        ps_g = [P.psum(f"ps_g{i}", [128, 512], F32) for i in range(2)]
        ps_m = P.psum("ps_m", [128, 512], F32)
        ps_y = [P.psum(f"ps_y{i}", [128, 512], F32) for i in range(2)]
        bw1, bw2, bcst, bhv = P.buf("w1"), P.buf("w2"), P.buf("cst"), P.buf("hv")
        bu, buc, bx, bhg, bacc = P.buf("u"), P.buf("uc"), P.buf("x"), P.buf("hg"), P.buf("acc")
        bsg, bpa, bpg, bpm, bpy = P.bufs(2, "sg"), P.bufs(2, "psa"), P.bufs(2, "psg"), P.buf("psm"), P.bufs(2, "psy")
        for k in range(KD):
            P.dma("pool", w1_sb[:, k, :], w1[:, k, :], (), [bw1])
        for k in range(0, KD, 2):
            P.dma("pool", w2_sb[:, k:k + 2, :], w2[:, k:k + 2, :], (), [bw2])
        P.dma("sp", b1_sb[:], b1, (), [bcst])
        P.dma("sp", wdw_sb[:], wdw, (), [bcst])
        P.dma("sp", cv[:], cvec, (), [bcst])
        P.dma("sp", hv_sb[:], hv, (), [bhv])
        for j in range(2):
            P.tt("dve", gb[:, j, :], M["g"][j], cv[:, 3, :], ALU.mult, [M["buf"], bcst], [bcst])
        P.memset("pool", uc[:], 0.0, [buc])
        groups1 = [(0, 512, 0), (512, 512, 0), (1024, 512, 0), (1536, 512, 0), (2048, 2 * HALO, 0), (TE, TC, 1)]
        ai = 0
        for (c0, n, isctx) in groups1:
            P.dma("sp", xg[:, :, 0:n], xT[:, :, c0:c0 + n], (), [bx])
            norm_mod(P, C, xg[:, :, 0:n], n, M["a"][isctx], M["b"][isctx], M["buf"], hg[:, :, 0:n], bx, bhg)
            for oc in range(KD):
                pa, pg = ps_a[ai % 2], ps_g[ai % 2]
                bpa_, bpg_ = bpa[ai % 2], bpg[ai % 2]
                s_t, bs_ = sg_[ai % 2], bsg[ai % 2]
                ai += 1
                for k in range(KD):
                    P.mm(pa[:, 0:n], w1_sb[:, k, oc * 128:(oc + 1) * 128], hg[:, k, 0:n], k == 0, k == KD - 1,
                         [bw1, bhg], [bpa_])
                for k in range(KD):
                    P.mm(pg[:, 0:n], w1_sb[:, k, D + oc * 128:D + (oc + 1) * 128], hg[:, k, 0:n], k == 0, k == KD - 1,
                         [bw1, bhg], [bpg_])
                P.act(s_t[:, 0:n], pg[:, 0:n], AF.Sigmoid, [bpg_, bcst], [bs_], bias=b1_sb[:, 8 + oc:9 + oc], scale=1.0)
                if isctx:
                    dst, bd = uc[:, oc, HALO:HALO + TC], buc
                else:
                    dst, bd = uT[:, oc, c0:c0 + n], bu
                P.stt("dve", dst, pa[:, 0:n], b1_sb[:, oc:oc + 1], s_t[:, 0:n], ALU.add, ALU.mult,
                      [bpa_, bcst, bs_], [bd])
        for oc in range(KD):
            P.ts("dve", uT[:, oc, 0:HALO], uT[:, oc, 0:HALO], hv_sb[:, 0:1], ALU.mult, [bu, bhv], [bu])
            P.ts("dve", uT[:, oc, HALO + TL:TE], uT[:, oc, HALO + TL:TE], hv_sb[:, 1:2], ALU.mult, [bu, bhv], [bu])
        outs = []
        yi = 0
        groups2 = [(c0, N2, 0) for c0 in range(0, TL, N2)] + [(0, TC, 1)]
        for gi, (c0, n, isctx) in enumerate(groups2):
            src, bsrc = (uc, buc) if isctx else (uT, bu)
            NDV = 4
            for oc in range(NDV):
                P.ts("dve", acc[:, oc, 0:n], src[:, oc, c0:c0 + n], wdw_sb[:, oc, 0:1], ALU.mult, [bsrc, bcst],
                     [baccs[oc], bacc])
            for oc in range(NDV, KD):
                P.act(acc[:, oc, 0:n], src[:, oc, c0:c0 + n], AF.Identity, [bsrc, bcst], [baccs[oc], bacc],
                      bias=cv[:, 0, oc:oc + 1], scale=wdw_sb[:, oc, 0:1])
                P.act(ctmps[oc][1][:, 0:n], src[:, oc, c0 + 1:c0 + 1 + n], AF.Identity, [bsrc, bcst], [bcts[oc][1]],
                      scale=wdw_sb[:, oc, 1:2])
            for k in range(1, 31):
                for oc in range(NDV):
                    P.stt("dve", acc[:, oc, 0:n], src[:, oc, c0 + k:c0 + k + n], wdw_sb[:, oc, k:k + 1],
                          acc[:, oc, 0:n], ALU.mult, ALU.add, [bsrc, bcst, baccs[oc]], [baccs[oc]])
                for oc in range(NDV, KD):
                    if k + 1 < 31:
                        P.act(ctmps[oc][(k + 1) % 2][:, 0:n], src[:, oc, c0 + k + 1:c0 + k + 1 + n], AF.Identity,
                              [bsrc, bcst], [bcts[oc][(k + 1) % 2]], scale=wdw_sb[:, oc, k + 1:k + 2])
                    P.tt("pool", acc[:, oc, 0:n], acc[:, oc, 0:n], ctmps[oc][k % 2][:, 0:n], ALU.add,
                         [bcts[oc][k % 2], baccs[oc]], [baccs[oc]])
            for oc in range(NDV):
                P.ts("dve", acc[:, oc, 0:n], acc[:, oc, 0:n], cv[:, 0, oc:oc + 1], ALU.add,
                     [baccs[oc], bcst], [baccs[oc]])
            P.op("dve", lambda e: e.memset(jn[:], 0.0), baccs, [bacc])
            P.act(C.sq[:, :, 0:n], acc[:, :, 0:n], AF.Copy, [bacc], [C.b_sq])
            for k in range(KD):
                P.mm(ps_m[:, 0:n], C.ones[:], C.sq[:, k, 0:n], k == 0, k == KD - 1, [C.b_ones, C.b_sq], [bpm])
            P.tt("dve", acc[:, :, 0:n], acc[:, :, 0:n],
                 view(ps_m[:, 0:n], [list(ps_m[:].ap[0]), [0, KD], [1, n]]), ALU.subtract, [bacc, bpm], [bacc])
            norm_mod(P, C, acc[:, :, 0:n], n, cv[:, 1, :], cv[:, 2, :], bcst, hg[:, :, 0:n], bacc, bhg, func=AF.Silu)
            xc0 = TE + c0 if isctx else HALO + c0
            P.dma("sp", xg[:, :, 0:n], xT[:, :, xc0:xc0 + n], (), [bx])
            for oc in range(KD):
                py, by_ = ps_y[yi % 2], bpy[yi % 2]
                yi += 1
                for k in range(KD):
                    P.mm(py[:, 0:n], w2_sb[:, k, oc * 128:(oc + 1) * 128], hg[:, k, 0:n], k == 0, k == KD - 1,
                         [bw2, bhg], [by_])
                P.stt("dve", xg[:, oc, 0:n], py[:, 0:n], M["g"][isctx][:, oc:oc + 1], xg[:, oc, 0:n],
                      ALU.mult, ALU.add, [by_, M["buf"], bx], [bx])
                P.ts("dve", xg[:, oc, 0:n], xg[:, oc, 0:n], gb[:, isctx, oc:oc + 1], ALU.add, [bx, bcst], [bx])
            bo = P.buf("out")
            oc0 = TL + c0 if isctx else c0
            P.dma("sp", xo[:, :, oc0:oc0 + n], xg[:, :, 0:n], [bx], [bo])
            outs.append(bo)
        P.wait_all("sp", outs)
    return None


def win_masks():
    kk = np.arange(128)[:, None, None]
    r = np.arange(6)[None, :, None]
    qq = np.arange(512)[None, None, :]
    return (np.abs(128 * (r - 1) + kk - qq) <= 128).astype(np.float32).astype(NPBF)


def gather_kv_win(kTs, vPs):
    out = []
    for c in range(NCORES):
        q = c % 4
        zk = np.zeros_like(kTs[c][:, :, 0:128])
        zv = np.zeros_like(vPs[c][:, 0:1])
        kprev = kTs[c - 1][:, :, TL - 128:TL] if q > 0 else zk
        knext = kTs[c + 1][:, :, 0:128] if q < 3 else zk
        vprev = vPs[c - 1][:, 15:16] if q > 0 else zv
        vnext = vPs[c + 1][:, 0:1] if q < 3 else zv
        kT = np.concatenate([kTs[c][:, :, TL:TT], kprev, kTs[c][:, :, 0:TL], knext], axis=2)
        vP = np.concatenate([vPs[c][:, 16:18], vprev, vPs[c][:, 0:16], vnext], axis=1)
        out.append((np.ascontiguousarray(kT), np.ascontiguousarray(vP)))
    return out


def pre_inputs(i, xs, mods, wqkv, qg, kg, nq, nk, cos, sin, norm1_g):
    ident = np.eye(128, dtype=np.float32).astype(NPBF)
    gvec = np.concatenate([np.tile(qg, nq), np.tile(kg, nk)]).astype(np.float32)
    gvec = np.ascontiguousarray(np.broadcast_to(gvec[None, :], (128, gvec.size)))
    ng1 = colv(norm1_g)
    w = wl(wqkv)
    return [{"xT": xs[c], "modT": mods[c // 4][i], "ng": ng1, "wqkv": w, "gvec": gvec,
             "cs": cs_for_core(cos, sin, c % 4), "ident": ident} for c in range(NCORES)], ng1


def layer_win(i, xs, mods, inp, cos, sin):
    maps, ng1 = pre_inputs(i, xs, mods, inp["swa_w_qkv"][0], inp["swa_q_g"][0], inp["swa_k_g"][0], 16, 4,
                           cos, sin, inp["norm1_g"][i])
    res = run(prog("pre_gqa", build_pre_gqa, "gqa"), maps)
    kv = gather_kv_win([r["kT"] for r in res], [r["vP"] for r in res])
    wo = np.ascontiguousarray(inp["swa_w_o"][0].reshape(16, 64, D).transpose(1, 0, 2))
    masks = win_masks()
    sink = np.ascontiguousarray(inp["swa_sink"][0].reshape(1, 16))
    maps = [{"xT": xs[c], "modT": mods[c // 4][i], "ng": ng1, "qT": res[c]["qT"], "kT": kv[c][0], "vP": kv[c][1],
             "wo": wo, "masks": masks, "sink": sink} for c in range(NCORES)]
    res2 = run(prog("att", build_att, "win", False), maps)
    return [r["xo"] for r in res2]


def layer_diff(i, xs, mods, inp, cos, sin):
    maps, ng1 = pre_inputs(i, xs, mods, inp["diff_w_qkv"][0], inp["diff_q_g"][0], inp["diff_k_g"][0], 16, 16,
                           cos, sin, inp["norm1_g"][i])
    res = run(prog("pre_gqa", build_pre_gqa, "diff"), maps)
    kv = gather_kv_dense([r["kT"] for r in res], [r["vP"] for r in res])
    kv = [(k, np.ascontiguousarray(v.transpose(0, 2, 1, 3))) for (k, v) in kv]
    wo = wl(inp["diff_w_o"][0])
    lamv = np.ascontiguousarray(np.stack([inp["diff_lam_q1"][0], inp["diff_lam_k1"][0], inp["diff_lam_q2"][0],
                                          inp["diff_lam_k2"][0]])[None])
    slg = np.ascontiguousarray(inp["diff_subln_g"][0].reshape(128, 1))
    lam_init = 0.8 - 0.6 * math.exp(-0.3 * i)
    maps = [{"xT": xs[c], "modT": mods[c // 4][i], "ng": ng1, "qT": res[c]["qT"], "kT": kv[c // 4][0],
             "vP": kv[c // 4][1], "wo": wo, "lamv": lamv, "slg": slg} for c in range(NCORES)]
    res2 = run(prog("att_diff", build_att_diff, lam_init), maps)
    return [r["xo"] for r in res2]


def layer_conv(i, xs, mods, inp):
    ng1 = colv(inp["norm1_g"][i])
    w1 = wl(inp["conv_w_pw1"][0])
    b1 = colv(inp["conv_b_pw1"][0])
    wdw = np.ascontiguousarray(inp["conv_w_dw"][0].T.reshape(KD, 128, 31).transpose(1, 0, 2))
    cvec = np.ascontiguousarray(np.stack([colv(inp["conv_b_dw"][0]), colv(inp["conv_ln_g"][0]),
                                          colv(inp["conv_ln_b"][0]), colv(inp["conv_b_pw2"][0])], axis=1))
    w2 = wl(inp["conv_w_pw2"][0])
    maps = []
    for c in range(NCORES):
        q = c % 4
        z = np.zeros((128, KD, HALO), np.float32)
        left = xs[c - 1][:, :, TL - HALO:TL] if q > 0 else z
        right = xs[c + 1][:, :, 0:HALO] if q < 3 else z
        xe = np.ascontiguousarray(np.concatenate([left, xs[c][:, :, 0:TL], right, xs[c][:, :, TL:TT]], axis=2))
        hv = np.ascontiguousarray(np.broadcast_to(np.array([[float(q > 0), float(q < 3)]], np.float32), (128, 2)))
        maps.append({"xT": xe, "modT": mods[c // 4][i], "ng": ng1, "w1": w1, "b1": b1, "wdw": wdw, "cvec": cvec,
                     "w2": w2, "hv": hv})
    res = run(prog("conv", build_conv), maps)
    return [r["xo"] for r in res]


def kernel(**inputs):
    inp = {k: np.asarray(v, dtype=np.float32) for k, v in inputs.items()}
    cos, sin = rope_tables()
    mods = run_mod(inp["c"], inp["c_ctx"], inp["mod_w"], inp["mod_b"])
    xs = shard_x(inp["x"], inp["ctx"])
    xs = layer_gqa_dense(0, xs, mods, inp, True, cos, sin)
    xs = layer_mlp(0, xs, mods, inp, True)
    xs = layer_conv(1, xs, mods, inp)
    xs = layer_mlp(1, xs, mods, inp, True)
    xs = layer_diff(2, xs, mods, inp, cos, sin)
    xs = layer_mlp(2, xs, mods, inp, True)
    xs = layer_win(3, xs, mods, inp, cos, sin)
    xs = layer_mlp(3, xs, mods, inp, False)
    return unshard_x(xs)


GROUPS4 = [[0, 1, 2, 3], [4, 5, 6, 7]]


def build_fused(stop_after=99):
    nc = new_nc()
    I = {}
    step = [0]

    class Stop(Exception):
        pass

    def chk():
        step[0] += 1
        if step[0] > stop_after:
            raise Stop()

    def inp(name, shape, dt=F32):
        I[name] = din(nc, name, shape, dt)
        return I[name]

    xT_in = inp("xT", [128, KD, TT])
    inp("cT", [128, KD, 2])
    inp("mod_w", [DEPTH, 128, KD, 1536])
    inp("mod_b", [DEPTH, 128, 12])
    inp("ng1", [DEPTH, 128, KD])
    inp("ng2", [DEPTH, 128, KD])
    inp("wu", [DEPTH, 128, KD, DFF])
    inp("wd", [DEPTH, 128, 32, D])
    inp("cs", [128, 16, 2, 32])
    inp("ident", [128, 128], BF16)
    inp("gqa_wqkv", [128, KD, 1536]); inp("gqa_gvec", [128, 1280]); inp("gqa_wo", [64, 16, D])
    inp("swa_wqkv", [128, KD, 1536]); inp("swa_gvec", [128, 1280]); inp("swa_wo", [64, 16, D])
    inp("masks", [128, 6, 512], BF16); inp("sink", [1, 16]); inp("selv", [128, 2, 4])
    inp("diff_wqkv", [128, KD, 3072]); inp("diff_gvec", [128, 2048]); inp("diff_wo", [128, 8, D])
    inp("lamv", [1, 4, 64]); inp("slg", [128, 1])
    inp("w1", [128, KD, 2 * D]); inp("b1", [128, 16]); inp("wdw", [128, KD, 31]); inp("cvec", [128, 4, KD])
    inp("w2", [128, KD, D]); inp("hv", [128, 2])
    xo = dout(nc, "xo", [128, KD, TT])
    modT = dint(nc, "modT_i", [DEPTH, 128, 48, 2])
    xA = dint(nc, "xA", [128, KD, TT])
    xB = dint(nc, "xB", [128, KD, TT])
    qT = dint(nc, "qT_i", [128, KD, TT], BF16)
    kloc = dint(nc, "kloc", [4, 128, TT], BF16)
    vloc = dint(nc, "vloc", [2, 128, 9, 4, 65], BF16)
    kall = dint(nc, "kall", [4, 512, TT], BF16)
    vall = dint(nc, "vall", [2, 512, 9, 4, 65], BF16)
    kloc2 = dint(nc, "kloc2", [8, 128, TT], BF16)
    vloc2 = dint(nc, "vloc2", [8, 128, 18, 128], BF16)
    kall2 = dint(nc, "kall2", [8, 512, TT], BF16)
    vall2 = dint(nc, "vall2", [8, 512, 18, 128], BF16)
    xe_loc = dint(nc, "xe_loc", [128, KD * 2 * HALO])
    xe_all = dint(nc, "xe_all", [512, KD * 2 * HALO])
    xext = dint(nc, "xext", [128, KD, TE + TC])
    with contextlib.ExitStack() as st:
        P = Prog(nc, st)
        P.init_arena()
        try:
            _fused_body(P, I, chk, modT, xT_in, xA, xB, xo, qT, kloc, vloc, kall, vall, kloc2, vloc2, kall2, vall2,
                        xe_loc, xe_all, xext)
        except Stop:
            P.phase()
            stg = P.sbuf("stg", [128, KD, 512], F32)
            bs_ = P.buf("stg")
            for c0 in range(0, TT, 512):
                n = min(512, TT - c0)
                P.dma("sp", stg[:, :, 0:n], xT_in[:, :, c0:c0 + n], (), [bs_])
                P.dma("sp", xo[:, :, c0:c0 + n], stg[:, :, 0:n], [bs_], [P.buf("o")])
        P.barrier()
        P.emit()
    return nc


def _fused_body(P, I, chk, modT, xT_in, xA, xB, xo, qT, kloc, vloc, kall, vall, kloc2, vloc2, kall2, vall2,
                xe_loc, xe_all, xext):
    if True:
        chk()
        modloc = dint(P.nc, "modloc", [128, DEPTH * 24])
        modall = dint(P.nc, "modall", [512, DEPTH * 24])
        build_mod(P, {"cT": I["cT"], "mod_w": I["mod_w"], "mod_b": I["mod_b"], "modloc": modloc})
        P.coll("AllGather", GROUPS4, [(modloc, modall)])
        P.phase()
        mg = P.sbuf("mg", [128, 4, DEPTH, 12, 2], F32)
        bmg = P.buf("mg")
        for r in range(4):
            P.dma("sp", mg[:, r].rearrange("p a b c -> p (a b c)"), modall[r * 128:(r + 1) * 128, :], (), [bmg])
        for l in range(DEPTH):
            for r in range(4):
                P.dma("sp", modT[l][:, r * 12:(r + 1) * 12, :], mg[:, r, l], [bmg], [P.buf("modT")])

        kview = kloc.rearrange("g p t -> p g t")
        vtiles = [vloc[t // 9, :, t % 9] for t in range(18)]

        def cc_gqa():
            P.coll("AllGather", GROUPS4, [(kloc[g], kall[g]) for g in range(4)] +
                   [(vloc[hf].rearrange("p t g d -> p (t g d)"), vall[hf].rearrange("p t g d -> p (t g d)"))
                    for hf in range(2)])

        def mlp(l, xin, xout, need_ctx):
            build_mlp(P, {"xT": xin, "modT": modT[l], "ng": I["ng2"][l], "wu": I["wu"][l], "wd": I["wd"][l],
                          "xo": xout}, need_ctx)

        chk()
        build_pre_gqa(P, {"xT": xT_in, "modT": modT[0], "ng": I["ng1"][0], "wqkv": I["gqa_wqkv"],
                          "gvec": I["gqa_gvec"], "cs": I["cs"], "ident": I["ident"], "qT": qT, "kT": kview,
                          "vP_tiles": vtiles}, "gqa")
        chk()
        cc_gqa()
        build_att(P, {"xT": xT_in, "modT": modT[0], "ng": I["ng1"][0], "qT": qT, "kloc": kloc, "vloc": vloc,
                      "kall": kall, "vall": vall, "wo": I["gqa_wo"], "xo": xA}, "dense", True)
        chk()
        mlp(0, xA, xB, True)
        chk()
        P.phase()
        edge = P.sbuf("edge", [128, KD, 2, HALO], F32)
        bed = P.buf("edge")
        P.dma("sp", edge[:, :, 0, :], xB[:, :, 0:HALO], (), [bed])
        P.dma("sp", edge[:, :, 1, :], xB[:, :, TL - HALO:TL], (), [bed])
        P.dma("sp", xe_loc, edge[:].rearrange("p a b c -> p (a b c)"), [bed], [P.buf("xe")])
        P.coll("AllGather", GROUPS4, [(xe_loc, xe_all)])
        P.phase()
        ea = P.sbuf("ea", [128, 4, KD, 2, HALO], F32)
        sv = P.sbuf("sv2", [128, 2, 4], F32)
        hl = P.sbuf("hl", [128, 2, KD, HALO], F32)
        bea, bsv, bhl = P.buf("ea"), P.buf("sv2"), P.buf("hl")
        for r in range(4):
            P.dma("sp", ea[:, r].rearrange("p a b c -> p (a b c)"), xe_all[r * 128:(r + 1) * 128, :], (), [bea])
        P.dma("sp", sv[:], I["selv"], (), [bsv])
        for side in range(2):
            src_i = 1 - side
            P.ts("dve", hl[:, side], ea[:, 0, :, src_i, :], sv[:, side, 0:1], ALU.mult, [bea, bsv], [bhl])
            for r in range(1, 4):
                P.stt("dve", hl[:, side], ea[:, r, :, src_i, :], sv[:, side, r:r + 1], hl[:, side], ALU.mult, ALU.add,
                      [bea, bsv, bhl], [bhl])
        bxe = P.buf("xext")
        P.dma("sp", xext[:, :, 0:HALO], hl[:, 0], [bhl], [bxe])
        P.dma("sp", xext[:, :, HALO + TL:TE], hl[:, 1], [bhl], [bxe])
        stage = P.sbuf("stage", [128, KD, 512], F32)
        bst = P.buf("stage")
        for c0 in range(0, TT, 512):
            n = min(512, TT - c0)
            P.dma("sp", stage[:, :, 0:n], xB[:, :, c0:c0 + n], (), [bst])
            d0 = HALO + c0 if c0 < TL else TE + (c0 - TL)
            P.dma("sp", xext[:, :, d0:d0 + n], stage[:, :, 0:n], [bst], [bxe])
        chk()
        build_conv(P, {"xT": xext, "modT": modT[1], "ng": I["ng1"][1], "w1": I["w1"], "b1": I["b1"],
                       "wdw": I["wdw"], "cvec": I["cvec"], "w2": I["w2"], "hv": I["hv"], "xo": xA})
        chk()
        mlp(1, xA, xB, True)
        chk()
        build_pre_gqa(P, {"xT": xB, "modT": modT[2], "ng": I["ng1"][2], "wqkv": I["diff_wqkv"],
                          "gvec": I["diff_gvec"], "cs": I["cs"], "ident": I["ident"], "qT": qT,
                          "kT": kloc2.rearrange("g p t -> p g t"),
                          "vP_tiles": [vloc2.rearrange("h p t d -> p h t d")[:, :, t, :] for t in range(18)]}, "diff")
        chk()
        P.coll("AllGather", GROUPS4, [(kloc2[h], kall2[h]) for h in range(8)] +
               [(vloc2[h].rearrange("p t d -> p (t d)"), vall2[h].rearrange("p t d -> p (t d)")) for h in range(8)])
        build_att_diff(P, {"xT": xB, "modT": modT[2], "ng": I["ng1"][2], "qT": qT, "kloc": kloc2, "vloc": vloc2,
                           "kall": kall2, "vall": vall2, "wo": I["diff_wo"], "lamv": I["lamv"], "slg": I["slg"],
                           "xo": xA}, 0.8 - 0.6 * math.exp(-0.3 * 2))
        chk()
        mlp(2, xA, xB, True)
        chk()
        build_pre_gqa(P, {"xT": xB, "modT": modT[3], "ng": I["ng1"][3], "wqkv": I["swa_wqkv"],
                          "gvec": I["swa_gvec"], "cs": I["cs"], "ident": I["ident"], "qT": qT, "kT": kview,
                          "vP_tiles": vtiles}, "gqa")
        cc_gqa()
        build_att(P, {"xT": xB, "modT": modT[3], "ng": I["ng1"][3], "qT": qT, "kloc": kloc, "vloc": vloc,
                      "kall": kall, "vall": vall, "wo": I["swa_wo"], "masks": I["masks"], "sink": I["sink"],
                      "selv": I["selv"], "xo": xA}, "win", False)
        chk()
        mlp(3, xA, xo, False)


def kernel(**inputs):
    inp = {k: np.asarray(v, dtype=np.float32) for k, v in inputs.items()}
    cos, sin = rope_tables()
    xs = shard_x(inp["x"], inp["ctx"])
    ident = np.eye(128, dtype=np.float32).astype(NPBF)

    def gv(qg, kg, nq, nk):
        g = np.concatenate([np.tile(qg, nq), np.tile(kg, nk)]).astype(np.float32)
        return np.ascontiguousarray(np.broadcast_to(g[None, :], (128, g.size)))

    def wo64(w):
        return np.ascontiguousarray(w.reshape(16, 64, D).transpose(1, 0, 2))

    shared = {
        "ng1": np.stack([colv(inp["norm1_g"][l]) for l in range(DEPTH)]),
        "ng2": np.stack([colv(inp["norm2_g"][l]) for l in range(DEPTH)]),
        "wu": np.stack([wl(inp["mlp_up"][l]) for l in range(DEPTH)]),
        "wd": np.stack([wl(inp["mlp_down"][l]) for l in range(DEPTH)]),
        "ident": ident,
        "gqa_wqkv": wl(inp["gqa_w_qkv"][0]), "gqa_gvec": gv(inp["gqa_q_g"][0], inp["gqa_k_g"][0], 16, 4),
        "gqa_wo": wo64(inp["gqa_w_o"][0]),
        "swa_wqkv": wl(inp["swa_w_qkv"][0]), "swa_gvec": gv(inp["swa_q_g"][0], inp["swa_k_g"][0], 16, 4),
        "swa_wo": wo64(inp["swa_w_o"][0]),
        "masks": win_masks(), "sink": np.ascontiguousarray(inp["swa_sink"][0].reshape(1, 16)),
        "diff_wqkv": wl(inp["diff_w_qkv"][0]), "diff_gvec": gv(inp["diff_q_g"][0], inp["diff_k_g"][0], 16, 16),
        "diff_wo": wl(inp["diff_w_o"][0]),
        "lamv": np.ascontiguousarray(np.stack([inp["diff_lam_q1"][0], inp["diff_lam_k1"][0], inp["diff_lam_q2"][0],
                                               inp["diff_lam_k2"][0]])[None]),
        "slg": np.ascontiguousarray(inp["diff_subln_g"][0].reshape(128, 1)),
        "w1": wl(inp["conv_w_pw1"][0]), "b1": colv(inp["conv_b_pw1"][0]),
        "wdw": np.ascontiguousarray(inp["conv_w_dw"][0].T.reshape(KD, 128, 31).transpose(1, 0, 2)),
        "cvec": np.ascontiguousarray(np.stack([colv(inp["conv_b_dw"][0]), colv(inp["conv_ln_g"][0]),
                                               colv(inp["conv_ln_b"][0]), colv(inp["conv_b_pw2"][0])], axis=1)),
        "w2": wl(inp["conv_w_pw2"][0]),
    }
    maps = []
    for c in range(NCORES):
        b, q = c // 4, c % 4
        selv = np.zeros((128, 2, 4), np.float32)
        if q > 0:
            selv[:, 0, q - 1] = 1.0
        if q < 3:
            selv[:, 1, q + 1] = 1.0
        m = dict(shared)
        mc = slice(q * 1536, (q + 1) * 1536)
        m.update({"mod_w": np.ascontiguousarray(inp["mod_w"][:, :, mc].reshape(DEPTH, KD, 128, 1536).transpose(0, 2, 1, 3)),
                  "mod_b": np.ascontiguousarray(inp["mod_b"][:, mc].reshape(DEPTH, 12, 128).transpose(0, 2, 1))})
        m.update({"xT": xs[c], "cT": np.ascontiguousarray(np.stack([colv(inp["c"][b]), colv(inp["c_ctx"])], axis=-1)),
                  "cs": cs_for_core(cos, sin, q), "selv": selv,
                  "hv": np.ascontiguousarray(np.broadcast_to(np.array([[float(q > 0), float(q < 3)]], np.float32),
                                                             (128, 2)))})
        maps.append(m)
    nc = prog("fused", build_fused)
    res = run(nc, maps)
    return unshard_x([r["xo"] for r in res])
```
